# Optimizing a Trainium2 kernel written in Bass

```python
import math
import jax, jax.numpy as jnp
from jax import lax
import numpy as np

D_MODEL = 1024
BATCH = 8
SEQ = 2048
DEPTH = 2
DEC_BATCH = 16
DEC_SEQ = 32
PAST_LEN = 1024

CHUNK = 64
N_META = 16
Q_BLOCK = 128
N_GROUPS = 4
GROUP_W = D_MODEL // N_GROUPS
HEAD_DIM = 64
N_HEADS = GROUP_W // HEAD_DIM
LRU_BLOCKS = 4
LRU_BW = GROUP_W // LRU_BLOCKS
LRU_C = 8.0
CONV_W = 4
D_FF = 2816
FFN_CONV = 3
ALPHA = (2.0 * DEPTH) ** 0.25
BETA_INIT = (8.0 * DEPTH) ** -0.25
LN_EPS = 1e-5
RMS_EPS = 1e-6
IN_SPLITS = (GROUP_W, GROUP_W,
             GROUP_W, GROUP_W, GROUP_W, N_HEADS,
             GROUP_W, GROUP_W, GROUP_W,
             GROUP_W, GROUP_W, GROUP_W, N_HEADS, N_HEADS, GROUP_W)
D_IN = 12 * GROUP_W + 3 * N_HEADS
STATE_NAMES = ('lru_conv', 'lru_h', 'fox_k', 'fox_v', 'fox_logf', 'sb_k', 'sb_v', 'dn_conv', 'dn_S', 'ffn_conv')

kernel_name = 'hybrid_streaming_encoder_step'


def layer_norm(x, g, b):
    xf = x.astype(jnp.float32)
    mu = jnp.mean(xf, -1, keepdims=True)
    var = jnp.mean(jnp.square(xf - mu), -1, keepdims=True)
    return ((xf - mu) * lax.rsqrt(var + LN_EPS) * g + b).astype(x.dtype)


def rms_norm(x, g):
    xf = x.astype(jnp.float32)
    return xf * lax.rsqrt(jnp.mean(xf * xf, -1, keepdims=True) + RMS_EPS) * g


def l2_normalize(x):
    xf = x.astype(jnp.float32)
    return xf * lax.rsqrt(jnp.sum(xf * xf, -1, keepdims=True) + RMS_EPS)


def split_projection(proj):
    idx, acc = [], 0
    for s in IN_SPLITS[:-1]:
        acc += s
        idx.append(acc)
    return jnp.split(proj, idx, axis=-1)


def causal_dwconv(u, buf, w, b=None):
    width, L = w.shape[0], u.shape[1]
    full = jnp.concatenate([buf.astype(u.dtype), u], axis=1)
    out = full[:, 0:L] * w[0]
    for i in range(1, width):
        out = out + full[:, i:i + L] * w[i]
    if b is not None:
        out = out + b
    return out, full[:, L:]


def rg_lru(x, h0, w_a, b_a, w_x, b_x, lam):
    B, L, W = x.shape
    xf = x.astype(jnp.float32)
    xb = xf.reshape(B, L, LRU_BLOCKS, LRU_BW)
    r = jax.nn.sigmoid(jnp.einsum('blni,nij->blnj', xb, w_a).reshape(B, L, W) + b_a)
    i = jax.nn.sigmoid(jnp.einsum('blni,nij->blnj', xb, w_x).reshape(B, L, W) + b_x)
    log_a = -LRU_C * r * jax.nn.softplus(-lam)
    a = jnp.exp(log_a)
    b = jnp.sqrt(-jnp.expm1(2.0 * log_a)) * (i * xf)
    b = b.at[:, 0].add(a[:, 0] * h0.astype(jnp.float32))

    def combine(lhs, rhs):
        a1, b1 = lhs
        a2, b2 = rhs
        return a1 * a2, a2 * b1 + b2

    _, h = lax.associative_scan(combine, (a, b), axis=1)
    return h, h[:, -1]


def fox_attend(q, k, v, f_q, f_k, q_pos, k_pos):
    s = jnp.einsum('bqhd,bkhd->bhqk', q, k).astype(jnp.float32) * (HEAD_DIM ** -0.5)
    s = s + (jnp.moveaxis(f_q, 2, 1)[..., :, None] - jnp.moveaxis(f_k, 2, 1)[..., None, :])
    s = jnp.where(k_pos[None, :] <= q_pos[:, None], s, -jnp.inf)
    p = jax.nn.softmax(s, axis=-1)
    return jnp.einsum('bhqk,bkhd->bqhd', p, v.astype(jnp.float32))


def sb_attend(q, k, v, q_pos, k_pos):
    z = jnp.einsum('bqhd,bkhd->bhqk', q, k).astype(jnp.float32) * (HEAD_DIM ** -0.5)
    mask = k_pos[None, :] < q_pos[:, None]
    log_keep = jnp.where(mask, jax.nn.log_sigmoid(-z), 0.0)
    later = lax.cumsum(log_keep, axis=3, reverse=True) - log_keep
    w = jnp.where(mask, jnp.exp(jax.nn.log_sigmoid(z) + later), 0.0)
    return jnp.einsum('bhqk,bkhd->bqhd', w, v.astype(jnp.float32))


def sweep_query_blocks(fn, q_args, q_pos):
    Lq = q_pos.shape[0]
    if Lq <= Q_BLOCK:
        return fn(*q_args, q_pos)
    nb = -(-Lq // Q_BLOCK)
    pad = nb * Q_BLOCK - Lq

    def to_blocks(a):
        a = jnp.pad(a, [(0, 0), (0, pad)] + [(0, 0)] * (a.ndim - 2))
        return jnp.moveaxis(a.reshape(a.shape[0], nb, Q_BLOCK, *a.shape[2:]), 1, 0)

    qp = jnp.pad(q_pos, (0, pad), mode='edge').reshape(nb, Q_BLOCK)
    out = lax.map(lambda blk: fn(*blk[0], blk[1]), (tuple(to_blocks(a) for a in q_args), qp))
    out = jnp.moveaxis(out, 0, 1)
    return out.reshape(out.shape[0], nb * Q_BLOCK, *out.shape[3:])[:, :Lq]


def gated_delta_rule(q, k, v, g, beta, S0, front):
    B, L, H, DK = q.shape
    DV = v.shape[-1]
    n = -(-(front + L) // CHUNK)
    back = n * CHUNK - front - L

    def blocks(a):
        a = jnp.pad(a, [(0, 0), (front, back)] + [(0, 0)] * (a.ndim - 2))
        a = a.reshape(B, n, CHUNK, H, *a.shape[3:])
        return jnp.moveaxis(a, 3, 1)

    q, k, v, g, beta = (blocks(a) for a in (q, k, v, g, beta))
    gc = jnp.cumsum(g, axis=-1)
    idx = jnp.arange(CHUNK)
    causal = idx[:, None] >= idx[None, :]
    strict = idx[:, None] > idx[None, :]
    decay = jnp.exp(jnp.where(causal, gc[..., :, None] - gc[..., None, :], -jnp.inf))
    kb = k * beta[..., None]
    m = jnp.where(strict, jnp.einsum('bhntd,bhnsd->bhnts', kb, k) * decay, 0.0)
    eye = jnp.eye(CHUNK, dtype=m.dtype)
    rhs = jnp.concatenate([v * beta[..., None], kb * jnp.exp(gc)[..., None]], axis=-1)
    sol = lax.linalg.triangular_solve(m + eye, rhs, left_side=True, lower=True, unit_diagonal=True)
    u, w = sol[..., :DV], sol[..., DV:]
    a_in = jnp.einsum('bhntd,bhnsd->bhnts', q, k) * decay

    def step(S, blk):
        qi, ki, ui, wi, gi, ai = blk
        v_new = ui - jnp.einsum('bhtk,bhkv->bhtv', wi, S)
        o = (jnp.einsum('bhtk,bhkv->bhtv', qi * jnp.exp(gi)[..., None], S)
             + jnp.einsum('bhts,bhsv->bhtv', ai, v_new))
        g_last = gi[..., -1]
        S = (S * jnp.exp(g_last)[..., None, None]
             + jnp.einsum('bhtk,bhtv->bhkv', ki * jnp.exp(g_last[..., None] - gi)[..., None], v_new))
        return S, o

    xs = tuple(jnp.moveaxis(a, 2, 0) for a in (q, k, u, w, gc, a_in))
    S, o = lax.scan(step, S0, xs)
    o = jnp.moveaxis(o, 0, 2).reshape(B, H, n * CHUNK, DV)
    return jnp.moveaxis(o, 1, 2)[:, front:front + L], S


def token_mixers(h, l, P, st, front):
    B, L, _ = h.shape
    f32 = jnp.float32
    heads = lambda a: a.reshape(B, L, N_HEADS, HEAD_DIM)
    proj = h @ P['w_in'][l] + P['b_in'][l]
    (lru_x, lru_gate, fox_q, fox_k, fox_v, fox_f, sb_q, sb_k, sb_v,
     dn_q, dn_k, dn_v, dn_a, dn_b, dn_gate) = split_projection(proj)
    past = st['fox_k'].shape[1]
    q_pos = past + jnp.arange(L)
    k_pos = jnp.arange(past + L)

    xa, lru_conv_new = causal_dwconv(lru_x, st['lru_conv'], P['lru_conv_w'][l], P['lru_conv_b'][l])
    ha, h_last = rg_lru(xa, st['lru_h'], P['lru_w_a'][l], P['lru_b_a'][l],
                        P['lru_w_x'][l], P['lru_b_x'][l], P['lru_lambda'][l])
    out_a = jax.nn.gelu(lru_gate.astype(f32)) * ha

    fk_new, fv_new = heads(fox_k), heads(fox_v)
    logf_new = jax.nn.log_sigmoid(fox_f.astype(f32))
    fk_all = jnp.concatenate([st['fox_k'].astype(fk_new.dtype), fk_new], axis=1)
    fv_all = jnp.concatenate([st['fox_v'].astype(fv_new.dtype), fv_new], axis=1)
    F = jnp.cumsum(jnp.concatenate([st['fox_logf'].astype(f32), logf_new], axis=1), axis=1)
    out_b = sweep_query_blocks(
        lambda q, fq, qp: fox_attend(q, fk_all, fv_all, fq, F, qp, k_pos),
        (heads(fox_q), F[:, past:]), q_pos)

    sk_new, sv_new = heads(sb_k), heads(sb_v)
    sk_all = jnp.concatenate([st['sb_k'].astype(sk_new.dtype), sk_new], axis=1)
    sv_all = jnp.concatenate([st['sb_v'].astype(sv_new.dtype), sv_new], axis=1)
    out_c = sweep_query_blocks(
        lambda q, qp: sb_attend(q, sk_all, sv_all, qp, k_pos), (heads(sb_q),), q_pos)

    qkv, dn_conv_new = causal_dwconv(jnp.concatenate([dn_q, dn_k, dn_v], axis=-1),
                                     st['dn_conv'], P['dn_conv_w'][l])
    q_d, k_d, v_d = jnp.split(jax.nn.silu(qkv.astype(f32)), 3, axis=-1)
    q_d = l2_normalize(heads(q_d)) * (HEAD_DIM ** -0.5)
    k_d = l2_normalize(heads(k_d))
    g = -jnp.exp(P['dn_a_log'][l]) * jax.nn.softplus(dn_a.astype(f32) + P['dn_dt_bias'][l])
    beta = jax.nn.sigmoid(dn_b.astype(f32))
    o_d, S_new = gated_delta_rule(q_d, k_d, heads(v_d), g, beta, st['dn_S'].astype(f32), front)
    o_d = rms_norm(o_d, P['dn_norm_g'][l]) * jax.nn.silu(heads(dn_gate).astype(f32))

    gains = P['grp_norm_g'][l]
    cat = jnp.concatenate([rms_norm(out_a, gains[0]),
                           rms_norm(out_b.reshape(B, L, GROUP_W), gains[1]),
                           rms_norm(out_c.reshape(B, L, GROUP_W), gains[2]),
                           o_d.reshape(B, L, GROUP_W)], axis=-1).astype(h.dtype)
    out = cat @ P['w_o'][l]
    new = {'lru_conv': lru_conv_new, 'lru_h': h_last, 'fox_k': fk_new, 'fox_v': fv_new,
           'fox_logf': logf_new, 'sb_k': sk_new, 'sb_v': sv_new,
           'dn_conv': dn_conv_new, 'dn_S': S_new}
    return out, new


def conv_ffn(x, buf, w_in, conv_w, conv_b, w_out):
    gate, up = jnp.split(x @ w_in, 2, axis=-1)
    gate_c, new_buf = causal_dwconv(gate, buf, conv_w, conv_b)
    return (jax.nn.gelu(gate_c) * up) @ w_out, new_buf


def trunk(x, st, P, front):
    x = layer_norm(x, P['ln_in_g'], P['ln_in_b'])
    new = {n: [] for n in STATE_NAMES}
    for l in range(DEPTH):
        st_l = {n: st[n][l] for n in STATE_NAMES}
        m, ns = token_mixers(x, l, P, st_l, front)
        x = layer_norm(ALPHA * x + m, P['ln1_g'][l], P['ln1_b'][l])
        f, fbuf = conv_ffn(x, st_l['ffn_conv'], P['ffn_w_in'][l], P['ffn_conv_w'][l],
                           P['ffn_conv_b'][l], P['ffn_w_out'][l])
        x = layer_norm(ALPHA * x + f, P['ln2_g'][l], P['ln2_b'][l])
        ns['ffn_conv'] = fbuf
        for n in STATE_NAMES:
            new[n].append(ns[n])
    return x, {n: jnp.stack(new[n]) for n in STATE_NAMES}


def empty_state(batch, dtype):
    f32 = jnp.float32
    return {'lru_conv': jnp.zeros((DEPTH, batch, CONV_W - 1, GROUP_W), dtype),
            'lru_h': jnp.zeros((DEPTH, batch, GROUP_W), f32),
            'fox_k': jnp.zeros((DEPTH, batch, 0, N_HEADS, HEAD_DIM), dtype),
            'fox_v': jnp.zeros((DEPTH, batch, 0, N_HEADS, HEAD_DIM), dtype),
            'fox_logf': jnp.zeros((DEPTH, batch, 0, N_HEADS), f32),
            'sb_k': jnp.zeros((DEPTH, batch, 0, N_HEADS, HEAD_DIM), dtype),
            'sb_v': jnp.zeros((DEPTH, batch, 0, N_HEADS, HEAD_DIM), dtype),
            'dn_conv': jnp.zeros((DEPTH, batch, CONV_W - 1, 3 * GROUP_W), dtype),
            'dn_S': jnp.zeros((DEPTH, batch, N_HEADS, HEAD_DIM, HEAD_DIM), f32),
            'ffn_conv': jnp.zeros((DEPTH, batch, FFN_CONV - 1, D_FF), dtype)}


def setup_inputs(seed: int = 0) -> dict:
    key = jax.random.key(seed)
    ks = iter(jax.random.split(key, 48))
    nrm = lambda shape, scale=1.0: jax.random.normal(next(ks), shape, jnp.float32) * scale
    gain = lambda shape: 1.0 + nrm(shape, 0.02)
    u_lam = jax.random.uniform(next(ks), (DEPTH, GROUP_W), minval=0.9, maxval=0.999)
    s_lam = u_lam ** (1.0 / LRU_C)
    dt = jnp.exp(jax.random.uniform(next(ks), (DEPTH, N_HEADS), minval=math.log(1e-3), maxval=math.log(1e-1)))
    a_init = jax.random.uniform(next(ks), (DEPTH, N_HEADS), minval=1.0, maxval=16.0)
    return {
        'x_prompt': nrm((BATCH, SEQ, D_MODEL)),
        'x_sample': nrm((DEC_BATCH, DEC_SEQ, D_MODEL)),
        'state_lru_conv': nrm((DEPTH, DEC_BATCH, CONV_W - 1, GROUP_W)),
        'state_lru_h': nrm((DEPTH, DEC_BATCH, GROUP_W), 0.5),
        'cache_fox_k': nrm((DEPTH, DEC_BATCH, PAST_LEN, N_HEADS, HEAD_DIM)),
        'cache_fox_v': nrm((DEPTH, DEC_BATCH, PAST_LEN, N_HEADS, HEAD_DIM)),
        'cache_fox_logf': jax.nn.log_sigmoid(nrm((DEPTH, DEC_BATCH, PAST_LEN, N_HEADS)) + 2.0),
        'cache_sb_k': nrm((DEPTH, DEC_BATCH, PAST_LEN, N_HEADS, HEAD_DIM)),
        'cache_sb_v': nrm((DEPTH, DEC_BATCH, PAST_LEN, N_HEADS, HEAD_DIM)),
        'state_dn_conv': nrm((DEPTH, DEC_BATCH, CONV_W - 1, 3 * GROUP_W)),
        'state_dn_S': nrm((DEPTH, DEC_BATCH, N_HEADS, HEAD_DIM, HEAD_DIM), 0.1),
        'state_ffn_conv': nrm((DEPTH, DEC_BATCH, FFN_CONV - 1, D_FF)),
        'meta_tokens': nrm((N_META, D_MODEL)),
        'ln_in_g': gain((D_MODEL,)),
        'ln_in_b': nrm((D_MODEL,), 0.02),
        'w_in': nrm((DEPTH, D_MODEL, D_IN), D_MODEL ** -0.5),
        'b_in': nrm((DEPTH, D_IN), 0.02),
        'lru_conv_w': nrm((DEPTH, CONV_W, GROUP_W), CONV_W ** -0.5),
        'lru_conv_b': nrm((DEPTH, GROUP_W), 0.02),
        'lru_w_a': nrm((DEPTH, LRU_BLOCKS, LRU_BW, LRU_BW), LRU_BW ** -0.5),
        'lru_b_a': nrm((DEPTH, GROUP_W), 0.02),
        'lru_w_x': nrm((DEPTH, LRU_BLOCKS, LRU_BW, LRU_BW), LRU_BW ** -0.5),
        'lru_b_x': nrm((DEPTH, GROUP_W), 0.02),
        'lru_lambda': jnp.log(s_lam) - jnp.log1p(-s_lam),
        'dn_conv_w': nrm((DEPTH, CONV_W, 3 * GROUP_W), CONV_W ** -0.5),
        'dn_a_log': jnp.log(a_init),
        'dn_dt_bias': dt + jnp.log(-jnp.expm1(-dt)),
        'dn_norm_g': gain((DEPTH, HEAD_DIM)),
        'grp_norm_g': gain((DEPTH, 3, GROUP_W)),
        'w_o': nrm((DEPTH, D_MODEL, D_MODEL), BETA_INIT * D_MODEL ** -0.5),
        'ln1_g': gain((DEPTH, D_MODEL)),
        'ln1_b': nrm((DEPTH, D_MODEL), 0.02),
        'ffn_w_in': nrm((DEPTH, D_MODEL, 2 * D_FF), D_MODEL ** -0.5),
        'ffn_conv_w': nrm((DEPTH, FFN_CONV, D_FF), FFN_CONV ** -0.5),
        'ffn_conv_b': nrm((DEPTH, D_FF), 0.02),
        'ffn_w_out': nrm((DEPTH, D_FF, D_MODEL), BETA_INIT * D_FF ** -0.5),
        'ln2_g': gain((DEPTH, D_MODEL)),
        'ln2_b': nrm((DEPTH, D_MODEL), 0.02),
    }


def reference(x_prompt, x_sample, state_lru_conv, state_lru_h, cache_fox_k, cache_fox_v,
              cache_fox_logf, cache_sb_k, cache_sb_v, state_dn_conv, state_dn_S, state_ffn_conv,
              meta_tokens, ln_in_g, ln_in_b, w_in, b_in, lru_conv_w, lru_conv_b, lru_w_a, lru_b_a,
              lru_w_x, lru_b_x, lru_lambda, dn_conv_w, dn_a_log, dn_dt_bias, dn_norm_g, grp_norm_g,
              w_o, ln1_g, ln1_b, ffn_w_in, ffn_conv_w, ffn_conv_b, ffn_w_out, ln2_g, ln2_b):
    P = {'ln_in_g': ln_in_g, 'ln_in_b': ln_in_b, 'w_in': w_in, 'b_in': b_in,
         'lru_conv_w': lru_conv_w, 'lru_conv_b': lru_conv_b, 'lru_w_a': lru_w_a, 'lru_b_a': lru_b_a,
         'lru_w_x': lru_w_x, 'lru_b_x': lru_b_x, 'lru_lambda': lru_lambda,
         'dn_conv_w': dn_conv_w, 'dn_a_log': dn_a_log, 'dn_dt_bias': dn_dt_bias, 'dn_norm_g': dn_norm_g,
         'grp_norm_g': grp_norm_g, 'w_o': w_o, 'ln1_g': ln1_g, 'ln1_b': ln1_b,
         'ffn_w_in': ffn_w_in, 'ffn_conv_w': ffn_conv_w, 'ffn_conv_b': ffn_conv_b, 'ffn_w_out': ffn_w_out,
         'ln2_g': ln2_g, 'ln2_b': ln2_b}
    bp = x_prompt.shape[0]
    meta = jnp.broadcast_to(meta_tokens.astype(x_prompt.dtype), (bp, N_META, D_MODEL))
    xp = jnp.concatenate([meta, x_prompt], axis=1)
    yp, ps = trunk(xp, empty_state(bp, x_prompt.dtype), P, (-N_META) % CHUNK)
    y_prompt = yp[:, N_META:]
    st_in = {'lru_conv': state_lru_conv, 'lru_h': state_lru_h, 'fox_k': cache_fox_k, 'fox_v': cache_fox_v,
             'fox_logf': cache_fox_logf, 'sb_k': cache_sb_k, 'sb_v': cache_sb_v,
             'dn_conv': state_dn_conv, 'dn_S': state_dn_S, 'ffn_conv': state_ffn_conv}
    y_sample, ss = trunk(x_sample, st_in, P, 0)
    return (y_prompt, y_sample,
            ps['lru_conv'], ps['lru_h'], ps['fox_k'], ps['fox_v'], ps['fox_logf'],
            ps['sb_k'], ps['sb_v'], ps['dn_conv'], ps['dn_S'], ps['ffn_conv'],
            ss['lru_conv'], ss['lru_h'], ss['fox_k'], ss['fox_v'], ss['fox_logf'],
            ss['sb_k'], ss['sb_v'], ss['dn_conv'], ss['dn_S'], ss['ffn_conv'])
```

```python
import math
import os
import numpy as np
from contextlib import ExitStack
import concourse.bass as bass
import concourse.mybir as mybir
from concourse.bass_utils import run_bass_kernel_spmd

F32, BF16 = mybir.dt.float32, mybir.dt.bfloat16
ALU = mybir.AluOpType
AF = mybir.ActivationFunctionType
AX = mybir.AxisListType

D = 1024
NL = 2
SEQ = 2048
NMETA = 16
DSEQ = 32
PAST = 1024
NT = 80 + SEQ
DFF = 2816
NFC = DFF // 128
DIN = 3084
ALPHA = (2.0 * NL) ** 0.25
LN_EPS = 1e-5
RMS_EPS = 1e-6
NEG = -30000.0
O_LX, O_LG, O_FQ, O_FK, O_FV, O_FF, O_SQ, O_SK, O_SV, O_DQ, O_DA, O_DB, O_DG = (
    0, 256, 512, 768, 1024, 1280, 1284, 1540, 1796, 2052, 2820, 2824, 2828)

C_ID = 0
C_ONE = 128
C_BD = 256
C_UT = 384
C_SEL4 = 512
C_SELP = 1024
C_CM = 1280
C_CM0 = 1536
C_ZT = 1616
C_I4 = 1636
C_UTI = 2148
CW = 2276

P_BIN = 0
P_BF = 20
P_BA = 21
P_DT = 22
P_AL = 23
P_BB = 24
P_LCW = 25
P_LCB = 33
P_LBA = 35
P_LBX = 37
P_LAM = 39
P_DCW = 41
P_GN = 65
P_DNG = 71
P_LN1G = 72
P_LN1B = 80
P_LN2G = 88
P_LN2B = 96
P_FCW = 104
P_FCB = 170
PW = 192


class Buf:
    __slots__ = ("w", "r")

    def __init__(self):
        self.w = None
        self.r = []


class Tile:
    def __init__(self, h, bufs=None):
        self.h = h
        self.bufs = bufs if bufs is not None else [Buf()]

    def __getitem__(self, idx):
        return View(self.h[idx], self.bufs)

    def v(self, ap):
        return View(ap, self.bufs)


class View:
    __slots__ = ("ap", "bufs")

    def __init__(self, ap, bufs):
        self.ap = ap
        self.bufs = bufs


class K:
    def __init__(self, nc, st, n_dma_sems=32):
        self.nc = nc
        self.st = st
        self.engs = {}
        self.EPOCH = 6000
        for nm, h, ns in (("pe", nc.tensor, 7), ("dve", nc.vector, 5), ("act", nc.scalar, 4),
                          ("pool", nc.gpsimd, 3), ("sp", nc.sync, 2)):
            sems = [st.enter_context(nc.semaphore("s_%s%d" % (nm, i))) for i in range(ns)]
            self.engs[nm] = dict(h=h, sem=sems[0], sems=sems, si=0, cnt=0, seen={}, last=None, total=0)
        self.dsems = {q: [dict(sem=st.enter_context(nc.semaphore("d%s%d" % (q, i))), cnt=0)
                          for i in range(n_dma_sems)] for q in ("sp", "pool")}
        self.drr = {"sp": 0, "pool": 0}
        self.out_toks = []
        self.nins = 0

    def sb(self, name, shape, dt=F32, stack=None):
        self.uid = getattr(self, "uid", 0) + 1
        return Tile((stack or self.st).enter_context(self.nc.sbuf_tensor("%s_%d" % (name, self.uid), list(shape), dt)))

    def barrier(self):
        toks = [o["last"] for o in self.engs.values() if o["last"] is not None]
        toks += [(d["sem"], d["cnt"]) for q in self.dsems.values() for d in q if d["cnt"]]
        for nm in ("pe", "dve", "act", "pool", "sp"):
            e = self.engs[nm]
            for t in toks:
                if not any(t[0] is s_ for s_ in e["sems"]):
                    self._wait(e, t)

    def _wait(self, e, tok):
        if tok is None:
            return
        sem, val = tok
        k = id(sem)
        if e["seen"].get(k, 0) >= val:
            return
        e["seen"][k] = val
        e["h"].wait_ge(sem, val)

    def _need(self, e, rb, wb, pe=False, extra=()):
        need = {}

        def add(tok):
            if tok is None:
                return
            sem, val = tok
            k = id(sem)
            if e["seen"].get(k, 0) >= val:
                return
            if k not in need or need[k][1] < val:
                need[k] = (sem, val)
        for b in rb:
            add(b.w)
        for b in wb:
            if not (pe and b.w is not None and any(b.w[0] is s_ for s_ in e["sems"])):
                add(b.w)
            for t in b.r:
                add(t)
        for t in extra:
            add(t)
        return list(need.values())

    def _emit(self, e, need, fn):
        embed = bool(os.environ.get('KEMBED'))
        for tok in (need[:-1] if embed else need):
            self._wait(e, tok)
        ins_ = fn(e["h"])
        if need and embed:
            sem, val = need[-1]
            e["seen"][id(sem)] = val
            ins_._wait_ge(sem, val)
        return ins_

    def _commit(self, tok, rb, wb):
        for b in rb:
            b.r = [t for t in b.r if t[0] is not tok[0]]
            b.r.append(tok)
        for b in wb:
            b.w = tok
            b.r = []

    @staticmethod
    def _bl(views):
        out = []
        for v in views:
            if v is None or not isinstance(v, View):
                continue
            for b in v.bufs:
                if b not in out:
                    out.append(b)
        return out

    def op(self, eng, fn, outs, ins, extra=()):
        e = self.engs[eng]
        rb, wb = self._bl(ins), self._bl(outs)
        need = self._need(e, rb, wb, pe=(eng == "pe"), extra=extra)
        ins_ = self._emit(e, need, fn)
        e["cnt"] += 1
        e["total"] += 1
        ins_.then_inc(e["sem"], 1)
        tok = (e["sem"], e["cnt"])
        e["last"] = tok
        self._commit(tok, rb, wb)
        self.nins += 1
        if e["cnt"] >= self.EPOCH:
            e["si"] += 1
            e["sem"] = e["sems"][e["si"]]
            e["cnt"] = 0
        return tok

    def dma(self, out, in_, eng="sp", is_out=False):
        e = self.engs[eng]
        o_ap = out.ap if isinstance(out, View) else out
        i_ap = in_.ap if isinstance(in_, View) else in_
        rb, wb = self._bl([in_]), self._bl([out])
        d = self.dsems[eng][self.drr[eng]]
        self.drr[eng] = (self.drr[eng] + 1) % len(self.dsems[eng])
        extra = [(d["sem"], d["cnt"])] if d["cnt"] > 0 else []
        need = self._need(e, rb, wb, extra=extra)
        ins_ = self._emit(e, need, lambda h: h.dma_start(out=o_ap, in_=i_ap))
        d["cnt"] += 16
        ins_.then_inc(d["sem"], 16)
        tok = (d["sem"], d["cnt"])
        self._commit(tok, rb, wb)
        if is_out:
            self.out_toks.append(tok)
        self.nins += 1
        return tok

    def finish(self):
        e = self.engs["sp"]
        last = {}
        for sem, val in self.out_toks:
            last[id(sem)] = (sem, max(val, last.get(id(sem), (None, 0))[1]))
        for tok in last.values():
            self._wait(e, tok)
        for nm in ("pe", "dve", "act", "pool"):
            o = self.engs[nm]
            if o["last"] is not None:
                self._wait(e, o["last"])

    def mm(self, o, lhsT, rhs, start=True, stop=True, tp=None, ser=False):
        kw = {} if tp is None else dict(tile_position=tp)
        extra = ()
        if ser:
            e = self.engs["pe"]
            if e["last"] is not None:
                extra = [e["last"]]
        return self.op("pe", lambda e: e.matmul(o.ap, lhsT.ap, rhs.ap, start=start, stop=stop, **kw),
                       [o], [lhsT, rhs], extra=extra)

    def tr(self, o, a, ident, tp=None):
        kw = {} if tp is None else dict(tile_position=tp)
        return self.op("pe", lambda e: e.transpose(o.ap, a.ap, ident.ap, **kw), [o], [a, ident])

    def act(self, o, a, func, bias=None, scale=None):
        kw = {}
        ins = [a]
        if bias is not None:
            kw["bias"] = bias.ap if isinstance(bias, View) else bias
            ins.append(bias)
        if scale is not None:
            kw["scale"] = scale.ap if isinstance(scale, View) else scale
            ins.append(scale)
        return self.op("act", lambda e: e.activation(o.ap, a.ap, func, **kw), [o], ins)

    def tt(self, o, a, b, op, eng="dve"):
        return self.op(eng, lambda e: e.tensor_tensor(o.ap, a.ap, b.ap, op), [o], [a, b])

    def ts(self, o, a, s1, s2, op0, op1=None, eng="dve"):
        g = lambda s: s.ap if isinstance(s, View) else s
        if op1 is None:
            return self.op(eng, lambda e: e.tensor_scalar(o.ap, a.ap, g(s1), None, op0), [o], [a, s1])
        return self.op(eng, lambda e: e.tensor_scalar(o.ap, a.ap, g(s1), g(s2), op0, op1), [o], [a, s1, s2])

    def stt(self, o, a, s, b, op0, op1):
        g = s.ap if isinstance(s, View) else s
        return self.op("dve", lambda e: e.scalar_tensor_tensor(o.ap, a.ap, g, b.ap, op0, op1), [o], [a, s, b])

    def cp(self, o, a, eng="dve"):
        if eng == "act":
            return self.op("act", lambda e: e.copy(o.ap, a.ap), [o], [a])
        return self.op(eng, lambda e: e.tensor_copy(o.ap, a.ap), [o], [a])

    def memset(self, o, val, eng="dve"):
        return self.op(eng, lambda e: e.memset(o.ap, val), [o], [])

    def recip(self, o, a):
        return self.op("dve", lambda e: e.reciprocal(o.ap, a.ap), [o], [a])

    def scan(self, o, d0, d1, init):
        g = init.ap if isinstance(init, View) else init
        return self.op("dve", lambda e: e.tensor_tensor_scan(o.ap, d0.ap, d1.ap, g, ALU.mult, ALU.add),
                       [o], [d0, d1, init])

    def asel(self, o, a, pattern, cmp, fill, base, cm):
        return self.op("pool", lambda e: e.affine_select(o.ap, a.ap, pattern, cmp, fill, base=base,
                                                         channel_multiplier=cm), [o], [a])


TN = 256
import os
STAGE = int(os.environ.get('KSTAGE', '9'))
NLRUN = int(os.environ.get('KNL', '2'))
DNL = int(os.environ.get('KDN', '9'))
KSQ = int(os.environ.get('KSQ', '9'))
KD2 = int(os.environ.get('KD2', '9'))
KL0 = int(os.environ.get('KL0', '0'))
KDNM = int(os.environ.get('KDNM', '3'))
TILES = [(0, 80, [(1, 16, 32, PAST), (2, 48, 32, PAST), (0, 0, 16, 0)])]
for _j in range(SEQ // TN):
    TILES.append((80 + TN * _j, TN, [(0, 80 + TN * _j, TN, 16 + TN * _j)]))
FFN_GROUPS = [[0, 1, 2, 3], [4, 5, 6], [7, 8]]
BIGW = 80 + 3 * TN


def build_program():
    nc = bass.Bass("TRN2", target_bir_lowering=False)
    dram_in = lambda n, s: nc.dram_tensor(n, list(s), F32, kind="ExternalInput").ap()
    dram_out = lambda n, s: nc.dram_tensor(n, list(s), F32, kind="ExternalOutput").ap()
    xT = dram_in("xT", [128, 8, NT])
    cst_d = dram_in("cst", [128, CW])
    gpk_d = dram_in("gpk", [128, 16])
    lpk_d = dram_in("lpk", [NL, 128, PW])
    lbd_d = dram_in("lbd", [NL, 128, 2, 2, 128])
    GW = {"A": (O_LX, 512), "B": (O_FQ, 772), "C": (O_SQ, 768), "D": (O_DQ, 1032)}
    w_g_d = {g: dram_in("w_in" + g, [NL, 128, 8, n_]) for g, (o_, n_) in GW.items()}
    b_in_d = dram_in("b_in", [NL, DIN])
    w_o_d = dram_in("w_o", [NL, 128, 8, D])
    wf_in_d = dram_in("ffn_w_in", [NL, NFC, 128, 8, 256])
    wf_out_d = dram_in("ffn_w_out", [NL, 8, 128, NFC, 128])
    st_lconv = dram_in("st_lconv", [NL, 2, 128, 2, 3])
    st_lh = dram_in("st_lh", [NL, 2, 128, 2])
    st_fk = dram_in("st_fk", [NL, 2, 128, 2, PAST])
    st_fv = dram_in("st_fv", [NL, 2, PAST, 256])
    st_flf = dram_in("st_flf", [NL, 2, 4, PAST])
    st_sk = dram_in("st_sk", [NL, 2, 128, 2, PAST])
    st_sv = dram_in("st_sv", [NL, 2, PAST, 256])
    st_dconv = dram_in("st_dconv", [NL, 2, 128, 6, 3])
    st_dS = dram_in("st_dS", [NL, 2, 128, 2, 64])
    st_fconv = dram_in("st_fconv", [NL, 2, 128, NFC, 2])
    o_y = dram_out("o_y", [128, 8, NT])
    o_lconv = dram_out("o_lconv", [NL, 3, 128, 2, 3])
    o_lh = dram_out("o_lh", [NL, 3, 128, 2])
    o_fk = dram_out("o_fk", [NL, 128, 2, NT])
    o_fv = dram_out("o_fv", [NL, NT, 256])
    o_flf = dram_out("o_flf", [NL, 4, NT])
    o_sk = dram_out("o_sk", [NL, 128, 2, NT])
    o_sv = dram_out("o_sv", [NL, NT, 256])
    o_dconv = dram_out("o_dconv", [NL, 3, 128, 6, 3])
    o_dS = dram_out("o_dS", [NL, 3, 128, 2, 64])
    o_fconv = dram_out("o_fconv", [NL, 3, 128, NFC, 2])

    with ExitStack() as st:
        k = K(nc, st)
        X = k.sb("X", [128, 8, NT])
        Xt = [Tile(X.h) for _ in TILES]
        xv = lambda ti, c, a=0, n_=None: Xt[ti].v(X.h[:, c, TILES[ti][0] + a:TILES[ti][0] + a + (TILES[ti][1] if n_ is None else n_)])
        PS = st.enter_context(nc.psum_tensor("PS", [128, 8, 512], F32))
        PSb = [Buf() for _ in range(8)]
        rot = [0]

        def pbank(i):
            return Tile(PS[:, i, :], [PSb[i]])

        def prot():
            i = rot[0]
            rot[0] = (rot[0] + 1) % 3
            return pbank(i)

        CST = k.sb("CST", [128, CW])
        CSTB = k.sb("CSTB", [128, 512], BF16)
        OS = k.sb("OS", [128, 3, 128], BF16)
        GPK = k.sb("GPK", [128, 16])
        LPK = [k.sb("LPK%d" % l, [128, PW]) for l in range(NL)]
        LBD = k.sb("LBD", [128, 2, 2, 128])
        W = [k.sb("W%d" % i, [128, TN]) for i in range(12)]
        WB = [k.sb("WB%d" % i, [128, TN], BF16) for i in range(4)]
        U = k.sb("U", [128, 2, TN + 3])
        Uh = [Tile(U.h), Tile(U.h)]
        QKV = k.sb("QKV", [128, 6, TN])
        QT = k.sb("QT", [128, 2, TN], BF16)
        QTN = k.sb("QTN", [128, 2, TN], BF16)
        R4 = [k.sb("R4_%d" % i, [4, 256]) for i in range(3)]
        ONES4 = k.sb("ONES4", [4, 256])
        HISTL = [k.sb("HISTL%d" % s, [128, 2, 3]) for s in range(3)]
        HL = [k.sb("HL%d" % s, [128, 2]) for s in range(3)]
        HISTD = [k.sb("HISTD%d" % s, [128, 6, 3]) for s in range(3)]
        SDN = [k.sb("SDN%d" % s, [128, 2, 64]) for s in range(3)]
        HISTF = [k.sb("HISTF%d" % s, [128, NFC, 2]) for s in range(3)]
        CN = k.sb("CN", [128, 4])
        DNC = k.sb("DNC", [4, 4])
        VTM = k.sb("VTM", [128, 256])
        LSUM = k.sb("LSUM", [128, TN])
        cI = lambda n: CST[0:n, C_ID:C_ID + n]

        k.dma(CST[:, :], cst_d[:, :])
        k.dma(GPK[:, :], gpk_d[:, :])
        for l in range(NL):
            k.dma(LPK[l][:, :], lpk_d[l])
        for ti, (c0, n, segs) in enumerate(TILES):
            k.dma(Xt[ti].v(X.h[:, :, c0:c0 + n]), xT[:, :, c0:c0 + n])
        k.cp(CSTB[:, :], CST[:, 0:512])
        onesb = CSTB
        k.ts(OS[:, 0, :], CST[:, C_ONE:C_ONE + 128], 1.0 / 1024, None, ALU.mult)
        k.ts(OS[:, 1, :], CST[:, C_ONE:C_ONE + 128], 1.0 / 256, None, ALU.mult)
        k.ts(OS[:, 2, :], CST[:, C_BD:C_BD + 128], 1.0 / 64, None, ALU.mult)
        k.memset(ONES4[:, :], 1.0)

        def layer_norm(ti, gcol, bcol, pk):
            c0, n, _ = TILES[ti]
            pm, pq = pbank(6), pbank(7)
            for c in range(8):
                zb, sq = WB[c % 2], WB[2 + c % 2]
                k.cp(zb[:, 0:n], xv(ti, c), eng="pool")
                k.act(sq[:, 0:n], xv(ti, c), AF.Square)
                k.mm(pm[:, 0:n], OS[:, 0, :], zb[:, 0:n], start=(c == 0), stop=(c == 7))
                k.mm(pq[:, 0:n], OS[:, 0, :], sq[:, 0:n], start=(c == 0), stop=(c == 7))
            mean, msq, rstd, nmr = W[8], W[9], W[10], W[11]
            k.cp(mean[:, 0:n], pm[:, 0:n], eng="act")
            k.act(msq[:, 0:n], pm[:, 0:n], AF.Square)
            k.tt(rstd[:, 0:n], pq[:, 0:n], msq[:, 0:n], ALU.subtract)
            k.act(rstd[:, 0:n], rstd[:, 0:n], AF.Sqrt, bias=EPS_LN[:, 0:1])
            k.recip(rstd[:, 0:n], rstd[:, 0:n])
            k.stt(nmr[:, 0:n], mean[:, 0:n], -1.0, rstd[:, 0:n], ALU.mult, ALU.mult)
            for c in range(8):
                t = W[c % 2]
                k.tt(t[:, 0:n], xv(ti, c), rstd[:, 0:n], ALU.mult)
                k.tt(t[:, 0:n], t[:, 0:n], nmr[:, 0:n], ALU.add)
                k.act(xv(ti, c), t[:, 0:n], AF.Identity, bias=pk[:, bcol + c:bcol + c + 1],
                      scale=pk[:, gcol + c:gcol + c + 1])

        EPS_LN = k.sb("EPSLN", [128, 2])
        k.memset(EPS_LN[:, 0:1], LN_EPS)
        k.memset(EPS_LN[:, 1:2], RMS_EPS)

        def make_xb(xb, ti):
            c0, n, _ = TILES[ti]
            for c in range(8):
                k.cp(xb[:, c, 0:n], xv(ti, c), eng=("pool" if c % 2 else "dve"))
            return xb

        def load_w(dst, src_ap, eng="pool"):
            k.dma(dst, src_ap.rearrange("(c p) n -> p c n", p=128), eng=eng)

        def proj_fm(xb, a, n, wt, wcol, ps_view, M=128):
            for c in range(8):
                k.mm(ps_view, wt[:, c, wcol:wcol + M], xb[:, c, a:a + n], start=(c == 0), stop=(c == 7))

        for ti in range(len(TILES)):
            layer_norm(ti, 0, 8, GPK)

        for l in range(KL0, NLRUN):
            pk = LPK[l]
            k.dma(LBD[:, :, :, :], lbd_d[l])
            k.act(CN[:, 0:2], pk[:, P_LAM:P_LAM + 2], AF.Exp, scale=-1.0)
            k.act(CN[:, 0:2], CN[:, 0:2], AF.Ln, bias=1.0)
            k.ts(CN[:, 2:4], CN[:, 0:2], -16.0, None, ALU.mult)
            k.ts(CN[:, 0:2], CN[:, 0:2], -8.0, None, ALU.mult)
            k.act(DNC[:, 0:1], pk[0:4, P_AL:P_AL + 1], AF.Exp)
            k.ts(DNC[:, 0:1], DNC[:, 0:1], -1.0, None, ALU.mult)
            k.tt(DNC[:, 1:2], pk[0:4, P_BA:P_BA + 1], pk[0:4, P_DT:P_DT + 1], ALU.add)
            for s in range(3):
                if s == 0:
                    for t_ in (HISTL[0][:, :, :], HL[0][:, :], HISTD[0][:, :, :], SDN[0][:, :, :], HISTF[0][:, :, :]):
                        k.memset(t_, 0.0)
                else:
                    k.dma(HISTL[s][:, :, :], st_lconv[l, s - 1])
                    k.dma(HL[s][:, :], st_lh[l, s - 1])
                    k.dma(HISTD[s][:, :, :], st_dconv[l, s - 1])
                    k.dma(SDN[s][:, :, :], st_dS[l, s - 1])
                    k.dma(HISTF[s][:, :, :], st_fconv[l, s - 1])

            with ExitStack() as msc:
                CAT = k.sb("CAT", [128, 8, NT], BF16, msc)
                CATt = [[Tile(CAT.h) for _ in TILES] for _ in range(8)]
                catv = lambda j, ti: CATt[j][ti].v(CAT.h[:, j, TILES[ti][0]:TILES[ti][0] + TILES[ti][1]])
                WIN = k.sb("WIN", [128, 8, 1032], BF16, msc)
                XB = k.sb("XB", [128, 8, TN], BF16, msc)

                def rms_group(ti, srcs, gcols, catbase, six=1):
                    c0, n, _ = TILES[ti]
                    pq = pbank(7)
                    for i, s in enumerate(srcs):
                        sq = WB[2 + i]
                        k.act(sq[:, 0:n], s, AF.Square)
                        k.mm(pq[:, 0:n], OS[:, six, :], sq[:, 0:n], start=(i == 0), stop=(i == len(srcs) - 1))
                    rstd = W[10]
                    k.act(rstd[:, 0:n], pq[:, 0:n], AF.Sqrt, bias=EPS_LN[:, 1:2])
                    k.recip(rstd[:, 0:n], rstd[:, 0:n])
                    return rstd

                k.dma(WIN[:, :, 0:512], w_g_d["A"][l], eng="pool")
                for ti, (c0, n, segs) in enumerate(TILES if STAGE >= 1 else []):
                    make_xb(XB, ti)
                    GT = QKV
                    for c in range(2):
                        ps = prot()
                        proj_fm(XB, 0, n, WIN, 256 + 128 * c, ps[:, 0:n])
                        k.act(GT[:, c, 0:n], ps[:, 0:n], AF.Gelu_apprx_tanh, bias=pk[:, P_BIN + 2 + c:P_BIN + 3 + c])
                    def lru_chain(c, psx):
                        xa, r, ig, a, a2, h = (W[0], W[2], W[3], W[4], W[5], W[6]) if c == 0 else (W[1], W[7], W[8], W[9], W[10], W[11])
                        pr, pi = (pbank(4), pbank(5)) if c == 0 else (pbank(6), pbank(7))
                        for (sq_, sc0, sn, pos0) in segs:
                            o = sc0 - c0
                            k.cp(U[:, c, 0:3], HISTL[sq_][:, c, :]); yield
                            k.act(U[:, c, 3:3 + sn], psx[:, o:o + sn], AF.Identity, bias=pk[:, P_BIN + c:P_BIN + c + 1]); yield
                            k.cp(HISTL[sq_][:, c, :], U[:, c, sn:sn + 3]); yield
                            k.ts(xa[:, 0:sn], U[:, c, 3:3 + sn], pk[:, P_LCW + 4 * c + 3:P_LCW + 4 * c + 4],
                                 pk[:, P_LCB + c:P_LCB + c + 1], ALU.mult, ALU.add); yield
                            for i in range(3):
                                k.stt(xa[:, 0:sn], U[:, c, i:i + sn], pk[:, P_LCW + 4 * c + i:P_LCW + 4 * c + i + 1],
                                      xa[:, 0:sn], ALU.mult, ALU.add); yield
                            k.mm(pr[:, 0:sn], LBD[:, 0, c, :], xa[:, 0:sn]); yield
                            k.mm(pi[:, 0:sn], LBD[:, 1, c, :], xa[:, 0:sn]); yield
                            k.act(r[:, 0:sn], pr[:, 0:sn], AF.Sigmoid, bias=pk[:, P_LBA + c:P_LBA + c + 1]); yield
                            k.act(ig[:, 0:sn], pi[:, 0:sn], AF.Sigmoid, bias=pk[:, P_LBX + c:P_LBX + c + 1]); yield
                            k.act(a[:, 0:sn], r[:, 0:sn], AF.Exp, scale=CN[:, c:c + 1]); yield
                            k.act(a2[:, 0:sn], r[:, 0:sn], AF.Exp, scale=CN[:, 2 + c:3 + c]); yield
                            k.ts(a2[:, 0:sn], a2[:, 0:sn], -1.0, 1.0, ALU.mult, ALU.add); yield
                            k.act(a2[:, 0:sn], a2[:, 0:sn], AF.Sqrt); yield
                            k.tt(ig[:, 0:sn], ig[:, 0:sn], xa[:, 0:sn], ALU.mult); yield
                            k.tt(ig[:, 0:sn], ig[:, 0:sn], a2[:, 0:sn], ALU.mult); yield
                            k.scan(h[:, 0:sn], a[:, 0:sn], ig[:, 0:sn], HL[sq_][:, c:c + 1]); yield
                            k.cp(HL[sq_][:, c:c + 1], h[:, sn - 1:sn]); yield
                            k.tt(QKV[:, 2 + c, o:o + sn], h[:, 0:sn], GT[:, c, o:o + sn], ALU.mult); yield

                    psxs = []
                    for c in range(2):
                        psx = prot()
                        proj_fm(XB, 0, n, WIN, 128 * c, psx[:, 0:n])
                        psxs.append(psx)
                    gens = [lru_chain(0, psxs[0]), lru_chain(1, psxs[1])]
                    while gens:
                        for g_ in list(gens):
                            try:
                                next(g_)
                            except StopIteration:
                                gens.remove(g_)
                    for (sq_, sc0, sn, pos0) in segs:
                        if sq_ != 0 or sc0 + sn == NT:
                            k.dma(o_lconv[l, sq_], HISTL[sq_][:, :, :], is_out=True)
                            k.dma(o_lh[l, sq_], HL[sq_][:, :], is_out=True)
                    rstd = rms_group(ti, [QKV[:, 2, 0:n], QKV[:, 3, 0:n]], P_GN, 0)
                    for i in range(2):
                        k.stt(catv(i, ti), QKV[:, 2 + i, 0:n], pk[:, P_GN + i:P_GN + i + 1], rstd[:, 0:n], ALU.mult, ALU.mult)

                with ExitStack() as asc:
                    KT = k.sb("KT", [128, 2, 2176], BF16, asc)
                    VC = k.sb("VC", [128, 17, 256], BF16, asc)
                    FT = k.sb("FT", [4, 2176], F32, asc)
                    NFK = k.sb("NFK", [128, 17, 4], F32, asc)
                    LS = [LSUM, k.sb("LSUMB", [128, TN], F32, asc)]
                    for grp in (("fox", "sb") if STAGE >= 3 else (("fox",) if STAGE >= 2 else ())):
                        fox = grp == "fox"
                        oq, ov = (O_FQ, O_FV) if fox else (O_SQ, O_SV)
                        bq = P_BIN + (4 if fox else 8)
                        o_k, o_v = (o_fk, o_fv) if fox else (o_sk, o_sv)
                        s_k, s_v = (st_fk, st_fv) if fox else (st_sk, st_sv)
                        catb = 2 if fox else 4
                        wcols = 768 + (4 if fox else 0)
                        k.dma(WIN[:, :, 0:wcols], w_g_d["B" if fox else "C"][l], eng="pool")
                        VBIAS = W[11]
                        k.dma(VBIAS[:, 0:256], b_in_d[l, ov:ov + 256].partition_broadcast(128))
                        for ti, (c0, n, segs) in enumerate(TILES):
                            make_xb(XB, ti)
                            KF = QKV
                            for c in range(2):
                                ps = prot()
                                proj_fm(XB, 0, n, WIN, 128 * c, ps[:, 0:n])
                                k.ts(QT[:, c, 0:n], ps[:, 0:n], pk[:, bq + c:bq + c + 1], 0.125, ALU.add, ALU.mult)
                                if not fox:
                                    k.ts(QTN[:, c, 0:n], ps[:, 0:n], pk[:, bq + c:bq + c + 1], -0.125, ALU.add, ALU.mult)
                                ps = prot()
                                proj_fm(XB, 0, n, WIN, 256 + 128 * c, ps[:, 0:n])
                                k.act(KF[:, c, 0:n], ps[:, 0:n], AF.Identity, bias=pk[:, bq + 2 + c:bq + 3 + c])
                            k.dma(o_k[l, :, :, c0:c0 + n], KF[:, 0:2, 0:n], is_out=True)
                            OB = [W[8], W[9]]
                            for (sq_, sc0, sn, pos0) in segs:
                                o = sc0 - c0
                                if sq_ == 0:
                                    newb = [(0, 16)] if pos0 == 0 else [(1 + (pos0 - 16) // 128 + i, 128) for i in range(sn // 128)]
                                    oldb = [] if pos0 == 0 else [(0, 16)] + [(1 + i, 128) for i in range((pos0 - 16) // 128)]
                                else:
                                    oldb = [(i, 128) for i in range(8)]
                                    newb = [(8, 32)]
                                    for c in range(2):
                                        k.dma(KT[:, c, 0:PAST], s_k[l, sq_ - 1, :, c, :], eng="pool")
                                    k.dma(VC[:, 0:8, :], s_v[l, sq_ - 1].rearrange("(b p) f -> p b f", p=128), eng="pool")
                                kc = 0
                                for (b, nk) in newb:
                                    for c in range(2):
                                        k.cp(KT[:, c, 128 * b:128 * b + nk], KF[:, c, o + kc:o + kc + nk], eng="pool")
                                    pv = prot()
                                    for c in range(8):
                                        k.mm(pv[0:nk, 0:256], XB[:, c, o + kc:o + kc + nk], WIN[:, c, 512:768],
                                             start=(c == 0), stop=(c == 7))
                                    k.tt(VTM[0:nk, :], pv[0:nk, 0:256], VBIAS[0:nk, 0:256], ALU.add)
                                    k.dma(o_v[l, sc0 + kc:sc0 + kc + nk, :], VTM[0:nk, :], is_out=True)
                                    k.cp(VC[0:nk, b, :], VTM[0:nk, :], eng="pool")
                                    kc += nk
                                slot0 = newb[0][0] * 128
                                if fox:
                                    psf = prot()
                                    for c in range(8):
                                        k.mm(psf[0:4, 0:sn], WIN[:, c, 768:772], XB[:, c, o:o + sn], start=(c == 0), stop=(c == 7))
                                    lf, e1, cl = R4[0], R4[1], R4[2]
                                    k.ts(e1[0:4, 0:sn], psf[0:4, 0:sn], pk[0:4, P_BF:P_BF + 1], -1.0, ALU.add, ALU.mult)
                                    k.act(e1[0:4, 0:sn], e1[0:4, 0:sn], AF.Exp)
                                    k.act(e1[0:4, 0:sn], e1[0:4, 0:sn], AF.Ln, bias=1.0)
                                    k.ts(lf[0:4, 0:sn], e1[0:4, 0:sn], -1.0, None, ALU.mult)
                                    k.dma(o_flf[l, :, sc0:sc0 + sn], lf[0:4, 0:sn], is_out=True)
                                    if sq_ != 0:
                                        for hh in range(4):
                                            k.dma(cl[0:4, 0:256], st_flf[l, sq_ - 1, :, 256 * hh:256 * hh + 256])
                                            k.scan(FT[0:4, 256 * hh:256 * hh + 256], ONES4[0:4, 0:256], cl[0:4, 0:256],
                                                   (0.0 if hh == 0 else FT[0:4, 256 * hh - 1:256 * hh]))
                                        init = FT[0:4, PAST - 1:PAST]
                                    elif pos0 == 0:
                                        init = 0.0
                                    else:
                                        init = FT[0:4, slot0 - 1:slot0] if pos0 > 16 else FT[0:4, 15:16]
                                    k.scan(FT[0:4, slot0:slot0 + sn], ONES4[0:4, 0:sn], lf[0:4, 0:sn], init)
                                    for (b, nk) in ((oldb if sq_ != 0 else []) + newb):
                                        pt = prot()
                                        k.tr(pt[0:nk, 0:4], FT[0:4, 128 * b:128 * b + nk], cI(4))
                                        k.ts(NFK[0:nk, b, :], pt[0:nk, 0:4], -1.0, None, ALU.mult)
                                    FQ = [W[4 + h] for h in range(4)]
                                    for h in range(4):
                                        pf = prot()
                                        k.mm(pf[:, 0:sn], CST[0:4, C_SEL4 + 128 * h:C_SEL4 + 128 * h + 128], FT[0:4, slot0:slot0 + sn])
                                        k.cp(FQ[h][:, 0:sn], pf[:, 0:sn], eng="act")
                                blocks = oldb + newb
                                nold = len(oldb)
                                kcq, acc = {}, 0
                                for bi in range(nold, len(blocks)):
                                    kcq[bi] = acc
                                    acc += blocks[bi][1]
                                order = list(range(len(blocks)))
                                if not fox:
                                    order = order[::-1]
                                units = []
                                for pr_ in range(2):
                                    for hh in range(2):
                                        for ii, bi in enumerate(order):
                                            units.append(dict(pr=pr_, hh=hh, h=2 * pr_ + hh, ii=ii, bi=bi, b=blocks[bi][0],
                                                              nk=blocks[bi][1], diag=(bi >= nold), first=(ii == 0),
                                                              last=(ii == len(order) - 1)))
                                NU = len(units)

                                def stage_S(t):
                                    u = units[t]
                                    nk, b, h = u["nk"], u["b"], u["h"]
                                    rows = slice(64 * u["hh"], 64 * u["hh"] + 64)
                                    pS = prot()
                                    k.mm(pS[0:nk, 0:sn], KT[rows, u["pr"], 128 * b:128 * b + nk], QT[rows, u["pr"], o:o + sn])
                                    if fox:
                                        tmp, P = W[t % 4], WB[t % 4]
                                        if u["diag"]:
                                            k.stt(tmp[0:nk, 0:sn], pS[0:nk, 0:sn], 1.0, FQ[h][0:nk, 0:sn], ALU.mult, ALU.add)
                                            k.ts(tmp[0:nk, 0:sn], tmp[0:nk, 0:sn], NFK[0:nk, b, h:h + 1], 60.0, ALU.add, ALU.min)
                                            k.act(P[0:nk, 0:sn], tmp[0:nk, 0:sn], AF.Exp)
                                            k.asel(P[0:nk, 0:sn], P[0:nk, 0:sn], [[1, sn]], ALU.is_ge, 0.0, -kcq[u["bi"]], -1)
                                        else:
                                            k.stt(tmp[0:nk, 0:sn], pS[0:nk, 0:sn], 1.0, FQ[h][0:nk, 0:sn], ALU.mult, ALU.add)
                                            k.act(P[0:nk, 0:sn], tmp[0:nk, 0:sn], AF.Exp, bias=NFK[0:nk, b, h:h + 1])
                                    else:
                                        e_, sp = W[t % 4], W[4 + t % 4]
                                        k.act(e_[0:nk, 0:sn], pS[0:nk, 0:sn], AF.Exp)
                                        k.act(sp[0:nk, 0:sn], e_[0:nk, 0:sn], AF.Ln, bias=1.0)
                                        if u["diag"]:
                                            k.asel(sp[0:nk, 0:sn], sp[0:nk, 0:sn], [[1, sn]], ALU.is_ge, 0.0, -kcq[u["bi"]] - 1, -1)

                                def stage_L(t):
                                    u = units[t]
                                    nk, b = u["nk"], u["b"]
                                    rows = slice(64 * u["hh"], 64 * u["hh"] + 64)
                                    sp, P = W[4 + t % 4], WB[t % 4]
                                    pL = pbank(6 + t % 2)
                                    k.mm(pL[0:nk, 0:sn], CST[0:nk, C_UTI:C_UTI + nk], sp[0:nk, 0:sn], start=True, stop=False)
                                    if not u["first"]:
                                        k.mm(pL[0:nk, 0:sn], CST[:, C_ONE:C_ONE + nk], LS[t % 2][:, 0:sn], start=False, stop=False)
                                    k.mm(pL[0:nk, 0:sn], KT[rows, u["pr"], 128 * b:128 * b + nk], QTN[rows, u["pr"], o:o + sn],
                                         start=False, stop=True, ser=True)
                                    k.act(P[0:nk, 0:sn], pL[0:nk, 0:sn], AF.Exp, scale=-1.0)
                                    if u["diag"]:
                                        k.asel(P[0:nk, 0:sn], P[0:nk, 0:sn], [[1, sn]], ALU.is_ge, 0.0, -kcq[u["bi"]] - 1, -1)
                                    if not u["last"]:
                                        nxt, cur = LS[(t + 1) % 2], LS[t % 2]
                                        if u["first"]:
                                            if nk < 128:
                                                k.memset(nxt[:, 0:sn], 0.0)
                                            k.cp(nxt[0:nk, 0:sn], sp[0:nk, 0:sn])
                                        else:
                                            assert nk == 128
                                            k.tt(nxt[:, 0:sn], cur[:, 0:sn], sp[:, 0:sn], ALU.add)

                                def stage_O(t):
                                    u = units[t]
                                    nk, b, h, hh = u["nk"], u["b"], u["h"], u["hh"]
                                    rows = slice(64 * hh, 64 * hh + 64)
                                    P = WB[t % 4]
                                    pO = pbank(3 + u["pr"])
                                    k.mm(pO[rows, 0:sn], VC[0:nk, b, 64 * h:64 * h + 64], P[0:nk, 0:sn], start=u["first"], stop=u["last"], tp=(0, 64 * hh))
                                    if fox:
                                        pD = pbank(5 + u["pr"])
                                        k.mm(pD[rows, 0:sn], onesb[0:nk, 128:192], P[0:nk, 0:sn], start=u["first"], stop=u["last"], tp=(0, 64 * hh))
                                    if u["last"] and hh == 1:
                                        if fox:
                                            rD = W[10]
                                            k.recip(rD[:, 0:sn], pD[:, 0:sn])
                                            k.tt(OB[u["pr"]][:, o:o + sn], pO[:, 0:sn], rD[:, 0:sn], ALU.mult)
                                        else:
                                            k.cp(OB[u["pr"]][:, o:o + sn], pO[:, 0:sn], eng="act")

                                if fox:
                                    for t in range(NU + 2):
                                        if t < NU:
                                            stage_S(t)
                                        if t >= 2:
                                            stage_O(t - 2)
                                else:
                                    for t in range(NU + 3):
                                        if t < NU:
                                            stage_S(t)
                                        if 2 <= t < NU + 2:
                                            stage_L(t - 2)
                                        if t >= 3:
                                            stage_O(t - 3)
                            gc_ = P_GN + (2 if fox else 4)
                            rstd = rms_group(ti, [OB[0][:, 0:n], OB[1][:, 0:n]], gc_, catb)
                            for i in range(2):
                                k.stt(catv(catb + i, ti), OB[i][:, 0:n], pk[:, gc_ + i:gc_ + i + 1], rstd[:, 0:n], ALU.mult, ALU.mult)
                    k.barrier()

                with ExitStack() as dsc:
                    DNP = [k.sb("DNP%d" % i, [128, 2, TN], F32, dsc) for i in range(7)]
                    DNM = [k.sb("DNM%d" % i, [64, 512], F32, dsc) for i in range(6)]
                    DNT = [k.sb("DNT%d" % i, [64, 512], F32, dsc) for i in range(2)]
                    YH = k.sb("YH", [5, 4, 128], F32, dsc)
                    G5 = k.sb("G5", [5, TN], F32, dsc)
                    QN, KN, KB, KBG, QG, KDT, EGB = DNP
                    k.memset(G5[:, :], 1.0)
                    k.dma(WIN[:, :, 0:1032], w_g_d["D"][l], eng="pool")
                    for ti, (c0, n, segs) in enumerate(TILES if (STAGE >= 4 and (KDNM >> l) & 1) else []):
                        make_xb(XB, ti)
                        for c in range(6):
                            ps = prot()
                            proj_fm(XB, 0, n, WIN, 128 * c, ps[:, 0:n])
                            for (sq_, sc0, sn, pos0) in segs:
                                o = sc0 - c0
                                uu = lambda a, b_: U[:, c % 2, a:b_]
                                k.cp(uu(0, 3), HISTD[sq_][:, c, :])
                                k.act(uu(3, 3 + sn), ps[:, o:o + sn], AF.Identity, bias=pk[:, P_BIN + 12 + c:P_BIN + 13 + c])
                                k.cp(HISTD[sq_][:, c, :], uu(sn, sn + 3))
                                acc = W[0]
                                k.ts(acc[:, 0:sn], uu(3, 3 + sn), pk[:, P_DCW + 4 * c + 3:P_DCW + 4 * c + 4], None, ALU.mult)
                                for i in range(3):
                                    k.stt(acc[:, 0:sn], uu(i, i + sn), pk[:, P_DCW + 4 * c + i:P_DCW + 4 * c + i + 1],
                                          acc[:, 0:sn], ALU.mult, ALU.add)
                                k.act(QKV[:, c, o:o + sn], acc[:, 0:sn], AF.Silu)
                        for (sq_, sc0, sn, pos0) in segs:
                            if sq_ != 0 or sc0 + sn == NT:
                                k.dma(o_dconv[l, sq_], HISTD[sq_][:, :, :], is_out=True)
                        if DNL < 2:
                            continue
                        psa, psb = prot(), prot()
                        proj_fm(XB, 0, n, WIN, 768, psa[0:4, 0:n], M=4)
                        proj_fm(XB, 0, n, WIN, 772, psb[0:4, 0:n], M=4)
                        g_, be = R4[0], R4[1]
                        k.act(g_[0:4, 0:n], psa[0:4, 0:n], AF.Exp, bias=DNC[:, 1:2])
                        k.act(g_[0:4, 0:n], g_[0:4, 0:n], AF.Ln, bias=1.0)
                        k.ts(g_[0:4, 0:n], g_[0:4, 0:n], DNC[:, 0:1], None, ALU.mult)
                        k.act(be[0:4, 0:n], psb[0:4, 0:n], AF.Sigmoid, bias=pk[0:4, P_BB:P_BB + 1])
                        cm = CST[0:4, C_CM0:C_CM0 + n] if ti == 0 else CST[0:4, C_CM:C_CM + n]
                        k.scan(G5[0:4, 0:n], cm, g_[0:4, 0:n], 0.0)
                        if KD2 < 2:
                            continue
                        for c in range(4):
                            sq = WB[2 + c % 2]
                            k.act(sq[:, 0:n], QKV[:, c, 0:n], AF.Square)
                            pq = pbank(7)
                            k.mm(pq[:, 0:n], CSTB[:, 256:384], sq[:, 0:n])
                            rs = W[1]
                            k.act(rs[:, 0:n], pq[:, 0:n], AF.Sqrt, bias=EPS_LN[:, 1:2])
                            k.recip(rs[:, 0:n], rs[:, 0:n])
                            if c < 2:
                                k.stt(QN[:, c, 0:n], QKV[:, c, 0:n], 0.125, rs[:, 0:n], ALU.mult, ALU.mult)
                            else:
                                k.tt(KN[:, c - 2, 0:n], QKV[:, c, 0:n], rs[:, 0:n], ALU.mult)
                        if KD2 < 3:
                            continue
                        BBt, GBt = W[2], W[3]
                        for p in range(2):
                            selp = CST[0:4, C_SELP + 128 * p:C_SELP + 128 * p + 128]
                            pb_, pg_ = prot(), prot()
                            k.mm(pb_[:, 0:n], selp, be[0:4, 0:n])
                            k.mm(pg_[:, 0:n], selp, G5[0:4, 0:n])
                            k.cp(BBt[:, 0:n], pb_[:, 0:n], eng="act")
                            k.cp(GBt[:, 0:n], pg_[:, 0:n])
                            k.act(EGB[:, p, 0:n], pg_[:, 0:n], AF.Exp)
                            k.tt(KB[:, p, 0:n], KN[:, p, 0:n], BBt[:, 0:n], ALU.mult)
                            k.tt(KBG[:, p, 0:n], KB[:, p, 0:n], EGB[:, p, 0:n], ALU.mult)
                            k.tt(QG[:, p, 0:n], QN[:, p, 0:n], EGB[:, p, 0:n], ALU.mult)
                            k.tt(QKV[:, 4 + p, 0:n], QKV[:, 4 + p, 0:n], BBt[:, 0:n], ALU.mult)
                            if ti == 0:
                                chs = [(16, 32), (48, 32), (0, 16)]
                            else:
                                chs = [(64 * i, 64) for i in range(n // 64)]
                            for (co, cs) in chs:
                                k.ts(KDT[:, p, co:co + cs], GBt[:, co:co + cs], GBt[:, co + cs - 1:co + cs], -1.0,
                                     ALU.subtract, ALU.mult)
                            k.act(KDT[:, p, 0:n], KDT[:, p, 0:n], AF.Exp)
                            k.tt(KDT[:, p, 0:n], KDT[:, p, 0:n], KN[:, p, 0:n], ALU.mult)
                        if DNL < 3:
                            continue
                        if ti == 0:
                            batches = [[(16, 32, 1), (48, 32, 2)], [(0, 16, 0)]]
                        else:
                            batches = [[(128 * b_ + 64 * i, 64, 0) for i in range(2)] for b_ in range(n // 128)]
                        for bat in batches:
                            nb = len(bat)
                            wdt = nb * 256
                            mcs = max(cs for (_, cs, _) in bat)
                            nlv = {64: 5, 32: 4, 16: 3}[mcs]
                            psA, psB, psC = pbank(3), pbank(4), pbank(5)
                            if mcs < 64:
                                for pz in (psA, psB, psC):
                                    k.memset(pz[0:64, 0:wdt], 0.0)
                            sl = lambda T, j, h, r, c_: T[0:r, (j * 4 + h) * 64:(j * 4 + h) * 64 + c_]
                            for j, (co, cs, sq_) in enumerate(bat):
                                for h in range(4):
                                    py = prot()
                                    k.mm(py[0:5, 0:cs], CST[0:5, C_ZT + 5 * h:C_ZT + 5 * h + 5], G5[0:5, co:co + cs])
                                    k.cp(YH[0:5, h, 64 * j:64 * j + cs], py[0:5, 0:cs], eng="act")
                            for j, (co, cs, sq_) in enumerate(bat):
                                for h in range(4):
                                    k.mm(sl(psA, j, h, cs, cs), G5[0:5, co:co + cs], YH[0:5, h, 64 * j:64 * j + cs])
                                    k.mm(sl(psB, j, h, cs, cs), YH[0:5, h, 64 * j:64 * j + cs], G5[0:5, co:co + cs])
                            E1, E2 = DNM[0], DNM[1]
                            k.ts(E1[:, 0:wdt], psA[0:64, 0:wdt], 0.0, None, ALU.min)
                            k.act(E1[:, 0:wdt], E1[:, 0:wdt], AF.Exp)
                            k.ts(E2[:, 0:wdt], psB[0:64, 0:wdt], 0.0, None, ALU.min)
                            k.act(E2[:, 0:wdt], E2[:, 0:wdt], AF.Exp)
                            for j, (co, cs, sq_) in enumerate(bat):
                                for h in range(4):
                                    rows = slice(64 * (h % 2), 64 * (h % 2) + 64)
                                    p = h // 2
                                    k.mm(sl(psA, j, h, cs, cs), KB[rows, p, co:co + cs], KN[rows, p, co:co + cs])
                                    k.mm(sl(psB, j, h, cs, cs), KN[rows, p, co:co + cs], KB[rows, p, co:co + cs])
                                    k.mm(sl(psC, j, h, cs, cs), KN[rows, p, co:co + cs], QN[rows, p, co:co + cs])
                            Pm, PTm, AinT, AT = DNM[2], DNM[3], DNM[4], DNM[5]
                            k.tt(Pm[:, 0:wdt], psA[0:64, 0:wdt], E1[:, 0:wdt], ALU.mult)
                            k.asel(Pm[:, 0:wdt], Pm[:, 0:wdt], [[0, nb * 4], [-1, 64]], ALU.is_ge, 0.0, -1, 1)
                            k.tt(PTm[:, 0:wdt], psB[0:64, 0:wdt], E2[:, 0:wdt], ALU.mult)
                            k.asel(PTm[:, 0:wdt], PTm[:, 0:wdt], [[0, nb * 4], [1, 64]], ALU.is_ge, 0.0, -1, -1)
                            k.tt(AinT[:, 0:wdt], psC[0:64, 0:wdt], E2[:, 0:wdt], ALU.mult)
                            k.asel(AinT[:, 0:wdt], AinT[:, 0:wdt], [[0, nb * 4], [1, 64]], ALU.is_ge, 0.0, 0, -1)
                            k.stt(AT[:, 0:wdt], PTm[:, 0:wdt], -1.0, CST[0:64, C_I4:C_I4 + wdt], ALU.mult, ALU.add)
                            if DNL < 4:
                                continue
                            Pn, PTn = DNM[0], DNM[1]
                            for lv in range(nlv):
                                for j, (co, cs, sq_) in enumerate(bat):
                                    for h in range(4):
                                        k.mm(sl(psA, j, h, cs, cs), sl(PTm, j, h, cs, cs), sl(Pm, j, h, cs, cs))
                                        if lv < nlv - 1:
                                            k.mm(sl(psB, j, h, cs, cs), sl(Pm, j, h, cs, cs), sl(PTm, j, h, cs, cs))
                                k.cp(Pn[:, 0:wdt], psA[0:64, 0:wdt], eng="act")
                                if lv < nlv - 1:
                                    k.cp(PTn[:, 0:wdt], psB[0:64, 0:wdt])
                                for j, (co, cs, sq_) in enumerate(bat):
                                    for h in range(4):
                                        k.mm(sl(psC, j, h, cs, cs), sl(Pn, j, h, cs, cs), sl(AT, j, h, cs, cs))
                                k.tt(AT[:, 0:wdt], AT[:, 0:wdt], psC[0:64, 0:wdt], ALU.add)
                                Pm, Pn = Pn, Pm
                                PTm, PTn = PTn, PTm
                            if DNL < 5:
                                continue
                            VBt, KDt = DNT
                            for j, (co, cs, sq_) in enumerate(bat):
                                for p in range(2):
                                    k.tr(psA[0:cs, 256 * j + 128 * p:256 * j + 128 * p + 128], QKV[:, 4 + p, co:co + cs], CST[:, C_ID:C_ID + 128])
                                    k.tr(psB[0:cs, 256 * j + 128 * p:256 * j + 128 * p + 128], KDT[:, p, co:co + cs], CST[:, C_ID:C_ID + 128])
                            k.cp(VBt[:, 0:wdt], psA[0:64, 0:wdt], eng="act")
                            k.cp(KDt[:, 0:wdt], psB[0:64, 0:wdt])
                            if DNL < 6:
                                continue
                            for j, (co, cs, sq_) in enumerate(bat):
                                S = SDN[sq_]
                                p1 = prot()
                                for h in range(4):
                                    rows = slice(64 * (h % 2), 64 * (h % 2) + 64)
                                    k.mm(p1[0:cs, 64 * h:64 * h + 64], KBG[rows, h // 2, co:co + cs], S[rows, h // 2, :], ser=True)
                                RB, VN, OTM = W[4], W[5], W[6]
                                k.tt(RB[0:cs, 0:256], VBt[0:cs, 256 * j:256 * j + 256], p1[0:cs, 0:256], ALU.subtract)
                                if KSQ < 2:
                                    continue
                                p2 = prot()
                                for h in range(4):
                                    k.mm(p2[0:cs, 64 * h:64 * h + 64], sl(AT, j, h, cs, cs), RB[0:cs, 64 * h:64 * h + 64])
                                k.cp(VN[0:cs, 0:256], p2[0:cs, 0:256], eng="act")
                                if KSQ < 3:
                                    continue
                                p3 = prot()
                                for h in range(4):
                                    rows = slice(64 * (h % 2), 64 * (h % 2) + 64)
                                    k.mm(p3[0:cs, 64 * h:64 * h + 64], QG[rows, h // 2, co:co + cs], S[rows, h // 2, :], ser=True)
                                k.cp(OTM[0:cs, 0:256], p3[0:cs, 0:256], eng="act")
                                p3b = prot()
                                for h in range(4):
                                    k.mm(p3b[0:cs, 64 * h:64 * h + 64], sl(AinT, j, h, cs, cs), VN[0:cs, 64 * h:64 * h + 64])
                                k.tt(OTM[0:cs, 0:256], OTM[0:cs, 0:256], p3b[0:cs, 0:256], ALU.add)
                                if KSQ < 4:
                                    continue
                                p4 = pbank(6 + j % 2)
                                for h in range(4):
                                    rows = slice(64 * (h % 2), 64 * (h % 2) + 64)
                                    k.mm(p4[rows, 64 * (h // 2):64 * (h // 2) + 64], KDt[0:cs, 256 * j + 64 * h:256 * j + 64 * h + 64],
                                         VN[0:cs, 64 * h:64 * h + 64], tp=(0, 64 * (h % 2)))
                                if KSQ < 5:
                                    continue
                                for p in range(2):
                                    k.stt(S[:, p, :], S[:, p, :], EGB[:, p, co + cs - 1:co + cs], p4[:, 64 * p:64 * p + 64], ALU.mult, ALU.add)
                                    p5 = prot()
                                    k.tr(p5[:, 0:cs], OTM[0:cs, 128 * p:128 * p + 128], cI(cs))
                                    k.cp(QKV[:, p, co:co + cs], p5[:, 0:cs])
                                if sq_ != 0 or (ti == len(TILES) - 1 and co + cs == n):
                                    k.dma(o_dS[l, sq_], S[:, :, :], is_out=True)
                        if DNL < 7:
                            continue
                        for p in range(2):
                            psg = prot()
                            proj_fm(XB, 0, n, WIN, 776 + 128 * p, psg[:, 0:n])
                            gs = W[7]
                            k.act(gs[:, 0:n], psg[:, 0:n], AF.Silu, bias=pk[:, P_BIN + 18 + p:P_BIN + 19 + p])
                            sq = WB[2 + p]
                            k.act(sq[:, 0:n], QKV[:, p, 0:n], AF.Square)
                            pq = pbank(7)
                            k.mm(pq[:, 0:n], OS[:, 2, :], sq[:, 0:n])
                            rs = W[1]
                            k.act(rs[:, 0:n], pq[:, 0:n], AF.Sqrt, bias=EPS_LN[:, 1:2])
                            k.recip(rs[:, 0:n], rs[:, 0:n])
                            k.stt(rs[:, 0:n], QKV[:, p, 0:n], pk[:, P_DNG:P_DNG + 1], rs[:, 0:n], ALU.mult, ALU.mult)
                            k.tt(catv(6 + p, ti), rs[:, 0:n], gs[:, 0:n], ALU.mult)
                    k.barrier()

                k.dma(WIN[:, :, 0:1024], w_o_d[l], eng="pool")
                for ti, (c0, n, segs) in enumerate(TILES if STAGE >= 5 else []):
                    for oc in range(8):
                        ps = prot()
                        for kc in range(8):
                            k.mm(ps[:, 0:n], WIN[:, kc, 128 * oc:128 * oc + 128], catv(kc, ti), start=(kc == 0), stop=(kc == 7))
                        k.stt(xv(ti, oc), xv(ti, oc), ALPHA, ps[:, 0:n], ALU.mult, ALU.add)
                    layer_norm(ti, P_LN1G, P_LN1B, pk)
                k.barrier()

            with ExitStack() as fsc:
                BIG = k.sb("BIG", [128, NFC, BIGW], BF16, fsc)
                BIGt = [[Tile(BIG.h) for _ in TILES] for _ in range(NFC)]
                XBF = [k.sb("XBF%d" % i, [128, 8, TN], BF16, fsc) for i in range(4)]
                WFG = [k.sb("WFG%d" % i, [128, 8, 256], BF16, fsc) for i in range(2)]
                WFD = [k.sb("WFD%d" % i, [128, NFC, 128], BF16, fsc) for i in range(2)]
                for G in (FFN_GROUPS if STAGE >= 6 else []):
                    g0 = TILES[G[0]][0]
                    bigv = lambda fc, ti, a=0, n_=None: BIGt[fc][ti].v(
                        BIG.h[:, fc, TILES[ti][0] - g0 + a:TILES[ti][0] - g0 + a + (TILES[ti][1] if n_ is None else n_)])
                    for gi, ti in enumerate(G):
                        make_xb(XBF[gi], ti)
                    for fc in range(NFC):
                        wg = WFG[fc % 2]
                        k.dma(wg[:, :, :], wf_in_d[l, fc], eng="pool")
                        for gi, ti in enumerate(G):
                            c0, n, segs = TILES[ti]
                            pg, pu = prot(), prot()
                            proj_fm(XBF[gi], 0, n, wg, 0, pg[:, 0:n])
                            proj_fm(XBF[gi], 0, n, wg, 128, pu[:, 0:n])
                            for (sq_, sc0, sn, pos0) in segs:
                                o = sc0 - c0
                                uu = lambda a, b_: U[:, fc % 2, a:b_]
                                k.cp(uu(0, 2), HISTF[sq_][:, fc, :])
                                k.cp(uu(2, 2 + sn), pg[:, o:o + sn], eng="act")
                                k.cp(HISTF[sq_][:, fc, :], uu(sn, sn + 2))
                                acc = W[fc % 2]
                                k.ts(acc[:, 0:sn], uu(2, 2 + sn), pk[:, P_FCW + 3 * fc + 2:P_FCW + 3 * fc + 3],
                                     pk[:, P_FCB + fc:P_FCB + fc + 1], ALU.mult, ALU.add)
                                for i in range(2):
                                    k.stt(acc[:, 0:sn], uu(i, i + sn), pk[:, P_FCW + 3 * fc + i:P_FCW + 3 * fc + i + 1],
                                          acc[:, 0:sn], ALU.mult, ALU.add)
                                k.act(acc[:, 0:sn], acc[:, 0:sn], AF.Gelu_apprx_tanh)
                                k.tt(bigv(fc, ti, o, sn), acc[:, 0:sn], pu[:, o:o + sn], ALU.mult)
                    for oc in range(8):
                        wd = WFD[oc % 2]
                        k.dma(wd[:, :, :], wf_out_d[l, oc], eng="pool")
                        for gi, ti in enumerate(G):
                            c0, n, segs = TILES[ti]
                            ps = prot()
                            for fc in range(NFC):
                                k.mm(ps[:, 0:n], wd[:, fc, :], bigv(fc, ti), start=(fc == 0), stop=(fc == NFC - 1))
                            k.stt(xv(ti, oc), xv(ti, oc), ALPHA, ps[:, 0:n], ALU.mult, ALU.add)
                    for ti in G:
                        layer_norm(ti, P_LN2G, P_LN2B, pk)
                for s in range(3):
                    k.dma(o_fconv[l, s], HISTF[s][:, :, :], is_out=True)
                k.barrier()

        for ti, (c0, n, segs) in enumerate(TILES):
            k.dma(o_y[:, :, c0:c0 + n], Xt[ti].v(X.h[:, :, c0:c0 + n]), is_out=True)
        k.finish()
        print("instructions:", k.nins)
    return nc


def _fm(a, nch):
    return np.ascontiguousarray(a.reshape(nch, 128).T)


def _consts():
    c = np.zeros((128, CW), np.float32)
    c[:, C_ID:C_ID + 128] = np.eye(128)
    c[:, C_ONE:C_ONE + 128] = 1.0
    c[0:64, C_BD:C_BD + 64] = 1.0
    c[64:128, C_BD + 64:C_BD + 128] = 1.0
    j = np.arange(128)
    c[:, C_UT:C_UT + 128] = (j[:, None] > j[None, :]).astype(np.float32)
    c[:, C_UTI:C_UTI + 128] = (j[:, None] >= j[None, :]).astype(np.float32)
    for h in range(4):
        c[h, C_SEL4 + 128 * h:C_SEL4 + 128 * h + 128] = 1.0
        c[4, C_ZT + 5 * h + h] = 1.0
        c[h, C_ZT + 5 * h + 4] = -1.0
    for p in range(2):
        c[2 * p, C_SELP + 128 * p:C_SELP + 128 * p + 64] = 1.0
        c[2 * p + 1, C_SELP + 128 * p + 64:C_SELP + 128 * p + 128] = 1.0
    cm = np.ones(256, np.float32)
    cm[::64] = 0.0
    c[0:4, C_CM:C_CM + 256] = cm
    cm0 = np.ones(80, np.float32)
    cm0[[0, 16, 48]] = 0.0
    c[0:4, C_CM0:C_CM0 + 80] = cm0
    c[0:64, C_I4:C_I4 + 512] = np.tile(np.eye(64, dtype=np.float32), (1, 8))
    return c


_NC_CACHE = {}


def prepare(inp):
    f = lambda n: np.asarray(inp[n], np.float32)
    xp, xs = f("x_prompt"), f("x_sample")
    B = xp.shape[0]
    cst = _consts()
    gpk = np.concatenate([_fm(f("ln_in_g"), 8), _fm(f("ln_in_b"), 8)], axis=1)
    b_in = f("b_in")
    lpk = np.zeros((NL, 128, PW), np.float32)
    lbd = np.zeros((NL, 128, 2, 2, 128), np.float32)
    for l in range(NL):
        p = lpk[l]
        bi = b_in[l]
        cols = [bi[O_LX:O_LX + 256], bi[O_LG:O_LG + 256], bi[O_FQ:O_FQ + 256], bi[O_FK:O_FK + 256],
                bi[O_SQ:O_SQ + 256], bi[O_SK:O_SK + 256]]
        for i, v in enumerate(cols):
            p[:, P_BIN + 2 * i:P_BIN + 2 * i + 2] = _fm(v, 2)
        p[:, P_BIN + 12:P_BIN + 18] = _fm(bi[O_DQ:O_DQ + 768], 6)
        p[:, P_BIN + 18:P_BIN + 20] = _fm(bi[O_DG:O_DG + 256], 2)
        p[0:4, P_BF] = bi[O_FF:O_FF + 4]
        p[0:4, P_BA] = bi[O_DA:O_DA + 4]
        p[0:4, P_DT] = f("dn_dt_bias")[l]
        p[0:4, P_AL] = f("dn_a_log")[l]
        p[0:4, P_BB] = bi[O_DB:O_DB + 4]
        lcw = f("lru_conv_w")[l]
        for c in range(2):
            for i in range(4):
                p[:, P_LCW + 4 * c + i] = lcw[i, 128 * c:128 * c + 128]
        p[:, P_LCB:P_LCB + 2] = _fm(f("lru_conv_b")[l], 2)
        p[:, P_LBA:P_LBA + 2] = _fm(f("lru_b_a")[l], 2)
        p[:, P_LBX:P_LBX + 2] = _fm(f("lru_b_x")[l], 2)
        p[:, P_LAM:P_LAM + 2] = _fm(f("lru_lambda")[l], 2)
        dcw = f("dn_conv_w")[l]
        for c in range(6):
            for i in range(4):
                p[:, P_DCW + 4 * c + i] = dcw[i, 128 * c:128 * c + 128]
        gn = f("grp_norm_g")[l]
        for g in range(3):
            p[:, P_GN + 2 * g:P_GN + 2 * g + 2] = _fm(gn[g], 2)
        p[:, P_DNG] = np.tile(f("dn_norm_g")[l], 2)
        p[:, P_LN1G:P_LN1G + 8] = _fm(f("ln1_g")[l], 8)
        p[:, P_LN1B:P_LN1B + 8] = _fm(f("ln1_b")[l], 8)
        p[:, P_LN2G:P_LN2G + 8] = _fm(f("ln2_g")[l], 8)
        p[:, P_LN2B:P_LN2B + 8] = _fm(f("ln2_b")[l], 8)
        fcw = f("ffn_conv_w")[l]
        for c in range(NFC):
            for i in range(3):
                p[:, P_FCW + 3 * c + i] = fcw[i, 128 * c:128 * c + 128]
        p[:, P_FCB:P_FCB + NFC] = _fm(f("ffn_conv_b")[l], NFC)
        for gi, nm in enumerate(("lru_w_a", "lru_w_x")):
            w = f(nm)[l]
            for n in range(4):
                c, hf = n // 2, n % 2
                lbd[l, 64 * hf:64 * hf + 64, gi, c, 64 * hf:64 * hf + 64] = w[n]
    pc = lambda w: np.ascontiguousarray(w.reshape(NL, -1, 128, w.shape[-1]).transpose(0, 2, 1, 3))
    w_in = f("w_in")
    shared = dict(cst=cst, gpk=gpk, lpk=lpk, lbd=lbd, b_in=b_in, w_o=pc(f("w_o")))
    for g, (o_, n_) in {"A": (O_LX, 512), "B": (O_FQ, 772), "C": (O_SQ, 768), "D": (O_DQ, 1032)}.items():
        shared["w_in" + g] = pc(w_in[:, :, o_:o_ + n_])
    wfi = f("ffn_w_in")
    gu = np.stack([wfi[:, :, :DFF].reshape(NL, 8, 128, NFC, 128), wfi[:, :, DFF:].reshape(NL, 8, 128, NFC, 128)], axis=4)
    shared["ffn_w_in"] = np.ascontiguousarray(gu.transpose(0, 3, 2, 1, 4, 5).reshape(NL, NFC, 128, 8, 256))
    wfo = f("ffn_w_out").reshape(NL, NFC, 128, 8, 128)
    shared["ffn_w_out"] = np.ascontiguousarray(wfo.transpose(0, 3, 2, 1, 4))
    if os.environ.get('KSWAP'):
        for k_ in ("lpk", "lbd", "w_inA", "w_inB", "w_inC", "w_inD", "b_in", "w_o", "ffn_w_in", "ffn_w_out"):
            a = np.array(shared[k_]); a[1] = a[0]; shared[k_] = a
    meta = f("meta_tokens")
    fmN = lambda a, nch: np.ascontiguousarray(a.T.reshape(nch, 128, -1).transpose(1, 0, 2))
    in_maps = []
    for core in range(B):
        sb = [2 * core, 2 * core + 1]
        xall = np.concatenate([meta, xs[sb[0]], xs[sb[1]], xp[core]], axis=0)
        m = dict(shared)
        m["xT"] = fmN(xall, 8)
        stk = lambda fn: np.ascontiguousarray(np.stack([np.stack([fn(l, b) for b in sb]) for l in range(NL)]))
        m["st_lconv"] = stk(lambda l, b: fmN(f("state_lru_conv")[l, b], 2))
        m["st_lh"] = stk(lambda l, b: _fm(f("state_lru_h")[l, b], 2))
        m["st_fk"] = stk(lambda l, b: fmN(f("cache_fox_k")[l, b].reshape(PAST, 256), 2))
        m["st_fv"] = stk(lambda l, b: f("cache_fox_v")[l, b].reshape(PAST, 256))
        m["st_flf"] = stk(lambda l, b: f("cache_fox_logf")[l, b].T)
        m["st_sk"] = stk(lambda l, b: fmN(f("cache_sb_k")[l, b].reshape(PAST, 256), 2))
        m["st_sv"] = stk(lambda l, b: f("cache_sb_v")[l, b].reshape(PAST, 256))
        m["st_dconv"] = stk(lambda l, b: fmN(f("state_dn_conv")[l, b], 6))
        m["st_dS"] = stk(lambda l, b: f("state_dn_S")[l, b].reshape(2, 2, 64, 64).transpose(1, 2, 0, 3).reshape(128, 2, 64))
        m["st_fconv"] = stk(lambda l, b: fmN(f("state_ffn_conv")[l, b], NFC))
        in_maps.append({k_: np.ascontiguousarray(v, dtype=np.float32) for k_, v in m.items()})
    return in_maps


def kernel(**inp):
    in_maps = prepare(inp)
    B = len(in_maps)
    if "nc" not in _NC_CACHE:
        _NC_CACHE["nc"] = build_program()
    res = run_bass_kernel_spmd(_NC_CACHE["nc"], in_maps, core_ids=list(range(B)))
    return assemble(res.results)


def assemble(R):
    B = len(R)
    Lp = NMETA + SEQ
    tm = lambda a: a.transpose(1, 0, 2).reshape(a.shape[1] * 128, a.shape[2]).T
    pcols = np.r_[0:16, 80:NT]
    y_p = np.stack([tm(R[c]["o_y"])[80:] for c in range(B)])
    y_s = np.stack([tm(R[c]["o_y"])[16 + 32 * j:48 + 32 * j] for c in range(B) for j in range(2)])

    def gather(fn_p, fn_s):
        P = np.stack([np.stack([fn_p(R[c], l) for c in range(B)]) for l in range(NL)])
        S = np.stack([np.stack([fn_s(R[c], l, j) for c in range(B) for j in range(2)]) for l in range(NL)])
        return P.astype(np.float32), S.astype(np.float32)

    scol = lambda j: slice(16 + 32 * j, 48 + 32 * j)
    lconv = gather(lambda r, l: tm(r["o_lconv"][l, 0]), lambda r, l, j: tm(r["o_lconv"][l, 1 + j]))
    lh = gather(lambda r, l: r["o_lh"][l, 0].T.reshape(256), lambda r, l, j: r["o_lh"][l, 1 + j].T.reshape(256))
    fk = gather(lambda r, l: tm(r["o_fk"][l])[pcols].reshape(Lp, 4, 64), lambda r, l, j: tm(r["o_fk"][l])[scol(j)].reshape(DSEQ, 4, 64))
    fv = gather(lambda r, l: r["o_fv"][l][pcols].reshape(Lp, 4, 64), lambda r, l, j: r["o_fv"][l][scol(j)].reshape(DSEQ, 4, 64))
    flf = gather(lambda r, l: r["o_flf"][l].T[pcols], lambda r, l, j: r["o_flf"][l].T[scol(j)])
    sk = gather(lambda r, l: tm(r["o_sk"][l])[pcols].reshape(Lp, 4, 64), lambda r, l, j: tm(r["o_sk"][l])[scol(j)].reshape(DSEQ, 4, 64))
    sv = gather(lambda r, l: r["o_sv"][l][pcols].reshape(Lp, 4, 64), lambda r, l, j: r["o_sv"][l][scol(j)].reshape(DSEQ, 4, 64))
    dconv = gather(lambda r, l: tm(r["o_dconv"][l, 0]), lambda r, l, j: tm(r["o_dconv"][l, 1 + j]))
    unS = lambda a: a.reshape(2, 64, 2, 64).transpose(2, 0, 1, 3).reshape(4, 64, 64)
    dS = gather(lambda r, l: unS(r["o_dS"][l, 0]), lambda r, l, j: unS(r["o_dS"][l, 1 + j]))
    fconv = gather(lambda r, l: tm(r["o_fconv"][l, 0]), lambda r, l, j: tm(r["o_fconv"][l, 1 + j]))
    outs = [y_p, y_s] + [g[0] for g in (lconv, lh, fk, fv, flf, sk, sv, dconv, dS, fconv)] + \
           [g[1] for g in (lconv, lh, fk, fv, flf, sk, sv, dconv, dS, fconv)]
    return tuple(np.ascontiguousarray(o, dtype=np.float32) for o in outs)
```

```python
import math
import os
import numpy as np
from contextlib import ExitStack
import concourse.bass as bass
import concourse.mybir as mybir
from concourse.bass_utils import run_bass_kernel_spmd

F32, BF16 = mybir.dt.float32, mybir.dt.bfloat16
ALU = mybir.AluOpType
AF = mybir.ActivationFunctionType
AX = mybir.AxisListType

D = 1024
NL = 2
SEQ = 2048
NMETA = 16
DSEQ = 32
PAST = 1024
NT = 80 + SEQ
DFF = 2816
NFC = DFF // 128
DIN = 3084
ALPHA = (2.0 * NL) ** 0.25
LN_EPS = 1e-5
RMS_EPS = 1e-6
NEG = -30000.0
O_LX, O_LG, O_FQ, O_FK, O_FV, O_FF, O_SQ, O_SK, O_SV, O_DQ, O_DA, O_DB, O_DG = (
    0, 256, 512, 768, 1024, 1280, 1284, 1540, 1796, 2052, 2820, 2824, 2828)

C_ID = 0
C_ONE = 128
C_BD = 256
C_UT = 384
C_SEL4 = 512
C_SELP = 1024
C_CM = 1280
C_CM0 = 1536
C_ZT = 1616
C_I4 = 1636
C_UTI = 2148
CW = 2276

P_BIN = 0
P_BF = 20
P_BA = 21
P_DT = 22
P_AL = 23
P_BB = 24
P_LCW = 25
P_LCB = 33
P_LBA = 35
P_LBX = 37
P_LAM = 39
P_DCW = 41
P_GN = 65
P_DNG = 71
P_LN1G = 72
P_LN1B = 80
P_LN2G = 88
P_LN2B = 96
P_FCW = 104
P_FCB = 170
PW = 192


class Buf:
    __slots__ = ("w", "r")

    def __init__(self):
        self.w = None
        self.r = []


class Tile:
    def __init__(self, h, bufs=None):
        self.h = h
        self.bufs = bufs if bufs is not None else [Buf()]

    def __getitem__(self, idx):
        return View(self.h[idx], self.bufs)

    def v(self, ap):
        return View(ap, self.bufs)


class View:
    __slots__ = ("ap", "bufs")

    def __init__(self, ap, bufs):
        self.ap = ap
        self.bufs = bufs


class K:
    def __init__(self, nc, st, n_dma_sems=32):
        self.nc = nc
        self.st = st
        self.engs = {}
        self.EPOCH = 6000
        for nm, h, ns in (("pe", nc.tensor, 7), ("dve", nc.vector, 5), ("act", nc.scalar, 4),
                          ("pool", nc.gpsimd, 3), ("sp", nc.sync, 2)):
            sems = [st.enter_context(nc.semaphore("s_%s%d" % (nm, i))) for i in range(ns)]
            self.engs[nm] = dict(h=h, sem=sems[0], sems=sems, si=0, cnt=0, seen={}, last=None, total=0)
        self.dsems = {q: [dict(sem=st.enter_context(nc.semaphore("d%s%d" % (q, i))), cnt=0)
                          for i in range(n_dma_sems)] for q in ("sp", "pool")}
        self.drr = {"sp": 0, "pool": 0}
        self.out_toks = []
        self.nins = 0

    def sb(self, name, shape, dt=F32, stack=None):
        self.uid = getattr(self, "uid", 0) + 1
        return Tile((stack or self.st).enter_context(self.nc.sbuf_tensor("%s_%d" % (name, self.uid), list(shape), dt)))

    def barrier(self):
        toks = [o["last"] for o in self.engs.values() if o["last"] is not None]
        toks += [(d["sem"], d["cnt"]) for q in self.dsems.values() for d in q if d["cnt"]]
        for nm in ("pe", "dve", "act", "pool", "sp"):
            e = self.engs[nm]
            for t in toks:
                if not any(t[0] is s_ for s_ in e["sems"]):
                    self._wait(e, t)

    def _wait(self, e, tok):
        if tok is None:
            return
        sem, val = tok
        k = id(sem)
        if e["seen"].get(k, 0) >= val:
            return
        e["seen"][k] = val
        e["h"].wait_ge(sem, val)

    def _need(self, e, rb, wb, pe=False, extra=()):
        need = {}

        def add(tok):
            if tok is None:
                return
            sem, val = tok
            k = id(sem)
            if e["seen"].get(k, 0) >= val:
                return
            if k not in need or need[k][1] < val:
                need[k] = (sem, val)
        for b in rb:
            add(b.w)
        for b in wb:
            if not (pe and b.w is not None and any(b.w[0] is s_ for s_ in e["sems"])):
                add(b.w)
            for t in b.r:
                add(t)
        for t in extra:
            add(t)
        return list(need.values())

    def _emit(self, e, need, fn):
        embed = bool(os.environ.get('KEMBED'))
        for tok in (need[:-1] if embed else need):
            self._wait(e, tok)
        ins_ = fn(e["h"])
        if need and embed:
            sem, val = need[-1]
            e["seen"][id(sem)] = val
            ins_._wait_ge(sem, val)
        return ins_

    def _commit(self, tok, rb, wb):
        for b in rb:
            b.r = [t for t in b.r if t[0] is not tok[0]]
            b.r.append(tok)
        for b in wb:
            b.w = tok
            b.r = []

    @staticmethod
    def _bl(views):
        out = []
        for v in views:
            if v is None or not isinstance(v, View):
                continue
            for b in v.bufs:
                if b not in out:
                    out.append(b)
        return out

    def op(self, eng, fn, outs, ins, extra=()):
        e = self.engs[eng]
        rb, wb = self._bl(ins), self._bl(outs)
        need = self._need(e, rb, wb, pe=(eng == "pe"), extra=extra)
        ins_ = self._emit(e, need, fn)
        e["cnt"] += 1
        e["total"] += 1
        ins_.then_inc(e["sem"], 1)
        tok = (e["sem"], e["cnt"])
        e["last"] = tok
        self._commit(tok, rb, wb)
        self.nins += 1
        if e["cnt"] >= self.EPOCH:
            e["si"] += 1
            e["sem"] = e["sems"][e["si"]]
            e["cnt"] = 0
        return tok

    def dma(self, out, in_, eng="sp", is_out=False):
        e = self.engs[eng]
        o_ap = out.ap if isinstance(out, View) else out
        i_ap = in_.ap if isinstance(in_, View) else in_
        rb, wb = self._bl([in_]), self._bl([out])
        d = self.dsems[eng][self.drr[eng]]
        self.drr[eng] = (self.drr[eng] + 1) % len(self.dsems[eng])
        extra = [(d["sem"], d["cnt"])] if d["cnt"] > 0 else []
        need = self._need(e, rb, wb, extra=extra)
        ins_ = self._emit(e, need, lambda h: h.dma_start(out=o_ap, in_=i_ap))
        d["cnt"] += 16
        ins_.then_inc(d["sem"], 16)
        tok = (d["sem"], d["cnt"])
        self._commit(tok, rb, wb)
        if is_out:
            self.out_toks.append(tok)
        self.nins += 1
        return tok

    def finish(self):
        e = self.engs["sp"]
        last = {}
        for sem, val in self.out_toks:
            last[id(sem)] = (sem, max(val, last.get(id(sem), (None, 0))[1]))
        for tok in last.values():
            self._wait(e, tok)
        for nm in ("pe", "dve", "act", "pool"):
            o = self.engs[nm]
            if o["last"] is not None:
                self._wait(e, o["last"])

    def mm(self, o, lhsT, rhs, start=True, stop=True, tp=None, ser=False):
        kw = {} if tp is None else dict(tile_position=tp)
        extra = ()
        if ser:
            e = self.engs["pe"]
            if e["last"] is not None:
                extra = [e["last"]]
        return self.op("pe", lambda e: e.matmul(o.ap, lhsT.ap, rhs.ap, start=start, stop=stop, **kw),
                       [o], [lhsT, rhs], extra=extra)

    def tr(self, o, a, ident, tp=None):
        kw = {} if tp is None else dict(tile_position=tp)
        return self.op("pe", lambda e: e.transpose(o.ap, a.ap, ident.ap, **kw), [o], [a, ident])

    def act(self, o, a, func, bias=None, scale=None):
        kw = {}
        ins = [a]
        if bias is not None:
            kw["bias"] = bias.ap if isinstance(bias, View) else bias
            ins.append(bias)
        if scale is not None:
            kw["scale"] = scale.ap if isinstance(scale, View) else scale
            ins.append(scale)
        return self.op("act", lambda e: e.activation(o.ap, a.ap, func, **kw), [o], ins)

    def tt(self, o, a, b, op, eng="dve"):
        return self.op(eng, lambda e: e.tensor_tensor(o.ap, a.ap, b.ap, op), [o], [a, b])

    def ts(self, o, a, s1, s2, op0, op1=None, eng="dve"):
        g = lambda s: s.ap if isinstance(s, View) else s
        if op1 is None:
            return self.op(eng, lambda e: e.tensor_scalar(o.ap, a.ap, g(s1), None, op0), [o], [a, s1])
        return self.op(eng, lambda e: e.tensor_scalar(o.ap, a.ap, g(s1), g(s2), op0, op1), [o], [a, s1, s2])

    def stt(self, o, a, s, b, op0, op1):
        g = s.ap if isinstance(s, View) else s
        return self.op("dve", lambda e: e.scalar_tensor_tensor(o.ap, a.ap, g, b.ap, op0, op1), [o], [a, s, b])

    def cp(self, o, a, eng="dve"):
        if eng == "act":
            return self.op("act", lambda e: e.copy(o.ap, a.ap), [o], [a])
        return self.op(eng, lambda e: e.tensor_copy(o.ap, a.ap), [o], [a])

    def memset(self, o, val, eng="dve"):
        return self.op(eng, lambda e: e.memset(o.ap, val), [o], [])

    def recip(self, o, a):
        return self.op("dve", lambda e: e.reciprocal(o.ap, a.ap), [o], [a])

    def scan(self, o, d0, d1, init):
        g = init.ap if isinstance(init, View) else init
        return self.op("dve", lambda e: e.tensor_tensor_scan(o.ap, d0.ap, d1.ap, g, ALU.mult, ALU.add),
                       [o], [d0, d1, init])

    def asel(self, o, a, pattern, cmp, fill, base, cm):
        return self.op("pool", lambda e: e.affine_select(o.ap, a.ap, pattern, cmp, fill, base=base,
                                                         channel_multiplier=cm), [o], [a])


TN = 256
import os
STAGE = int(os.environ.get('KSTAGE', '9'))
NLRUN = int(os.environ.get('KNL', '2'))
DNL = int(os.environ.get('KDN', '9'))
KSQ = int(os.environ.get('KSQ', '9'))
KD2 = int(os.environ.get('KD2', '9'))
KL0 = int(os.environ.get('KL0', '0'))
KDNM = int(os.environ.get('KDNM', '3'))
TILES = [(0, 80, [(1, 16, 32, PAST), (2, 48, 32, PAST), (0, 0, 16, 0)])]
for _j in range(SEQ // TN):
    TILES.append((80 + TN * _j, TN, [(0, 80 + TN * _j, TN, 16 + TN * _j)]))
FFN_GROUPS = [[0, 1, 2, 3], [4, 5, 6], [7, 8]]
BIGW = 80 + 3 * TN


def build_program():
    nc = bass.Bass("TRN2", target_bir_lowering=False)
    dram_in = lambda n, s: nc.dram_tensor(n, list(s), F32, kind="ExternalInput").ap()
    dram_out = lambda n, s: nc.dram_tensor(n, list(s), F32, kind="ExternalOutput").ap()
    xT = dram_in("xT", [128, 8, NT])
    cst_d = dram_in("cst", [128, CW])
    gpk_d = dram_in("gpk", [128, 16])
    lpk_d = dram_in("lpk", [NL, 128, PW])
    lbd_d = dram_in("lbd", [NL, 128, 2, 2, 128])
    GW = {"A": (O_LX, 512), "B": (O_FQ, 772), "C": (O_SQ, 768), "D": (O_DQ, 1032)}
    w_g_d = {g: dram_in("w_in" + g, [NL, 128, 8, n_]) for g, (o_, n_) in GW.items()}
    b_in_d = dram_in("b_in", [NL, DIN])
    w_o_d = dram_in("w_o", [NL, 128, 8, D])
    wf_in_d = dram_in("ffn_w_in", [NL, NFC, 128, 8, 256])
    wf_out_d = dram_in("ffn_w_out", [NL, 8, 128, NFC, 128])
    st_lconv = dram_in("st_lconv", [NL, 2, 128, 2, 3])
    st_lh = dram_in("st_lh", [NL, 2, 128, 2])
    st_fk = dram_in("st_fk", [NL, 2, 128, 2, PAST])
    st_fv = dram_in("st_fv", [NL, 2, PAST, 256])
    st_flf = dram_in("st_flf", [NL, 2, 4, PAST])
    st_sk = dram_in("st_sk", [NL, 2, 128, 2, PAST])
    st_sv = dram_in("st_sv", [NL, 2, PAST, 256])
    st_dconv = dram_in("st_dconv", [NL, 2, 128, 6, 3])
    st_dS = dram_in("st_dS", [NL, 2, 128, 2, 64])
    st_fconv = dram_in("st_fconv", [NL, 2, 128, NFC, 2])
    o_y = dram_out("o_y", [128, 8, NT])
    o_lconv = dram_out("o_lconv", [NL, 3, 128, 2, 3])
    o_lh = dram_out("o_lh", [NL, 3, 128, 2])
    o_fk = dram_out("o_fk", [NL, 128, 2, NT])
    o_fv = dram_out("o_fv", [NL, NT, 256])
    o_flf = dram_out("o_flf", [NL, 4, NT])
    o_sk = dram_out("o_sk", [NL, 128, 2, NT])
    o_sv = dram_out("o_sv", [NL, NT, 256])
    o_dconv = dram_out("o_dconv", [NL, 3, 128, 6, 3])
    o_dS = dram_out("o_dS", [NL, 3, 128, 2, 64])
    o_fconv = dram_out("o_fconv", [NL, 3, 128, NFC, 2])

    with ExitStack() as st:
        k = K(nc, st)
        X = k.sb("X", [128, 8, NT])
        Xt = [Tile(X.h) for _ in TILES]
        xv = lambda ti, c, a=0, n_=None: Xt[ti].v(X.h[:, c, TILES[ti][0] + a:TILES[ti][0] + a + (TILES[ti][1] if n_ is None else n_)])
        PS = st.enter_context(nc.psum_tensor("PS", [128, 8, 512], F32))
        PSb = [Buf() for _ in range(8)]
        rot = [0]

        def pbank(i):
            return Tile(PS[:, i, :], [PSb[i]])

        def prot():
            i = rot[0]
            rot[0] = (rot[0] + 1) % 3
            return pbank(i)

        CST = k.sb("CST", [128, CW])
        CSTB = k.sb("CSTB", [128, 512], BF16)
        OS = k.sb("OS", [128, 3, 128], BF16)
        GPK = k.sb("GPK", [128, 16])
        LPK = [k.sb("LPK%d" % l, [128, PW]) for l in range(NL)]
        LBD = k.sb("LBD", [128, 2, 2, 128])
        W = [k.sb("W%d" % i, [128, TN]) for i in range(12)]
        WB = [k.sb("WB%d" % i, [128, TN], BF16) for i in range(4)]
        U = k.sb("U", [128, 2, TN + 3])
        Uh = [Tile(U.h), Tile(U.h)]
        QKV = k.sb("QKV", [128, 6, TN])
        QT = k.sb("QT", [128, 2, TN], BF16)
        QTN = k.sb("QTN", [128, 2, TN], BF16)
        R4 = [k.sb("R4_%d" % i, [4, 256]) for i in range(3)]
        ONES4 = k.sb("ONES4", [4, 256])
        HISTL = [k.sb("HISTL%d" % s, [128, 2, 3]) for s in range(3)]
        HL = [k.sb("HL%d" % s, [128, 2]) for s in range(3)]
        HISTD = [k.sb("HISTD%d" % s, [128, 6, 3]) for s in range(3)]
        SDN = [k.sb("SDN%d" % s, [128, 2, 64]) for s in range(3)]
        HISTF = [k.sb("HISTF%d" % s, [128, NFC, 2]) for s in range(3)]
        CN = k.sb("CN", [128, 4])
        DNC = k.sb("DNC", [4, 4])
        VTM = k.sb("VTM", [128, 256])
        LSUM = k.sb("LSUM", [128, TN])
        cI = lambda n: CST[0:n, C_ID:C_ID + n]

        k.dma(CST[:, :], cst_d[:, :])
        k.dma(GPK[:, :], gpk_d[:, :])
        for l in range(NL):
            k.dma(LPK[l][:, :], lpk_d[l])
        for ti, (c0, n, segs) in enumerate(TILES):
            k.dma(Xt[ti].v(X.h[:, :, c0:c0 + n]), xT[:, :, c0:c0 + n])
        k.cp(CSTB[:, :], CST[:, 0:512])
        onesb = CSTB
        k.ts(OS[:, 0, :], CST[:, C_ONE:C_ONE + 128], 1.0 / 1024, None, ALU.mult)
        k.ts(OS[:, 1, :], CST[:, C_ONE:C_ONE + 128], 1.0 / 256, None, ALU.mult)
        k.ts(OS[:, 2, :], CST[:, C_BD:C_BD + 128], 1.0 / 64, None, ALU.mult)
        k.memset(ONES4[:, :], 1.0)

        def layer_norm(ti, gcol, bcol, pk):
            c0, n, _ = TILES[ti]
            pm, pq = pbank(6), pbank(7)
            for c in range(8):
                zb, sq = WB[c % 2], WB[2 + c % 2]
                k.cp(zb[:, 0:n], xv(ti, c), eng="pool")
                k.act(sq[:, 0:n], xv(ti, c), AF.Square)
                k.mm(pm[:, 0:n], OS[:, 0, :], zb[:, 0:n], start=(c == 0), stop=(c == 7))
                k.mm(pq[:, 0:n], OS[:, 0, :], sq[:, 0:n], start=(c == 0), stop=(c == 7))
            mean, msq, rstd, nmr = W[8], W[9], W[10], W[11]
            k.cp(mean[:, 0:n], pm[:, 0:n], eng="act")
            k.act(msq[:, 0:n], pm[:, 0:n], AF.Square)
            k.tt(rstd[:, 0:n], pq[:, 0:n], msq[:, 0:n], ALU.subtract)
            k.act(rstd[:, 0:n], rstd[:, 0:n], AF.Sqrt, bias=EPS_LN[:, 0:1])
            k.recip(rstd[:, 0:n], rstd[:, 0:n])
            k.stt(nmr[:, 0:n], mean[:, 0:n], -1.0, rstd[:, 0:n], ALU.mult, ALU.mult)
            for c in range(8):
                t = W[c % 2]
                k.tt(t[:, 0:n], xv(ti, c), rstd[:, 0:n], ALU.mult)
                k.tt(t[:, 0:n], t[:, 0:n], nmr[:, 0:n], ALU.add)
                k.act(xv(ti, c), t[:, 0:n], AF.Identity, bias=pk[:, bcol + c:bcol + c + 1],
                      scale=pk[:, gcol + c:gcol + c + 1])

        EPS_LN = k.sb("EPSLN", [128, 2])
        k.memset(EPS_LN[:, 0:1], LN_EPS)
        k.memset(EPS_LN[:, 1:2], RMS_EPS)

        def make_xb(xb, ti):
            c0, n, _ = TILES[ti]
            for c in range(8):
                k.cp(xb[:, c, 0:n], xv(ti, c), eng=("pool" if c % 2 else "dve"))
            return xb

        def load_w(dst, src_ap, eng="pool"):
            k.dma(dst, src_ap.rearrange("(c p) n -> p c n", p=128), eng=eng)

        def proj_fm(xb, a, n, wt, wcol, ps_view, M=128):
            for c in range(8):
                k.mm(ps_view, wt[:, c, wcol:wcol + M], xb[:, c, a:a + n], start=(c == 0), stop=(c == 7))

        for ti in range(len(TILES)):
            layer_norm(ti, 0, 8, GPK)

        for l in range(KL0, NLRUN):
            pk = LPK[l]
            k.dma(LBD[:, :, :, :], lbd_d[l])
            k.act(CN[:, 0:2], pk[:, P_LAM:P_LAM + 2], AF.Exp, scale=-1.0)
            k.act(CN[:, 0:2], CN[:, 0:2], AF.Ln, bias=1.0)
            k.ts(CN[:, 2:4], CN[:, 0:2], -16.0, None, ALU.mult)
            k.ts(CN[:, 0:2], CN[:, 0:2], -8.0, None, ALU.mult)
            k.act(DNC[:, 0:1], pk[0:4, P_AL:P_AL + 1], AF.Exp)
            k.ts(DNC[:, 0:1], DNC[:, 0:1], -1.0, None, ALU.mult)
            k.tt(DNC[:, 1:2], pk[0:4, P_BA:P_BA + 1], pk[0:4, P_DT:P_DT + 1], ALU.add)
            for s in range(3):
                if s == 0:
                    for t_ in (HISTL[0][:, :, :], HL[0][:, :], HISTD[0][:, :, :], SDN[0][:, :, :], HISTF[0][:, :, :]):
                        k.memset(t_, 0.0)
                else:
                    k.dma(HISTL[s][:, :, :], st_lconv[l, s - 1])
                    k.dma(HL[s][:, :], st_lh[l, s - 1])
                    k.dma(HISTD[s][:, :, :], st_dconv[l, s - 1])
                    k.dma(SDN[s][:, :, :], st_dS[l, s - 1])
                    k.dma(HISTF[s][:, :, :], st_fconv[l, s - 1])

            with ExitStack() as msc:
                CAT = k.sb("CAT", [128, 8, NT], BF16, msc)
                CATt = [[Tile(CAT.h) for _ in TILES] for _ in range(8)]
                catv = lambda j, ti: CATt[j][ti].v(CAT.h[:, j, TILES[ti][0]:TILES[ti][0] + TILES[ti][1]])
                WIN = k.sb("WIN", [128, 8, 1032], BF16, msc)
                XB = k.sb("XB", [128, 8, TN], BF16, msc)

                def rms_group(ti, srcs, gcols, catbase, six=1):
                    c0, n, _ = TILES[ti]
                    pq = pbank(7)
                    for i, s in enumerate(srcs):
                        sq = WB[2 + i]
                        k.act(sq[:, 0:n], s, AF.Square)
                        k.mm(pq[:, 0:n], OS[:, six, :], sq[:, 0:n], start=(i == 0), stop=(i == len(srcs) - 1))
                    rstd = W[10]
                    k.act(rstd[:, 0:n], pq[:, 0:n], AF.Sqrt, bias=EPS_LN[:, 1:2])
                    k.recip(rstd[:, 0:n], rstd[:, 0:n])
                    return rstd

                k.dma(WIN[:, :, 0:512], w_g_d["A"][l], eng="pool")
                for ti, (c0, n, segs) in enumerate(TILES if STAGE >= 1 else []):
                    make_xb(XB, ti)
                    GT = QKV
                    for c in range(2):
                        ps = prot()
                        proj_fm(XB, 0, n, WIN, 256 + 128 * c, ps[:, 0:n])
                        k.act(GT[:, c, 0:n], ps[:, 0:n], AF.Gelu_apprx_tanh, bias=pk[:, P_BIN + 2 + c:P_BIN + 3 + c])
                    def lru_chain(c, psx):
                        xa, r, ig, a, a2, h = (W[0], W[2], W[3], W[4], W[5], W[6]) if c == 0 else (W[1], W[7], W[8], W[9], W[10], W[11])
                        pr, pi = (pbank(4), pbank(5)) if c == 0 else (pbank(6), pbank(7))
                        for (sq_, sc0, sn, pos0) in segs:
                            o = sc0 - c0
                            k.cp(U[:, c, 0:3], HISTL[sq_][:, c, :]); yield
                            k.act(U[:, c, 3:3 + sn], psx[:, o:o + sn], AF.Identity, bias=pk[:, P_BIN + c:P_BIN + c + 1]); yield
                            k.cp(HISTL[sq_][:, c, :], U[:, c, sn:sn + 3]); yield
                            k.ts(xa[:, 0:sn], U[:, c, 3:3 + sn], pk[:, P_LCW + 4 * c + 3:P_LCW + 4 * c + 4],
                                 pk[:, P_LCB + c:P_LCB + c + 1], ALU.mult, ALU.add); yield
                            for i in range(3):
                                k.stt(xa[:, 0:sn], U[:, c, i:i + sn], pk[:, P_LCW + 4 * c + i:P_LCW + 4 * c + i + 1],
                                      xa[:, 0:sn], ALU.mult, ALU.add); yield
                            k.mm(pr[:, 0:sn], LBD[:, 0, c, :], xa[:, 0:sn]); yield
                            k.mm(pi[:, 0:sn], LBD[:, 1, c, :], xa[:, 0:sn]); yield
                            k.act(r[:, 0:sn], pr[:, 0:sn], AF.Sigmoid, bias=pk[:, P_LBA + c:P_LBA + c + 1]); yield
                            k.act(ig[:, 0:sn], pi[:, 0:sn], AF.Sigmoid, bias=pk[:, P_LBX + c:P_LBX + c + 1]); yield
                            k.act(a[:, 0:sn], r[:, 0:sn], AF.Exp, scale=CN[:, c:c + 1]); yield
                            k.act(a2[:, 0:sn], r[:, 0:sn], AF.Exp, scale=CN[:, 2 + c:3 + c]); yield
                            k.ts(a2[:, 0:sn], a2[:, 0:sn], -1.0, 1.0, ALU.mult, ALU.add); yield
                            k.act(a2[:, 0:sn], a2[:, 0:sn], AF.Sqrt); yield
                            k.tt(ig[:, 0:sn], ig[:, 0:sn], xa[:, 0:sn], ALU.mult); yield
                            k.tt(ig[:, 0:sn], ig[:, 0:sn], a2[:, 0:sn], ALU.mult); yield
                            k.scan(h[:, 0:sn], a[:, 0:sn], ig[:, 0:sn], HL[sq_][:, c:c + 1]); yield
                            k.cp(HL[sq_][:, c:c + 1], h[:, sn - 1:sn]); yield
                            k.tt(QKV[:, 2 + c, o:o + sn], h[:, 0:sn], GT[:, c, o:o + sn], ALU.mult); yield

                    psxs = []
                    for c in range(2):
                        psx = prot()
                        proj_fm(XB, 0, n, WIN, 128 * c, psx[:, 0:n])
                        psxs.append(psx)
                    gens = [lru_chain(0, psxs[0]), lru_chain(1, psxs[1])]
                    while gens:
                        for g_ in list(gens):
                            try:
                                next(g_)
                            except StopIteration:
                                gens.remove(g_)
                    for (sq_, sc0, sn, pos0) in segs:
                        if sq_ != 0 or sc0 + sn == NT:
                            k.dma(o_lconv[l, sq_], HISTL[sq_][:, :, :], is_out=True)
                            k.dma(o_lh[l, sq_], HL[sq_][:, :], is_out=True)
                    rstd = rms_group(ti, [QKV[:, 2, 0:n], QKV[:, 3, 0:n]], P_GN, 0)
                    for i in range(2):
                        k.stt(catv(i, ti), QKV[:, 2 + i, 0:n], pk[:, P_GN + i:P_GN + i + 1], rstd[:, 0:n], ALU.mult, ALU.mult)

                with ExitStack() as asc:
                    KT = k.sb("KT", [128, 2, 2176], BF16, asc)
                    VC = k.sb("VC", [128, 17, 256], BF16, asc)
                    FT = k.sb("FT", [4, 2176], F32, asc)
                    NFK = k.sb("NFK", [128, 17, 4], F32, asc)
                    LS = [LSUM, k.sb("LSUMB", [128, TN], F32, asc)]
                    for grp in (("fox", "sb") if STAGE >= 3 else (("fox",) if STAGE >= 2 else ())):
                        fox = grp == "fox"
                        oq, ov = (O_FQ, O_FV) if fox else (O_SQ, O_SV)
                        bq = P_BIN + (4 if fox else 8)
                        o_k, o_v = (o_fk, o_fv) if fox else (o_sk, o_sv)
                        s_k, s_v = (st_fk, st_fv) if fox else (st_sk, st_sv)
                        catb = 2 if fox else 4
                        wcols = 768 + (4 if fox else 0)
                        k.dma(WIN[:, :, 0:wcols], w_g_d["B" if fox else "C"][l], eng="pool")
                        VBIAS = W[11]
                        k.dma(VBIAS[:, 0:256], b_in_d[l, ov:ov + 256].partition_broadcast(128))
                        for ti, (c0, n, segs) in enumerate(TILES):
                            make_xb(XB, ti)
                            KF = QKV
                            for c in range(2):
                                ps = prot()
                                proj_fm(XB, 0, n, WIN, 128 * c, ps[:, 0:n])
                                k.ts(QT[:, c, 0:n], ps[:, 0:n], pk[:, bq + c:bq + c + 1], 0.125, ALU.add, ALU.mult)
                                if not fox:
                                    k.ts(QTN[:, c, 0:n], ps[:, 0:n], pk[:, bq + c:bq + c + 1], -0.125, ALU.add, ALU.mult)
                                ps = prot()
                                proj_fm(XB, 0, n, WIN, 256 + 128 * c, ps[:, 0:n])
                                k.act(KF[:, c, 0:n], ps[:, 0:n], AF.Identity, bias=pk[:, bq + 2 + c:bq + 3 + c])
                            k.dma(o_k[l, :, :, c0:c0 + n], KF[:, 0:2, 0:n], is_out=True)
                            OB = [W[8], W[9]]
                            for (sq_, sc0, sn, pos0) in segs:
                                o = sc0 - c0
                                if sq_ == 0:
                                    newb = [(0, 16)] if pos0 == 0 else [(1 + (pos0 - 16) // 128 + i, 128) for i in range(sn // 128)]
                                    oldb = [] if pos0 == 0 else [(0, 16)] + [(1 + i, 128) for i in range((pos0 - 16) // 128)]
                                else:
                                    oldb = [(i, 128) for i in range(8)]
                                    newb = [(8, 32)]
                                    for c in range(2):
                                        k.dma(KT[:, c, 0:PAST], s_k[l, sq_ - 1, :, c, :], eng="pool")
                                    k.dma(VC[:, 0:8, :], s_v[l, sq_ - 1].rearrange("(b p) f -> p b f", p=128), eng="pool")
                                kc = 0
                                for (b, nk) in newb:
                                    for c in range(2):
                                        k.cp(KT[:, c, 128 * b:128 * b + nk], KF[:, c, o + kc:o + kc + nk], eng="pool")
                                    pv = prot()
                                    for c in range(8):
                                        k.mm(pv[0:nk, 0:256], XB[:, c, o + kc:o + kc + nk], WIN[:, c, 512:768],
                                             start=(c == 0), stop=(c == 7))
                                    k.tt(VTM[0:nk, :], pv[0:nk, 0:256], VBIAS[0:nk, 0:256], ALU.add)
                                    k.dma(o_v[l, sc0 + kc:sc0 + kc + nk, :], VTM[0:nk, :], is_out=True)
                                    k.cp(VC[0:nk, b, :], VTM[0:nk, :], eng="pool")
                                    kc += nk
                                slot0 = newb[0][0] * 128
                                if fox:
                                    psf = prot()
                                    for c in range(8):
                                        k.mm(psf[0:4, 0:sn], WIN[:, c, 768:772], XB[:, c, o:o + sn], start=(c == 0), stop=(c == 7))
                                    lf, e1, cl = R4[0], R4[1], R4[2]
                                    k.ts(e1[0:4, 0:sn], psf[0:4, 0:sn], pk[0:4, P_BF:P_BF + 1], -1.0, ALU.add, ALU.mult)
                                    k.act(e1[0:4, 0:sn], e1[0:4, 0:sn], AF.Exp)
                                    k.act(e1[0:4, 0:sn], e1[0:4, 0:sn], AF.Ln, bias=1.0)
                                    k.ts(lf[0:4, 0:sn], e1[0:4, 0:sn], -1.0, None, ALU.mult)
                                    k.dma(o_flf[l, :, sc0:sc0 + sn], lf[0:4, 0:sn], is_out=True)
                                    if sq_ != 0:
                                        for hh in range(4):
                                            k.dma(cl[0:4, 0:256], st_flf[l, sq_ - 1, :, 256 * hh:256 * hh + 256])
                                            k.scan(FT[0:4, 256 * hh:256 * hh + 256], ONES4[0:4, 0:256], cl[0:4, 0:256],
                                                   (0.0 if hh == 0 else FT[0:4, 256 * hh - 1:256 * hh]))
                                        init = FT[0:4, PAST - 1:PAST]
                                    elif pos0 == 0:
                                        init = 0.0
                                    else:
                                        init = FT[0:4, slot0 - 1:slot0] if pos0 > 16 else FT[0:4, 15:16]
                                    k.scan(FT[0:4, slot0:slot0 + sn], ONES4[0:4, 0:sn], lf[0:4, 0:sn], init)
                                    for (b, nk) in ((oldb if sq_ != 0 else []) + newb):
                                        pt = prot()
                                        k.tr(pt[0:nk, 0:4], FT[0:4, 128 * b:128 * b + nk], cI(4))
                                        k.ts(NFK[0:nk, b, :], pt[0:nk, 0:4], -1.0, None, ALU.mult)
                                    FQ = [W[4 + h] for h in range(4)]
                                    for h in range(4):
                                        pf = prot()
                                        k.mm(pf[:, 0:sn], CST[0:4, C_SEL4 + 128 * h:C_SEL4 + 128 * h + 128], FT[0:4, slot0:slot0 + sn])
                                        k.cp(FQ[h][:, 0:sn], pf[:, 0:sn], eng="act")
                                blocks = oldb + newb
                                nold = len(oldb)
                                kcq, acc = {}, 0
                                for bi in range(nold, len(blocks)):
                                    kcq[bi] = acc
                                    acc += blocks[bi][1]
                                order = list(range(len(blocks)))
                                if not fox:
                                    order = order[::-1]
                                units = []
                                for pr_ in range(2):
                                    for hh in range(2):
                                        for ii, bi in enumerate(order):
                                            units.append(dict(pr=pr_, hh=hh, h=2 * pr_ + hh, ii=ii, bi=bi, b=blocks[bi][0],
                                                              nk=blocks[bi][1], diag=(bi >= nold), first=(ii == 0),
                                                              last=(ii == len(order) - 1)))
                                NU = len(units)

                                def stage_S(t):
                                    u = units[t]
                                    nk, b, h = u["nk"], u["b"], u["h"]
                                    rows = slice(64 * u["hh"], 64 * u["hh"] + 64)
                                    pS = prot()
                                    k.mm(pS[0:nk, 0:sn], KT[rows, u["pr"], 128 * b:128 * b + nk], QT[rows, u["pr"], o:o + sn])
                                    if fox:
                                        tmp, P = W[t % 4], WB[t % 4]
                                        if u["diag"]:
                                            k.stt(tmp[0:nk, 0:sn], pS[0:nk, 0:sn], 1.0, FQ[h][0:nk, 0:sn], ALU.mult, ALU.add)
                                            k.ts(tmp[0:nk, 0:sn], tmp[0:nk, 0:sn], NFK[0:nk, b, h:h + 1], 60.0, ALU.add, ALU.min)
                                            k.act(P[0:nk, 0:sn], tmp[0:nk, 0:sn], AF.Exp)
                                            k.asel(P[0:nk, 0:sn], P[0:nk, 0:sn], [[1, sn]], ALU.is_ge, 0.0, -kcq[u["bi"]], -1)
                                        else:
                                            k.stt(tmp[0:nk, 0:sn], pS[0:nk, 0:sn], 1.0, FQ[h][0:nk, 0:sn], ALU.mult, ALU.add)
                                            k.act(P[0:nk, 0:sn], tmp[0:nk, 0:sn], AF.Exp, bias=NFK[0:nk, b, h:h + 1])
                                    else:
                                        e_, sp = W[t % 4], W[4 + t % 4]
                                        k.act(e_[0:nk, 0:sn], pS[0:nk, 0:sn], AF.Exp)
                                        k.act(sp[0:nk, 0:sn], e_[0:nk, 0:sn], AF.Ln, bias=1.0)
                                        if u["diag"]:
                                            k.asel(sp[0:nk, 0:sn], sp[0:nk, 0:sn], [[1, sn]], ALU.is_ge, 0.0, -kcq[u["bi"]] - 1, -1)

                                def stage_L(t):
                                    u = units[t]
                                    nk, b = u["nk"], u["b"]
                                    rows = slice(64 * u["hh"], 64 * u["hh"] + 64)
                                    sp, P = W[4 + t % 4], WB[t % 4]
                                    pL = pbank(6 + t % 2)
                                    k.mm(pL[0:nk, 0:sn], CST[0:nk, C_UTI:C_UTI + nk], sp[0:nk, 0:sn], start=True, stop=False)
                                    if not u["first"]:
                                        k.mm(pL[0:nk, 0:sn], CST[:, C_ONE:C_ONE + nk], LS[t % 2][:, 0:sn], start=False, stop=False)
                                    k.mm(pL[0:nk, 0:sn], KT[rows, u["pr"], 128 * b:128 * b + nk], QTN[rows, u["pr"], o:o + sn],
                                         start=False, stop=True, ser=(u["first"] and nk < 128))
                                    k.act(P[0:nk, 0:sn], pL[0:nk, 0:sn], AF.Exp, scale=-1.0)
                                    if u["diag"]:
                                        k.asel(P[0:nk, 0:sn], P[0:nk, 0:sn], [[1, sn]], ALU.is_ge, 0.0, -kcq[u["bi"]] - 1, -1)
                                    if not u["last"]:
                                        nxt, cur = LS[(t + 1) % 2], LS[t % 2]
                                        if u["first"]:
                                            if nk < 128:
                                                k.memset(nxt[:, 0:sn], 0.0)
                                            k.cp(nxt[0:nk, 0:sn], sp[0:nk, 0:sn])
                                        else:
                                            assert nk == 128
                                            k.tt(nxt[:, 0:sn], cur[:, 0:sn], sp[:, 0:sn], ALU.add)

                                def stage_O(t):
                                    u = units[t]
                                    nk, b, h, hh = u["nk"], u["b"], u["h"], u["hh"]
                                    rows = slice(64 * hh, 64 * hh + 64)
                                    P = WB[t % 4]
                                    pO = pbank(3 + u["pr"])
                                    k.mm(pO[rows, 0:sn], VC[0:nk, b, 64 * h:64 * h + 64], P[0:nk, 0:sn], start=u["first"], stop=u["last"], tp=(0, 64 * hh))
                                    if fox:
                                        pD = pbank(5 + u["pr"])
                                        k.mm(pD[rows, 0:sn], onesb[0:nk, 128:192], P[0:nk, 0:sn], start=u["first"], stop=u["last"], tp=(0, 64 * hh))
                                    if u["last"] and hh == 1:
                                        if fox:
                                            rD = W[10]
                                            k.recip(rD[:, 0:sn], pD[:, 0:sn])
                                            k.tt(OB[u["pr"]][:, o:o + sn], pO[:, 0:sn], rD[:, 0:sn], ALU.mult)
                                        else:
                                            k.cp(OB[u["pr"]][:, o:o + sn], pO[:, 0:sn], eng="act")

                                if fox:
                                    for t in range(NU + 2):
                                        if t < NU:
                                            stage_S(t)
                                        if t >= 2:
                                            stage_O(t - 2)
                                else:
                                    for t in range(NU + 3):
                                        if t < NU:
                                            stage_S(t)
                                        if 2 <= t < NU + 2:
                                            stage_L(t - 2)
                                        if t >= 3:
                                            stage_O(t - 3)
                            gc_ = P_GN + (2 if fox else 4)
                            rstd = rms_group(ti, [OB[0][:, 0:n], OB[1][:, 0:n]], gc_, catb)
                            for i in range(2):
                                k.stt(catv(catb + i, ti), OB[i][:, 0:n], pk[:, gc_ + i:gc_ + i + 1], rstd[:, 0:n], ALU.mult, ALU.mult)
                    k.barrier()

                with ExitStack() as dsc:
                    DNP = [k.sb("DNP%d" % i, [128, 2, TN], F32, dsc) for i in range(7)]
                    DNM = [k.sb("DNM%d" % i, [64, 512], F32, dsc) for i in range(6)]
                    DNT = [k.sb("DNT%d" % i, [64, 512], F32, dsc) for i in range(2)]
                    YH = k.sb("YH", [5, 4, 128], F32, dsc)
                    G5 = k.sb("G5", [5, TN], F32, dsc)
                    QN, KN, KB, KBG, QG, KDT, EGB = DNP
                    k.memset(G5[:, :], 1.0)
                    k.dma(WIN[:, :, 0:1032], w_g_d["D"][l], eng="pool")
                    for ti, (c0, n, segs) in enumerate(TILES if (STAGE >= 4 and (KDNM >> l) & 1) else []):
                        make_xb(XB, ti)
                        for c in range(6):
                            ps = prot()
                            proj_fm(XB, 0, n, WIN, 128 * c, ps[:, 0:n])
                            for (sq_, sc0, sn, pos0) in segs:
                                o = sc0 - c0
                                uu = lambda a, b_: U[:, c % 2, a:b_]
                                k.cp(uu(0, 3), HISTD[sq_][:, c, :])
                                k.act(uu(3, 3 + sn), ps[:, o:o + sn], AF.Identity, bias=pk[:, P_BIN + 12 + c:P_BIN + 13 + c])
                                k.cp(HISTD[sq_][:, c, :], uu(sn, sn + 3))
                                acc = W[0]
                                k.ts(acc[:, 0:sn], uu(3, 3 + sn), pk[:, P_DCW + 4 * c + 3:P_DCW + 4 * c + 4], None, ALU.mult)
                                for i in range(3):
                                    k.stt(acc[:, 0:sn], uu(i, i + sn), pk[:, P_DCW + 4 * c + i:P_DCW + 4 * c + i + 1],
                                          acc[:, 0:sn], ALU.mult, ALU.add)
                                k.act(QKV[:, c, o:o + sn], acc[:, 0:sn], AF.Silu)
                        for (sq_, sc0, sn, pos0) in segs:
                            if sq_ != 0 or sc0 + sn == NT:
                                k.dma(o_dconv[l, sq_], HISTD[sq_][:, :, :], is_out=True)
                        if DNL < 2:
                            continue
                        psa, psb = prot(), prot()
                        proj_fm(XB, 0, n, WIN, 768, psa[0:4, 0:n], M=4)
                        proj_fm(XB, 0, n, WIN, 772, psb[0:4, 0:n], M=4)
                        g_, be = R4[0], R4[1]
                        k.act(g_[0:4, 0:n], psa[0:4, 0:n], AF.Exp, bias=DNC[:, 1:2])
                        k.act(g_[0:4, 0:n], g_[0:4, 0:n], AF.Ln, bias=1.0)
                        k.ts(g_[0:4, 0:n], g_[0:4, 0:n], DNC[:, 0:1], None, ALU.mult)
                        k.act(be[0:4, 0:n], psb[0:4, 0:n], AF.Sigmoid, bias=pk[0:4, P_BB:P_BB + 1])
                        cm = CST[0:4, C_CM0:C_CM0 + n] if ti == 0 else CST[0:4, C_CM:C_CM + n]
                        k.scan(G5[0:4, 0:n], cm, g_[0:4, 0:n], 0.0)
                        if KD2 < 2:
                            continue
                        for c in range(4):
                            sq = WB[2 + c % 2]
                            k.act(sq[:, 0:n], QKV[:, c, 0:n], AF.Square)
                            pq = pbank(7)
                            k.mm(pq[:, 0:n], CSTB[:, 256:384], sq[:, 0:n])
                            rs = W[1]
                            k.act(rs[:, 0:n], pq[:, 0:n], AF.Sqrt, bias=EPS_LN[:, 1:2])
                            k.recip(rs[:, 0:n], rs[:, 0:n])
                            if c < 2:
                                k.stt(QN[:, c, 0:n], QKV[:, c, 0:n], 0.125, rs[:, 0:n], ALU.mult, ALU.mult)
                            else:
                                k.tt(KN[:, c - 2, 0:n], QKV[:, c, 0:n], rs[:, 0:n], ALU.mult)
                        if KD2 < 3:
                            continue
                        BBt, GBt = W[2], W[3]
                        for p in range(2):
                            selp = CST[0:4, C_SELP + 128 * p:C_SELP + 128 * p + 128]
                            pb_, pg_ = prot(), prot()
                            k.mm(pb_[:, 0:n], selp, be[0:4, 0:n])
                            k.mm(pg_[:, 0:n], selp, G5[0:4, 0:n])
                            k.cp(BBt[:, 0:n], pb_[:, 0:n], eng="act")
                            k.cp(GBt[:, 0:n], pg_[:, 0:n])
                            k.act(EGB[:, p, 0:n], pg_[:, 0:n], AF.Exp)
                            k.tt(KB[:, p, 0:n], KN[:, p, 0:n], BBt[:, 0:n], ALU.mult)
                            k.tt(KBG[:, p, 0:n], KB[:, p, 0:n], EGB[:, p, 0:n], ALU.mult)
                            k.tt(QG[:, p, 0:n], QN[:, p, 0:n], EGB[:, p, 0:n], ALU.mult)
                            k.tt(QKV[:, 4 + p, 0:n], QKV[:, 4 + p, 0:n], BBt[:, 0:n], ALU.mult)
                            if ti == 0:
                                chs = [(16, 32), (48, 32), (0, 16)]
                            else:
                                chs = [(64 * i, 64) for i in range(n // 64)]
                            for (co, cs) in chs:
                                k.ts(KDT[:, p, co:co + cs], GBt[:, co:co + cs], GBt[:, co + cs - 1:co + cs], -1.0,
                                     ALU.subtract, ALU.mult)
                            k.act(KDT[:, p, 0:n], KDT[:, p, 0:n], AF.Exp)
                            k.tt(KDT[:, p, 0:n], KDT[:, p, 0:n], KN[:, p, 0:n], ALU.mult)
                        if DNL < 3:
                            continue
                        if ti == 0:
                            batches = [[(16, 32, 1), (48, 32, 2)], [(0, 16, 0)]]
                        else:
                            batches = [[(128 * b_ + 64 * i, 64, 0) for i in range(2)] for b_ in range(n // 128)]
                        for bat in batches:
                            nb = len(bat)
                            wdt = nb * 256
                            mcs = max(cs for (_, cs, _) in bat)
                            nlv = {64: 5, 32: 4, 16: 3}[mcs]
                            psA, psB, psC = pbank(3), pbank(4), pbank(5)
                            if mcs < 64:
                                for pz in (psA, psB, psC):
                                    k.memset(pz[0:64, 0:wdt], 0.0)
                            sl = lambda T, j, h, r, c_: T[0:r, (j * 4 + h) * 64:(j * 4 + h) * 64 + c_]
                            for j, (co, cs, sq_) in enumerate(bat):
                                for h in range(4):
                                    py = prot()
                                    k.mm(py[0:5, 0:cs], CST[0:5, C_ZT + 5 * h:C_ZT + 5 * h + 5], G5[0:5, co:co + cs])
                                    k.cp(YH[0:5, h, 64 * j:64 * j + cs], py[0:5, 0:cs], eng="act")
                            for j, (co, cs, sq_) in enumerate(bat):
                                for h in range(4):
                                    k.mm(sl(psA, j, h, cs, cs), G5[0:5, co:co + cs], YH[0:5, h, 64 * j:64 * j + cs])
                                    k.mm(sl(psB, j, h, cs, cs), YH[0:5, h, 64 * j:64 * j + cs], G5[0:5, co:co + cs])
                            E1, E2 = DNM[0], DNM[1]
                            k.ts(E1[:, 0:wdt], psA[0:64, 0:wdt], 0.0, None, ALU.min)
                            k.act(E1[:, 0:wdt], E1[:, 0:wdt], AF.Exp)
                            k.ts(E2[:, 0:wdt], psB[0:64, 0:wdt], 0.0, None, ALU.min)
                            k.act(E2[:, 0:wdt], E2[:, 0:wdt], AF.Exp)
                            for j, (co, cs, sq_) in enumerate(bat):
                                for h in range(4):
                                    rows = slice(64 * (h % 2), 64 * (h % 2) + 64)
                                    p = h // 2
                                    k.mm(sl(psA, j, h, cs, cs), KB[rows, p, co:co + cs], KN[rows, p, co:co + cs])
                                    k.mm(sl(psB, j, h, cs, cs), KN[rows, p, co:co + cs], KB[rows, p, co:co + cs])
                                    k.mm(sl(psC, j, h, cs, cs), KN[rows, p, co:co + cs], QN[rows, p, co:co + cs])
                            Pm, PTm, AinT, AT = DNM[2], DNM[3], DNM[4], DNM[5]
                            k.tt(Pm[:, 0:wdt], psA[0:64, 0:wdt], E1[:, 0:wdt], ALU.mult)
                            k.asel(Pm[:, 0:wdt], Pm[:, 0:wdt], [[0, nb * 4], [-1, 64]], ALU.is_ge, 0.0, -1, 1)
                            k.tt(PTm[:, 0:wdt], psB[0:64, 0:wdt], E2[:, 0:wdt], ALU.mult)
                            k.asel(PTm[:, 0:wdt], PTm[:, 0:wdt], [[0, nb * 4], [1, 64]], ALU.is_ge, 0.0, -1, -1)
                            k.tt(AinT[:, 0:wdt], psC[0:64, 0:wdt], E2[:, 0:wdt], ALU.mult)
                            k.asel(AinT[:, 0:wdt], AinT[:, 0:wdt], [[0, nb * 4], [1, 64]], ALU.is_ge, 0.0, 0, -1)
                            k.stt(AT[:, 0:wdt], PTm[:, 0:wdt], -1.0, CST[0:64, C_I4:C_I4 + wdt], ALU.mult, ALU.add)
                            if DNL < 4:
                                continue
                            Pn, PTn = DNM[0], DNM[1]
                            for lv in range(nlv):
                                for j, (co, cs, sq_) in enumerate(bat):
                                    for h in range(4):
                                        k.mm(sl(psA, j, h, cs, cs), sl(PTm, j, h, cs, cs), sl(Pm, j, h, cs, cs))
                                        if lv < nlv - 1:
                                            k.mm(sl(psB, j, h, cs, cs), sl(Pm, j, h, cs, cs), sl(PTm, j, h, cs, cs))
                                k.cp(Pn[:, 0:wdt], psA[0:64, 0:wdt], eng="act")
                                if lv < nlv - 1:
                                    k.cp(PTn[:, 0:wdt], psB[0:64, 0:wdt])
                                for j, (co, cs, sq_) in enumerate(bat):
                                    for h in range(4):
                                        k.mm(sl(psC, j, h, cs, cs), sl(Pn, j, h, cs, cs), sl(AT, j, h, cs, cs))
                                k.tt(AT[:, 0:wdt], AT[:, 0:wdt], psC[0:64, 0:wdt], ALU.add)
                                Pm, Pn = Pn, Pm
                                PTm, PTn = PTn, PTm
                            if DNL < 5:
                                continue
                            VBt, KDt = DNT
                            for j, (co, cs, sq_) in enumerate(bat):
                                for p in range(2):
                                    k.tr(psA[0:cs, 256 * j + 128 * p:256 * j + 128 * p + 128], QKV[:, 4 + p, co:co + cs], CST[:, C_ID:C_ID + 128])
                                    k.tr(psB[0:cs, 256 * j + 128 * p:256 * j + 128 * p + 128], KDT[:, p, co:co + cs], CST[:, C_ID:C_ID + 128])
                            k.cp(VBt[:, 0:wdt], psA[0:64, 0:wdt], eng="act")
                            k.cp(KDt[:, 0:wdt], psB[0:64, 0:wdt])
                            if DNL < 6:
                                continue
                            for j, (co, cs, sq_) in enumerate(bat):
                                S = SDN[sq_]
                                p1 = prot()
                                for h in range(4):
                                    rows = slice(64 * (h % 2), 64 * (h % 2) + 64)
                                    k.mm(p1[0:cs, 64 * h:64 * h + 64], KBG[rows, h // 2, co:co + cs], S[rows, h // 2, :], ser=True)
                                RB, VN, OTM = W[4], W[5], W[6]
                                k.tt(RB[0:cs, 0:256], VBt[0:cs, 256 * j:256 * j + 256], p1[0:cs, 0:256], ALU.subtract)
                                if KSQ < 2:
                                    continue
                                p2 = prot()
                                for h in range(4):
                                    k.mm(p2[0:cs, 64 * h:64 * h + 64], sl(AT, j, h, cs, cs), RB[0:cs, 64 * h:64 * h + 64])
                                k.cp(VN[0:cs, 0:256], p2[0:cs, 0:256], eng="act")
                                if KSQ < 3:
                                    continue
                                p3 = prot()
                                for h in range(4):
                                    rows = slice(64 * (h % 2), 64 * (h % 2) + 64)
                                    k.mm(p3[0:cs, 64 * h:64 * h + 64], QG[rows, h // 2, co:co + cs], S[rows, h // 2, :], ser=True)
                                k.cp(OTM[0:cs, 0:256], p3[0:cs, 0:256], eng="act")
                                p3b = prot()
                                for h in range(4):
                                    k.mm(p3b[0:cs, 64 * h:64 * h + 64], sl(AinT, j, h, cs, cs), VN[0:cs, 64 * h:64 * h + 64])
                                k.tt(OTM[0:cs, 0:256], OTM[0:cs, 0:256], p3b[0:cs, 0:256], ALU.add)
                                if KSQ < 4:
                                    continue
                                p4 = pbank(6 + j % 2)
                                for h in range(4):
                                    rows = slice(64 * (h % 2), 64 * (h % 2) + 64)
                                    k.mm(p4[rows, 64 * (h // 2):64 * (h // 2) + 64], KDt[0:cs, 256 * j + 64 * h:256 * j + 64 * h + 64],
                                         VN[0:cs, 64 * h:64 * h + 64], tp=(0, 64 * (h % 2)))
                                if KSQ < 5:
                                    continue
                                for p in range(2):
                                    k.stt(S[:, p, :], S[:, p, :], EGB[:, p, co + cs - 1:co + cs], p4[:, 64 * p:64 * p + 64], ALU.mult, ALU.add)
                                    p5 = prot()
                                    k.tr(p5[:, 0:cs], OTM[0:cs, 128 * p:128 * p + 128], cI(cs))
                                    k.cp(QKV[:, p, co:co + cs], p5[:, 0:cs])
                                if sq_ != 0 or (ti == len(TILES) - 1 and co + cs == n):
                                    k.dma(o_dS[l, sq_], S[:, :, :], is_out=True)
                        if DNL < 7:
                            continue
                        for p in range(2):
                            psg = prot()
                            proj_fm(XB, 0, n, WIN, 776 + 128 * p, psg[:, 0:n])
                            gs = W[7]
                            k.act(gs[:, 0:n], psg[:, 0:n], AF.Silu, bias=pk[:, P_BIN + 18 + p:P_BIN + 19 + p])
                            sq = WB[2 + p]
                            k.act(sq[:, 0:n], QKV[:, p, 0:n], AF.Square)
                            pq = pbank(7)
                            k.mm(pq[:, 0:n], OS[:, 2, :], sq[:, 0:n])
                            rs = W[1]
                            k.act(rs[:, 0:n], pq[:, 0:n], AF.Sqrt, bias=EPS_LN[:, 1:2])
                            k.recip(rs[:, 0:n], rs[:, 0:n])
                            k.stt(rs[:, 0:n], QKV[:, p, 0:n], pk[:, P_DNG:P_DNG + 1], rs[:, 0:n], ALU.mult, ALU.mult)
                            k.tt(catv(6 + p, ti), rs[:, 0:n], gs[:, 0:n], ALU.mult)
                    k.barrier()

                k.dma(WIN[:, :, 0:1024], w_o_d[l], eng="pool")
                for ti, (c0, n, segs) in enumerate(TILES if STAGE >= 5 else []):
                    for oc in range(8):
                        ps = prot()
                        for kc in range(8):
                            k.mm(ps[:, 0:n], WIN[:, kc, 128 * oc:128 * oc + 128], catv(kc, ti), start=(kc == 0), stop=(kc == 7))
                        k.stt(xv(ti, oc), xv(ti, oc), ALPHA, ps[:, 0:n], ALU.mult, ALU.add)
                    layer_norm(ti, P_LN1G, P_LN1B, pk)
                k.barrier()

            with ExitStack() as fsc:
                BIG = k.sb("BIG", [128, NFC, BIGW], BF16, fsc)
                BIGt = [[Tile(BIG.h) for _ in TILES] for _ in range(NFC)]
                XBF = [k.sb("XBF%d" % i, [128, 8, TN], BF16, fsc) for i in range(4)]
                WFG = [k.sb("WFG%d" % i, [128, 8, 256], BF16, fsc) for i in range(2)]
                WFD = [k.sb("WFD%d" % i, [128, NFC, 128], BF16, fsc) for i in range(2)]
                for G in (FFN_GROUPS if STAGE >= 6 else []):
                    g0 = TILES[G[0]][0]
                    bigv = lambda fc, ti, a=0, n_=None: BIGt[fc][ti].v(
                        BIG.h[:, fc, TILES[ti][0] - g0 + a:TILES[ti][0] - g0 + a + (TILES[ti][1] if n_ is None else n_)])
                    for gi, ti in enumerate(G):
                        make_xb(XBF[gi], ti)
                    for fc in range(NFC):
                        wg = WFG[fc % 2]
                        k.dma(wg[:, :, :], wf_in_d[l, fc], eng="pool")
                        for gi, ti in enumerate(G):
                            c0, n, segs = TILES[ti]
                            pg, pu = prot(), prot()
                            proj_fm(XBF[gi], 0, n, wg, 0, pg[:, 0:n])
                            proj_fm(XBF[gi], 0, n, wg, 128, pu[:, 0:n])
                            for (sq_, sc0, sn, pos0) in segs:
                                o = sc0 - c0
                                uu = lambda a, b_: U[:, fc % 2, a:b_]
                                k.cp(uu(0, 2), HISTF[sq_][:, fc, :])
                                k.cp(uu(2, 2 + sn), pg[:, o:o + sn], eng="act")
                                k.cp(HISTF[sq_][:, fc, :], uu(sn, sn + 2))
                                acc = W[fc % 2]
                                k.ts(acc[:, 0:sn], uu(2, 2 + sn), pk[:, P_FCW + 3 * fc + 2:P_FCW + 3 * fc + 3],
                                     pk[:, P_FCB + fc:P_FCB + fc + 1], ALU.mult, ALU.add)
                                for i in range(2):
                                    k.stt(acc[:, 0:sn], uu(i, i + sn), pk[:, P_FCW + 3 * fc + i:P_FCW + 3 * fc + i + 1],
                                          acc[:, 0:sn], ALU.mult, ALU.add)
                                k.act(acc[:, 0:sn], acc[:, 0:sn], AF.Gelu_apprx_tanh)
                                k.tt(bigv(fc, ti, o, sn), acc[:, 0:sn], pu[:, o:o + sn], ALU.mult)
                    for oc in range(8):
                        wd = WFD[oc % 2]
                        k.dma(wd[:, :, :], wf_out_d[l, oc], eng="pool")
                        for gi, ti in enumerate(G):
                            c0, n, segs = TILES[ti]
                            ps = prot()
                            for fc in range(NFC):
                                k.mm(ps[:, 0:n], wd[:, fc, :], bigv(fc, ti), start=(fc == 0), stop=(fc == NFC - 1))
                            k.stt(xv(ti, oc), xv(ti, oc), ALPHA, ps[:, 0:n], ALU.mult, ALU.add)
                    for ti in G:
                        layer_norm(ti, P_LN2G, P_LN2B, pk)
                for s in range(3):
                    k.dma(o_fconv[l, s], HISTF[s][:, :, :], is_out=True)
                k.barrier()

        for ti, (c0, n, segs) in enumerate(TILES):
            k.dma(o_y[:, :, c0:c0 + n], Xt[ti].v(X.h[:, :, c0:c0 + n]), is_out=True)
        k.finish()
        print("instructions:", k.nins)
    return nc


def _fm(a, nch):
    return np.ascontiguousarray(a.reshape(nch, 128).T)


def _consts():
    c = np.zeros((128, CW), np.float32)
    c[:, C_ID:C_ID + 128] = np.eye(128)
    c[:, C_ONE:C_ONE + 128] = 1.0
    c[0:64, C_BD:C_BD + 64] = 1.0
    c[64:128, C_BD + 64:C_BD + 128] = 1.0
    j = np.arange(128)
    c[:, C_UT:C_UT + 128] = (j[:, None] > j[None, :]).astype(np.float32)
    c[:, C_UTI:C_UTI + 128] = (j[:, None] >= j[None, :]).astype(np.float32)
    for h in range(4):
        c[h, C_SEL4 + 128 * h:C_SEL4 + 128 * h + 128] = 1.0
        c[4, C_ZT + 5 * h + h] = 1.0
        c[h, C_ZT + 5 * h + 4] = -1.0
    for p in range(2):
        c[2 * p, C_SELP + 128 * p:C_SELP + 128 * p + 64] = 1.0
        c[2 * p + 1, C_SELP + 128 * p + 64:C_SELP + 128 * p + 128] = 1.0
    cm = np.ones(256, np.float32)
    cm[::64] = 0.0
    c[0:4, C_CM:C_CM + 256] = cm
    cm0 = np.ones(80, np.float32)
    cm0[[0, 16, 48]] = 0.0
    c[0:4, C_CM0:C_CM0 + 80] = cm0
    c[0:64, C_I4:C_I4 + 512] = np.tile(np.eye(64, dtype=np.float32), (1, 8))
    return c


_NC_CACHE = {}


def prepare(inp):
    f = lambda n: np.asarray(inp[n], np.float32)
    xp, xs = f("x_prompt"), f("x_sample")
    B = xp.shape[0]
    cst = _consts()
    gpk = np.concatenate([_fm(f("ln_in_g"), 8), _fm(f("ln_in_b"), 8)], axis=1)
    b_in = f("b_in")
    lpk = np.zeros((NL, 128, PW), np.float32)
    lbd = np.zeros((NL, 128, 2, 2, 128), np.float32)
    for l in range(NL):
        p = lpk[l]
        bi = b_in[l]
        cols = [bi[O_LX:O_LX + 256], bi[O_LG:O_LG + 256], bi[O_FQ:O_FQ + 256], bi[O_FK:O_FK + 256],
                bi[O_SQ:O_SQ + 256], bi[O_SK:O_SK + 256]]
        for i, v in enumerate(cols):
            p[:, P_BIN + 2 * i:P_BIN + 2 * i + 2] = _fm(v, 2)
        p[:, P_BIN + 12:P_BIN + 18] = _fm(bi[O_DQ:O_DQ + 768], 6)
        p[:, P_BIN + 18:P_BIN + 20] = _fm(bi[O_DG:O_DG + 256], 2)
        p[0:4, P_BF] = bi[O_FF:O_FF + 4]
        p[0:4, P_BA] = bi[O_DA:O_DA + 4]
        p[0:4, P_DT] = f("dn_dt_bias")[l]
        p[0:4, P_AL] = f("dn_a_log")[l]
        p[0:4, P_BB] = bi[O_DB:O_DB + 4]
        lcw = f("lru_conv_w")[l]
        for c in range(2):
            for i in range(4):
                p[:, P_LCW + 4 * c + i] = lcw[i, 128 * c:128 * c + 128]
        p[:, P_LCB:P_LCB + 2] = _fm(f("lru_conv_b")[l], 2)
        p[:, P_LBA:P_LBA + 2] = _fm(f("lru_b_a")[l], 2)
        p[:, P_LBX:P_LBX + 2] = _fm(f("lru_b_x")[l], 2)
        p[:, P_LAM:P_LAM + 2] = _fm(f("lru_lambda")[l], 2)
        dcw = f("dn_conv_w")[l]
        for c in range(6):
            for i in range(4):
                p[:, P_DCW + 4 * c + i] = dcw[i, 128 * c:128 * c + 128]
        gn = f("grp_norm_g")[l]
        for g in range(3):
            p[:, P_GN + 2 * g:P_GN + 2 * g + 2] = _fm(gn[g], 2)
        p[:, P_DNG] = np.tile(f("dn_norm_g")[l], 2)
        p[:, P_LN1G:P_LN1G + 8] = _fm(f("ln1_g")[l], 8)
        p[:, P_LN1B:P_LN1B + 8] = _fm(f("ln1_b")[l], 8)
        p[:, P_LN2G:P_LN2G + 8] = _fm(f("ln2_g")[l], 8)
        p[:, P_LN2B:P_LN2B + 8] = _fm(f("ln2_b")[l], 8)
        fcw = f("ffn_conv_w")[l]
        for c in range(NFC):
            for i in range(3):
                p[:, P_FCW + 3 * c + i] = fcw[i, 128 * c:128 * c + 128]
        p[:, P_FCB:P_FCB + NFC] = _fm(f("ffn_conv_b")[l], NFC)
        for gi, nm in enumerate(("lru_w_a", "lru_w_x")):
            w = f(nm)[l]
            for n in range(4):
                c, hf = n // 2, n % 2
                lbd[l, 64 * hf:64 * hf + 64, gi, c, 64 * hf:64 * hf + 64] = w[n]
    pc = lambda w: np.ascontiguousarray(w.reshape(NL, -1, 128, w.shape[-1]).transpose(0, 2, 1, 3))
    w_in = f("w_in")
    shared = dict(cst=cst, gpk=gpk, lpk=lpk, lbd=lbd, b_in=b_in, w_o=pc(f("w_o")))
    for g, (o_, n_) in {"A": (O_LX, 512), "B": (O_FQ, 772), "C": (O_SQ, 768), "D": (O_DQ, 1032)}.items():
        shared["w_in" + g] = pc(w_in[:, :, o_:o_ + n_])
    wfi = f("ffn_w_in")
    gu = np.stack([wfi[:, :, :DFF].reshape(NL, 8, 128, NFC, 128), wfi[:, :, DFF:].reshape(NL, 8, 128, NFC, 128)], axis=4)
    shared["ffn_w_in"] = np.ascontiguousarray(gu.transpose(0, 3, 2, 1, 4, 5).reshape(NL, NFC, 128, 8, 256))
    wfo = f("ffn_w_out").reshape(NL, NFC, 128, 8, 128)
    shared["ffn_w_out"] = np.ascontiguousarray(wfo.transpose(0, 3, 2, 1, 4))
    if os.environ.get('KSWAP'):
        for k_ in ("lpk", "lbd", "w_inA", "w_inB", "w_inC", "w_inD", "b_in", "w_o", "ffn_w_in", "ffn_w_out"):
            a = np.array(shared[k_]); a[1] = a[0]; shared[k_] = a
    meta = f("meta_tokens")
    fmN = lambda a, nch: np.ascontiguousarray(a.T.reshape(nch, 128, -1).transpose(1, 0, 2))
    in_maps = []
    for core in range(B):
        sb = [2 * core, 2 * core + 1]
        xall = np.concatenate([meta, xs[sb[0]], xs[sb[1]], xp[core]], axis=0)
        m = dict(shared)
        m["xT"] = fmN(xall, 8)
        stk = lambda fn: np.ascontiguousarray(np.stack([np.stack([fn(l, b) for b in sb]) for l in range(NL)]))
        m["st_lconv"] = stk(lambda l, b: fmN(f("state_lru_conv")[l, b], 2))
        m["st_lh"] = stk(lambda l, b: _fm(f("state_lru_h")[l, b], 2))
        m["st_fk"] = stk(lambda l, b: fmN(f("cache_fox_k")[l, b].reshape(PAST, 256), 2))
        m["st_fv"] = stk(lambda l, b: f("cache_fox_v")[l, b].reshape(PAST, 256))
        m["st_flf"] = stk(lambda l, b: f("cache_fox_logf")[l, b].T)
        m["st_sk"] = stk(lambda l, b: fmN(f("cache_sb_k")[l, b].reshape(PAST, 256), 2))
        m["st_sv"] = stk(lambda l, b: f("cache_sb_v")[l, b].reshape(PAST, 256))
        m["st_dconv"] = stk(lambda l, b: fmN(f("state_dn_conv")[l, b], 6))
        m["st_dS"] = stk(lambda l, b: f("state_dn_S")[l, b].reshape(2, 2, 64, 64).transpose(1, 2, 0, 3).reshape(128, 2, 64))
        m["st_fconv"] = stk(lambda l, b: fmN(f("state_ffn_conv")[l, b], NFC))
        in_maps.append({k_: np.ascontiguousarray(v, dtype=np.float32) for k_, v in m.items()})
    return in_maps


def kernel(**inp):
    in_maps = prepare(inp)
    B = len(in_maps)
    if "nc" not in _NC_CACHE:
        _NC_CACHE["nc"] = build_program()
    res = run_bass_kernel_spmd(_NC_CACHE["nc"], in_maps, core_ids=list(range(B)))
    return assemble(res.results)


def assemble(R):
    B = len(R)
    Lp = NMETA + SEQ
    tm = lambda a: a.transpose(1, 0, 2).reshape(a.shape[1] * 128, a.shape[2]).T
    pcols = np.r_[0:16, 80:NT]
    y_p = np.stack([tm(R[c]["o_y"])[80:] for c in range(B)])
    y_s = np.stack([tm(R[c]["o_y"])[16 + 32 * j:48 + 32 * j] for c in range(B) for j in range(2)])

    def gather(fn_p, fn_s):
        P = np.stack([np.stack([fn_p(R[c], l) for c in range(B)]) for l in range(NL)])
        S = np.stack([np.stack([fn_s(R[c], l, j) for c in range(B) for j in range(2)]) for l in range(NL)])
        return P.astype(np.float32), S.astype(np.float32)

    scol = lambda j: slice(16 + 32 * j, 48 + 32 * j)
    lconv = gather(lambda r, l: tm(r["o_lconv"][l, 0]), lambda r, l, j: tm(r["o_lconv"][l, 1 + j]))
    lh = gather(lambda r, l: r["o_lh"][l, 0].T.reshape(256), lambda r, l, j: r["o_lh"][l, 1 + j].T.reshape(256))
    fk = gather(lambda r, l: tm(r["o_fk"][l])[pcols].reshape(Lp, 4, 64), lambda r, l, j: tm(r["o_fk"][l])[scol(j)].reshape(DSEQ, 4, 64))
    fv = gather(lambda r, l: r["o_fv"][l][pcols].reshape(Lp, 4, 64), lambda r, l, j: r["o_fv"][l][scol(j)].reshape(DSEQ, 4, 64))
    flf = gather(lambda r, l: r["o_flf"][l].T[pcols], lambda r, l, j: r["o_flf"][l].T[scol(j)])
    sk = gather(lambda r, l: tm(r["o_sk"][l])[pcols].reshape(Lp, 4, 64), lambda r, l, j: tm(r["o_sk"][l])[scol(j)].reshape(DSEQ, 4, 64))
    sv = gather(lambda r, l: r["o_sv"][l][pcols].reshape(Lp, 4, 64), lambda r, l, j: r["o_sv"][l][scol(j)].reshape(DSEQ, 4, 64))
    dconv = gather(lambda r, l: tm(r["o_dconv"][l, 0]), lambda r, l, j: tm(r["o_dconv"][l, 1 + j]))
    unS = lambda a: a.reshape(2, 64, 2, 64).transpose(2, 0, 1, 3).reshape(4, 64, 64)
    dS = gather(lambda r, l: unS(r["o_dS"][l, 0]), lambda r, l, j: unS(r["o_dS"][l, 1 + j]))
    fconv = gather(lambda r, l: tm(r["o_fconv"][l, 0]), lambda r, l, j: tm(r["o_fconv"][l, 1 + j]))
    outs = [y_p, y_s] + [g[0] for g in (lconv, lh, fk, fv, flf, sk, sv, dconv, dS, fconv)] + \
           [g[1] for g in (lconv, lh, fk, fv, flf, sk, sv, dconv, dS, fconv)]
    return tuple(np.ascontiguousarray(o, dtype=np.float32) for o in outs)
```

```python
import math
import os
import numpy as np
from contextlib import ExitStack
import concourse.bass as bass
import concourse.mybir as mybir
from concourse.bass_utils import run_bass_kernel_spmd

F32, BF16 = mybir.dt.float32, mybir.dt.bfloat16
ALU = mybir.AluOpType
AF = mybir.ActivationFunctionType
AX = mybir.AxisListType

D = 1024
NL = 2
SEQ = 2048
NMETA = 16
DSEQ = 32
PAST = 1024
NT = 80 + SEQ
DFF = 2816
NFC = DFF // 128
DIN = 3084
ALPHA = (2.0 * NL) ** 0.25
LN_EPS = 1e-5
RMS_EPS = 1e-6
NEG = -30000.0
O_LX, O_LG, O_FQ, O_FK, O_FV, O_FF, O_SQ, O_SK, O_SV, O_DQ, O_DA, O_DB, O_DG = (
    0, 256, 512, 768, 1024, 1280, 1284, 1540, 1796, 2052, 2820, 2824, 2828)

C_ID = 0
C_ONE = 128
C_BD = 256
C_UT = 384
C_SEL4 = 512
C_SELP = 1024
C_CM = 1280
C_CM0 = 1536
C_ZT = 1616
C_I4 = 1636
C_UTI = 2148
CW = 2276

P_BIN = 0
P_BF = 20
P_BA = 21
P_DT = 22
P_AL = 23
P_BB = 24
P_LCW = 25
P_LCB = 33
P_LBA = 35
P_LBX = 37
P_LAM = 39
P_DCW = 41
P_GN = 65
P_DNG = 71
P_LN1G = 72
P_LN1B = 80
P_LN2G = 88
P_LN2B = 96
P_FCW = 104
P_FCB = 170
PW = 192


class Buf:
    __slots__ = ("w", "r")

    def __init__(self):
        self.w = None
        self.r = []


class Tile:
    def __init__(self, h, bufs=None):
        self.h = h
        self.bufs = bufs if bufs is not None else [Buf()]

    def __getitem__(self, idx):
        return View(self.h[idx], self.bufs)

    def v(self, ap):
        return View(ap, self.bufs)


class View:
    __slots__ = ("ap", "bufs")

    def __init__(self, ap, bufs):
        self.ap = ap
        self.bufs = bufs


class K:
    def __init__(self, nc, st, n_dma_sems=32):
        self.nc = nc
        self.st = st
        self.engs = {}
        self.EPOCH = 6000
        for nm, h, ns in (("pe", nc.tensor, 7), ("dve", nc.vector, 5), ("act", nc.scalar, 4),
                          ("pool", nc.gpsimd, 3), ("sp", nc.sync, 2)):
            sems = [st.enter_context(nc.semaphore("s_%s%d" % (nm, i))) for i in range(ns)]
            self.engs[nm] = dict(h=h, sem=sems[0], sems=sems, si=0, cnt=0, seen={}, last=None, total=0)
        self.dsems = {q: [dict(sem=st.enter_context(nc.semaphore("d%s%d" % (q, i))), cnt=0)
                          for i in range(n_dma_sems)] for q in ("sp", "pool")}
        self.drr = {"sp": 0, "pool": 0}
        self.out_toks = []
        self.nins = 0

    def sb(self, name, shape, dt=F32, stack=None):
        self.uid = getattr(self, "uid", 0) + 1
        return Tile((stack or self.st).enter_context(self.nc.sbuf_tensor("%s_%d" % (name, self.uid), list(shape), dt)))

    def barrier(self):
        toks = [o["last"] for o in self.engs.values() if o["last"] is not None]
        toks += [(d["sem"], d["cnt"]) for q in self.dsems.values() for d in q if d["cnt"]]
        for nm in ("pe", "dve", "act", "pool", "sp"):
            e = self.engs[nm]
            for t in toks:
                if not any(t[0] is s_ for s_ in e["sems"]):
                    self._wait(e, t)

    def _wait(self, e, tok):
        if tok is None:
            return
        sem, val = tok
        k = id(sem)
        if e["seen"].get(k, 0) >= val:
            return
        e["seen"][k] = val
        e["h"].wait_ge(sem, val)

    def _need(self, e, rb, wb, pe=False, extra=()):
        need = {}

        def add(tok):
            if tok is None:
                return
            sem, val = tok
            k = id(sem)
            if e["seen"].get(k, 0) >= val:
                return
            if k not in need or need[k][1] < val:
                need[k] = (sem, val)
        for b in rb:
            add(b.w)
        for b in wb:
            if not (pe and b.w is not None and any(b.w[0] is s_ for s_ in e["sems"])):
                add(b.w)
            for t in b.r:
                add(t)
        for t in extra:
            add(t)
        return list(need.values())

    def _emit(self, e, need, fn):
        embed = bool(os.environ.get('KEMBED'))
        for tok in (need[:-1] if embed else need):
            self._wait(e, tok)
        ins_ = fn(e["h"])
        if need and embed:
            sem, val = need[-1]
            e["seen"][id(sem)] = val
            ins_._wait_ge(sem, val)
        return ins_

    def _commit(self, tok, rb, wb):
        for b in rb:
            b.r = [t for t in b.r if t[0] is not tok[0]]
            b.r.append(tok)
        for b in wb:
            b.w = tok
            b.r = []

    @staticmethod
    def _bl(views):
        out = []
        for v in views:
            if v is None or not isinstance(v, View):
                continue
            for b in v.bufs:
                if b not in out:
                    out.append(b)
        return out

    def op(self, eng, fn, outs, ins, extra=()):
        e = self.engs[eng]
        rb, wb = self._bl(ins), self._bl(outs)
        need = self._need(e, rb, wb, pe=(eng == "pe"), extra=extra)
        ins_ = self._emit(e, need, fn)
        e["cnt"] += 1
        e["total"] += 1
        ins_.then_inc(e["sem"], 1)
        tok = (e["sem"], e["cnt"])
        e["last"] = tok
        self._commit(tok, rb, wb)
        self.nins += 1
        if e["cnt"] >= self.EPOCH:
            e["si"] += 1
            e["sem"] = e["sems"][e["si"]]
            e["cnt"] = 0
        return tok

    def dma(self, out, in_, eng="sp", is_out=False):
        e = self.engs[eng]
        o_ap = out.ap if isinstance(out, View) else out
        i_ap = in_.ap if isinstance(in_, View) else in_
        rb, wb = self._bl([in_]), self._bl([out])
        d = self.dsems[eng][self.drr[eng]]
        self.drr[eng] = (self.drr[eng] + 1) % len(self.dsems[eng])
        extra = [(d["sem"], d["cnt"])] if d["cnt"] > 0 else []
        need = self._need(e, rb, wb, extra=extra)
        ins_ = self._emit(e, need, lambda h: h.dma_start(out=o_ap, in_=i_ap))
        d["cnt"] += 16
        ins_.then_inc(d["sem"], 16)
        tok = (d["sem"], d["cnt"])
        self._commit(tok, rb, wb)
        if is_out:
            self.out_toks.append(tok)
        self.nins += 1
        return tok

    def finish(self):
        e = self.engs["sp"]
        last = {}
        for sem, val in self.out_toks:
            last[id(sem)] = (sem, max(val, last.get(id(sem), (None, 0))[1]))
        for tok in last.values():
            self._wait(e, tok)
        for nm in ("pe", "dve", "act", "pool"):
            o = self.engs[nm]
            if o["last"] is not None:
                self._wait(e, o["last"])

    def mm(self, o, lhsT, rhs, start=True, stop=True, tp=None, ser=False):
        kw = {} if tp is None else dict(tile_position=tp)
        extra = ()
        if ser:
            e = self.engs["pe"]
            if e["last"] is not None:
                extra = [e["last"]]
        return self.op("pe", lambda e: e.matmul(o.ap, lhsT.ap, rhs.ap, start=start, stop=stop, **kw),
                       [o], [lhsT, rhs], extra=extra)

    def tr(self, o, a, ident, tp=None):
        kw = {} if tp is None else dict(tile_position=tp)
        return self.op("pe", lambda e: e.transpose(o.ap, a.ap, ident.ap, **kw), [o], [a, ident])

    def act(self, o, a, func, bias=None, scale=None):
        kw = {}
        ins = [a]
        if bias is not None:
            kw["bias"] = bias.ap if isinstance(bias, View) else bias
            ins.append(bias)
        if scale is not None:
            kw["scale"] = scale.ap if isinstance(scale, View) else scale
            ins.append(scale)
        return self.op("act", lambda e: e.activation(o.ap, a.ap, func, **kw), [o], ins)

    def tt(self, o, a, b, op, eng="dve"):
        return self.op(eng, lambda e: e.tensor_tensor(o.ap, a.ap, b.ap, op), [o], [a, b])

    def ts(self, o, a, s1, s2, op0, op1=None, eng="dve"):
        g = lambda s: s.ap if isinstance(s, View) else s
        if op1 is None:
            return self.op(eng, lambda e: e.tensor_scalar(o.ap, a.ap, g(s1), None, op0), [o], [a, s1])
        return self.op(eng, lambda e: e.tensor_scalar(o.ap, a.ap, g(s1), g(s2), op0, op1), [o], [a, s1, s2])

    def stt(self, o, a, s, b, op0, op1):
        g = s.ap if isinstance(s, View) else s
        return self.op("dve", lambda e: e.scalar_tensor_tensor(o.ap, a.ap, g, b.ap, op0, op1), [o], [a, s, b])

    def cp(self, o, a, eng="dve"):
        if eng == "act":
            return self.op("act", lambda e: e.copy(o.ap, a.ap), [o], [a])
        return self.op(eng, lambda e: e.tensor_copy(o.ap, a.ap), [o], [a])

    def memset(self, o, val, eng="dve"):
        return self.op(eng, lambda e: e.memset(o.ap, val), [o], [])

    def recip(self, o, a):
        return self.op("dve", lambda e: e.reciprocal(o.ap, a.ap), [o], [a])

    def scan(self, o, d0, d1, init):
        g = init.ap if isinstance(init, View) else init
        return self.op("dve", lambda e: e.tensor_tensor_scan(o.ap, d0.ap, d1.ap, g, ALU.mult, ALU.add),
                       [o], [d0, d1, init])

    def asel(self, o, a, pattern, cmp, fill, base, cm):
        return self.op("pool", lambda e: e.affine_select(o.ap, a.ap, pattern, cmp, fill, base=base,
                                                         channel_multiplier=cm), [o], [a])


TN = 256
import os
STAGE = int(os.environ.get('KSTAGE', '9'))
NLRUN = int(os.environ.get('KNL', '2'))
DNL = int(os.environ.get('KDN', '9'))
KSQ = int(os.environ.get('KSQ', '9'))
KD2 = int(os.environ.get('KD2', '9'))
KL0 = int(os.environ.get('KL0', '0'))
KDNM = int(os.environ.get('KDNM', '3'))
TILES = [(0, 80, [(1, 16, 32, PAST), (2, 48, 32, PAST), (0, 0, 16, 0)])]
for _j in range(SEQ // TN):
    TILES.append((80 + TN * _j, TN, [(0, 80 + TN * _j, TN, 16 + TN * _j)]))
FFN_GROUPS = [[0, 1, 2, 3], [4, 5, 6], [7, 8]]
BIGW = 80 + 3 * TN


def build_program():
    nc = bass.Bass("TRN2", target_bir_lowering=False)
    dram_in = lambda n, s: nc.dram_tensor(n, list(s), F32, kind="ExternalInput").ap()
    dram_out = lambda n, s: nc.dram_tensor(n, list(s), F32, kind="ExternalOutput").ap()
    xT = dram_in("xT", [128, 8, NT])
    cst_d = dram_in("cst", [128, CW])
    gpk_d = dram_in("gpk", [128, 16])
    lpk_d = dram_in("lpk", [NL, 128, PW])
    lbd_d = dram_in("lbd", [NL, 128, 2, 2, 128])
    GW = {"A": (O_LX, 512), "B": (O_FQ, 772), "C": (O_SQ, 768), "D": (O_DQ, 1032)}
    w_g_d = {g: dram_in("w_in" + g, [NL, 128, 8, n_]) for g, (o_, n_) in GW.items()}
    b_in_d = dram_in("b_in", [NL, DIN])
    w_o_d = dram_in("w_o", [NL, 128, 8, D])
    wf_in_d = dram_in("ffn_w_in", [NL, NFC, 128, 8, 256])
    wf_out_d = dram_in("ffn_w_out", [NL, 8, 128, NFC, 128])
    st_lconv = dram_in("st_lconv", [NL, 2, 128, 2, 3])
    st_lh = dram_in("st_lh", [NL, 2, 128, 2])
    st_fk = dram_in("st_fk", [NL, 2, 128, 2, PAST])
    st_fv = dram_in("st_fv", [NL, 2, PAST, 256])
    st_flf = dram_in("st_flf", [NL, 2, 4, PAST])
    st_sk = dram_in("st_sk", [NL, 2, 128, 2, PAST])
    st_sv = dram_in("st_sv", [NL, 2, PAST, 256])
    st_dconv = dram_in("st_dconv", [NL, 2, 128, 6, 3])
    st_dS = dram_in("st_dS", [NL, 2, 128, 2, 64])
    st_fconv = dram_in("st_fconv", [NL, 2, 128, NFC, 2])
    o_y = dram_out("o_y", [128, 8, NT])
    o_lconv = dram_out("o_lconv", [NL, 3, 128, 2, 3])
    o_lh = dram_out("o_lh", [NL, 3, 128, 2])
    o_fk = dram_out("o_fk", [NL, 128, 2, NT])
    o_fv = dram_out("o_fv", [NL, NT, 256])
    o_flf = dram_out("o_flf", [NL, 4, NT])
    o_sk = dram_out("o_sk", [NL, 128, 2, NT])
    o_sv = dram_out("o_sv", [NL, NT, 256])
    o_dconv = dram_out("o_dconv", [NL, 3, 128, 6, 3])
    o_dS = dram_out("o_dS", [NL, 3, 128, 2, 64])
    o_fconv = dram_out("o_fconv", [NL, 3, 128, NFC, 2])

    with ExitStack() as st:
        k = K(nc, st)
        X = k.sb("X", [128, 8, NT])
        Xt = [Tile(X.h) for _ in TILES]
        xv = lambda ti, c, a=0, n_=None: Xt[ti].v(X.h[:, c, TILES[ti][0] + a:TILES[ti][0] + a + (TILES[ti][1] if n_ is None else n_)])
        PS = st.enter_context(nc.psum_tensor("PS", [128, 8, 512], F32))
        PSb = [Buf() for _ in range(8)]
        rot = [0]

        def pbank(i):
            return Tile(PS[:, i, :], [PSb[i]])

        rotn = [3]

        def prot():
            i = rot[0] % rotn[0]
            rot[0] = (i + 1) % rotn[0]
            return pbank(i)

        CST = k.sb("CST", [128, CW])
        CSTB = k.sb("CSTB", [128, 512], BF16)
        OS = k.sb("OS", [128, 3, 128], BF16)
        GPK = k.sb("GPK", [128, 16])
        LPK = [k.sb("LPK%d" % l, [128, PW]) for l in range(NL)]
        LBD = k.sb("LBD", [128, 2, 2, 128])
        W = [k.sb("W%d" % i, [128, TN]) for i in range(12)]
        WB = [k.sb("WB%d" % i, [128, TN], BF16) for i in range(4)]
        U = k.sb("U", [128, 2, TN + 3])
        Uh = [Tile(U.h), Tile(U.h)]
        QKV = k.sb("QKV", [128, 6, TN])
        QT = k.sb("QT", [128, 2, TN], BF16)
        QTN = k.sb("QTN", [128, 2, TN], BF16)
        R4 = [k.sb("R4_%d" % i, [4, 256]) for i in range(3)]
        ONES4 = k.sb("ONES4", [4, 256])
        HISTL = [k.sb("HISTL%d" % s, [128, 2, 3]) for s in range(3)]
        HL = [k.sb("HL%d" % s, [128, 2]) for s in range(3)]
        HISTD = [k.sb("HISTD%d" % s, [128, 6, 3]) for s in range(3)]
        SDN = [k.sb("SDN%d" % s, [128, 2, 64]) for s in range(3)]
        HISTF = [k.sb("HISTF%d" % s, [128, NFC, 2]) for s in range(3)]
        CN = k.sb("CN", [128, 4])
        DNC = k.sb("DNC", [4, 4])
        VTM = k.sb("VTM", [128, 256])
        LSUM = k.sb("LSUM", [128, TN])
        cI = lambda n: CST[0:n, C_ID:C_ID + n]

        k.dma(CST[:, :], cst_d[:, :])
        k.dma(GPK[:, :], gpk_d[:, :])
        for l in range(NL):
            k.dma(LPK[l][:, :], lpk_d[l])
        for ti, (c0, n, segs) in enumerate(TILES):
            k.dma(Xt[ti].v(X.h[:, :, c0:c0 + n]), xT[:, :, c0:c0 + n])
        k.cp(CSTB[:, :], CST[:, 0:512])
        onesb = CSTB
        k.ts(OS[:, 0, :], CST[:, C_ONE:C_ONE + 128], 1.0 / 1024, None, ALU.mult)
        k.ts(OS[:, 1, :], CST[:, C_ONE:C_ONE + 128], 1.0 / 256, None, ALU.mult)
        k.ts(OS[:, 2, :], CST[:, C_BD:C_BD + 128], 1.0 / 64, None, ALU.mult)
        k.memset(ONES4[:, :], 1.0)

        def layer_norm(ti, gcol, bcol, pk):
            c0, n, _ = TILES[ti]
            pm, pq = pbank(6), pbank(7)
            for c in range(8):
                zb, sq = WB[c % 2], WB[2 + c % 2]
                k.cp(zb[:, 0:n], xv(ti, c), eng="pool")
                k.act(sq[:, 0:n], xv(ti, c), AF.Square)
                k.mm(pm[:, 0:n], OS[:, 0, :], zb[:, 0:n], start=(c == 0), stop=(c == 7))
                k.mm(pq[:, 0:n], OS[:, 0, :], sq[:, 0:n], start=(c == 0), stop=(c == 7))
            mean, msq, rstd, nmr = W[8], W[9], W[10], W[11]
            k.cp(mean[:, 0:n], pm[:, 0:n], eng="act")
            k.act(msq[:, 0:n], pm[:, 0:n], AF.Square)
            k.tt(rstd[:, 0:n], pq[:, 0:n], msq[:, 0:n], ALU.subtract)
            k.act(rstd[:, 0:n], rstd[:, 0:n], AF.Sqrt, bias=EPS_LN[:, 0:1])
            k.recip(rstd[:, 0:n], rstd[:, 0:n])
            k.stt(nmr[:, 0:n], mean[:, 0:n], -1.0, rstd[:, 0:n], ALU.mult, ALU.mult)
            for c in range(8):
                t = W[c % 2]
                k.tt(t[:, 0:n], xv(ti, c), rstd[:, 0:n], ALU.mult)
                k.tt(t[:, 0:n], t[:, 0:n], nmr[:, 0:n], ALU.add)
                k.act(xv(ti, c), t[:, 0:n], AF.Identity, bias=pk[:, bcol + c:bcol + c + 1],
                      scale=pk[:, gcol + c:gcol + c + 1])

        EPS_LN = k.sb("EPSLN", [128, 2])
        k.memset(EPS_LN[:, 0:1], LN_EPS)
        k.memset(EPS_LN[:, 1:2], RMS_EPS)

        def make_xb(xb, ti):
            c0, n, _ = TILES[ti]
            for c in range(8):
                k.cp(xb[:, c, 0:n], xv(ti, c), eng=("pool" if c % 2 else "dve"))
            return xb

        def load_w(dst, src_ap, eng="pool"):
            k.dma(dst, src_ap.rearrange("(c p) n -> p c n", p=128), eng=eng)

        def proj_fm(xb, a, n, wt, wcol, ps_view, M=128):
            for c in range(8):
                k.mm(ps_view, wt[:, c, wcol:wcol + M], xb[:, c, a:a + n], start=(c == 0), stop=(c == 7))

        for ti in range(len(TILES)):
            layer_norm(ti, 0, 8, GPK)

        for l in range(KL0, NLRUN):
            pk = LPK[l]
            rotn[0] = 3
            k.dma(LBD[:, :, :, :], lbd_d[l])
            k.act(CN[:, 0:2], pk[:, P_LAM:P_LAM + 2], AF.Exp, scale=-1.0)
            k.act(CN[:, 0:2], CN[:, 0:2], AF.Ln, bias=1.0)
            k.ts(CN[:, 2:4], CN[:, 0:2], -16.0, None, ALU.mult)
            k.ts(CN[:, 0:2], CN[:, 0:2], -8.0, None, ALU.mult)
            k.act(DNC[:, 0:1], pk[0:4, P_AL:P_AL + 1], AF.Exp)
            k.ts(DNC[:, 0:1], DNC[:, 0:1], -1.0, None, ALU.mult)
            k.tt(DNC[:, 1:2], pk[0:4, P_BA:P_BA + 1], pk[0:4, P_DT:P_DT + 1], ALU.add)
            for s in range(3):
                if s == 0:
                    for t_ in (HISTL[0][:, :, :], HL[0][:, :], HISTD[0][:, :, :], SDN[0][:, :, :], HISTF[0][:, :, :]):
                        k.memset(t_, 0.0)
                else:
                    k.dma(HISTL[s][:, :, :], st_lconv[l, s - 1])
                    k.dma(HL[s][:, :], st_lh[l, s - 1])
                    k.dma(HISTD[s][:, :, :], st_dconv[l, s - 1])
                    k.dma(SDN[s][:, :, :], st_dS[l, s - 1])
                    k.dma(HISTF[s][:, :, :], st_fconv[l, s - 1])

            with ExitStack() as msc:
                CAT = k.sb("CAT", [128, 8, NT], BF16, msc)
                CATt = [[Tile(CAT.h) for _ in TILES] for _ in range(8)]
                catv = lambda j, ti: CATt[j][ti].v(CAT.h[:, j, TILES[ti][0]:TILES[ti][0] + TILES[ti][1]])
                WIN = k.sb("WIN", [128, 8, 1032], BF16, msc)
                XB = k.sb("XB", [128, 8, TN], BF16, msc)

                def rms_group(ti, srcs, gcols, catbase, six=1):
                    c0, n, _ = TILES[ti]
                    pq = pbank(7)
                    for i, s in enumerate(srcs):
                        sq = WB[2 + i]
                        k.act(sq[:, 0:n], s, AF.Square)
                        k.mm(pq[:, 0:n], OS[:, six, :], sq[:, 0:n], start=(i == 0), stop=(i == len(srcs) - 1))
                    rstd = W[10]
                    k.act(rstd[:, 0:n], pq[:, 0:n], AF.Sqrt, bias=EPS_LN[:, 1:2])
                    k.recip(rstd[:, 0:n], rstd[:, 0:n])
                    return rstd

                k.dma(WIN[:, :, 0:512], w_g_d["A"][l], eng="pool")
                for ti, (c0, n, segs) in enumerate(TILES if STAGE >= 1 else []):
                    make_xb(XB, ti)
                    GT = QKV
                    for c in range(2):
                        ps = prot()
                        proj_fm(XB, 0, n, WIN, 256 + 128 * c, ps[:, 0:n])
                        k.act(GT[:, c, 0:n], ps[:, 0:n], AF.Gelu_apprx_tanh, bias=pk[:, P_BIN + 2 + c:P_BIN + 3 + c])
                    def lru_chain(c, psx):
                        xa, r, ig, a, a2, h = (W[0], W[2], W[3], W[4], W[5], W[6]) if c == 0 else (W[1], W[7], W[8], W[9], W[10], W[11])
                        pr, pi = (pbank(4), pbank(5)) if c == 0 else (pbank(6), pbank(7))
                        for (sq_, sc0, sn, pos0) in segs:
                            o = sc0 - c0
                            k.cp(U[:, c, 0:3], HISTL[sq_][:, c, :]); yield
                            k.act(U[:, c, 3:3 + sn], psx[:, o:o + sn], AF.Identity, bias=pk[:, P_BIN + c:P_BIN + c + 1]); yield
                            k.cp(HISTL[sq_][:, c, :], U[:, c, sn:sn + 3]); yield
                            k.ts(xa[:, 0:sn], U[:, c, 3:3 + sn], pk[:, P_LCW + 4 * c + 3:P_LCW + 4 * c + 4],
                                 pk[:, P_LCB + c:P_LCB + c + 1], ALU.mult, ALU.add); yield
                            for i in range(3):
                                k.stt(xa[:, 0:sn], U[:, c, i:i + sn], pk[:, P_LCW + 4 * c + i:P_LCW + 4 * c + i + 1],
                                      xa[:, 0:sn], ALU.mult, ALU.add); yield
                            k.mm(pr[:, 0:sn], LBD[:, 0, c, :], xa[:, 0:sn]); yield
                            k.mm(pi[:, 0:sn], LBD[:, 1, c, :], xa[:, 0:sn]); yield
                            k.act(r[:, 0:sn], pr[:, 0:sn], AF.Sigmoid, bias=pk[:, P_LBA + c:P_LBA + c + 1]); yield
                            k.act(ig[:, 0:sn], pi[:, 0:sn], AF.Sigmoid, bias=pk[:, P_LBX + c:P_LBX + c + 1]); yield
                            k.act(a[:, 0:sn], r[:, 0:sn], AF.Exp, scale=CN[:, c:c + 1]); yield
                            k.act(a2[:, 0:sn], r[:, 0:sn], AF.Exp, scale=CN[:, 2 + c:3 + c]); yield
                            k.ts(a2[:, 0:sn], a2[:, 0:sn], -1.0, 1.0, ALU.mult, ALU.add); yield
                            k.act(a2[:, 0:sn], a2[:, 0:sn], AF.Sqrt); yield
                            k.tt(ig[:, 0:sn], ig[:, 0:sn], xa[:, 0:sn], ALU.mult); yield
                            k.tt(ig[:, 0:sn], ig[:, 0:sn], a2[:, 0:sn], ALU.mult); yield
                            k.scan(h[:, 0:sn], a[:, 0:sn], ig[:, 0:sn], HL[sq_][:, c:c + 1]); yield
                            k.cp(HL[sq_][:, c:c + 1], h[:, sn - 1:sn]); yield
                            k.tt(QKV[:, 2 + c, o:o + sn], h[:, 0:sn], GT[:, c, o:o + sn], ALU.mult); yield

                    psxs = []
                    for c in range(2):
                        psx = prot()
                        proj_fm(XB, 0, n, WIN, 128 * c, psx[:, 0:n])
                        psxs.append(psx)
                    gens = [lru_chain(0, psxs[0]), lru_chain(1, psxs[1])]
                    while gens:
                        for g_ in list(gens):
                            try:
                                next(g_)
                            except StopIteration:
                                gens.remove(g_)
                    for (sq_, sc0, sn, pos0) in segs:
                        if sq_ != 0 or sc0 + sn == NT:
                            k.dma(o_lconv[l, sq_], HISTL[sq_][:, :, :], is_out=True)
                            k.dma(o_lh[l, sq_], HL[sq_][:, :], is_out=True)
                    rstd = rms_group(ti, [QKV[:, 2, 0:n], QKV[:, 3, 0:n]], P_GN, 0)
                    for i in range(2):
                        k.stt(catv(i, ti), QKV[:, 2 + i, 0:n], pk[:, P_GN + i:P_GN + i + 1], rstd[:, 0:n], ALU.mult, ALU.mult)

                with ExitStack() as asc:
                    KT = k.sb("KT", [128, 2, 2176], BF16, asc)
                    VC = k.sb("VC", [128, 17, 256], BF16, asc)
                    FT = k.sb("FT", [4, 2176], F32, asc)
                    NFK = k.sb("NFK", [128, 17, 4], F32, asc)
                    LS = [LSUM, k.sb("LSUMB", [128, TN], F32, asc)]
                    for grp in (("fox", "sb") if STAGE >= 3 else (("fox",) if STAGE >= 2 else ())):
                        fox = grp == "fox"
                        oq, ov = (O_FQ, O_FV) if fox else (O_SQ, O_SV)
                        bq = P_BIN + (4 if fox else 8)
                        o_k, o_v = (o_fk, o_fv) if fox else (o_sk, o_sv)
                        s_k, s_v = (st_fk, st_fv) if fox else (st_sk, st_sv)
                        catb = 2 if fox else 4
                        wcols = 768 + (4 if fox else 0)
                        k.dma(WIN[:, :, 0:wcols], w_g_d["B" if fox else "C"][l], eng="pool")
                        VBIAS = W[11]
                        k.dma(VBIAS[:, 0:256], b_in_d[l, ov:ov + 256].partition_broadcast(128))
                        for ti, (c0, n, segs) in enumerate(TILES):
                            make_xb(XB, ti)
                            KF = QKV
                            for c in range(2):
                                ps = prot()
                                proj_fm(XB, 0, n, WIN, 128 * c, ps[:, 0:n])
                                k.ts(QT[:, c, 0:n], ps[:, 0:n], pk[:, bq + c:bq + c + 1], 0.125, ALU.add, ALU.mult)
                                if not fox:
                                    k.ts(QTN[:, c, 0:n], ps[:, 0:n], pk[:, bq + c:bq + c + 1], -0.125, ALU.add, ALU.mult)
                                ps = prot()
                                proj_fm(XB, 0, n, WIN, 256 + 128 * c, ps[:, 0:n])
                                k.act(KF[:, c, 0:n], ps[:, 0:n], AF.Identity, bias=pk[:, bq + 2 + c:bq + 3 + c])
                            k.dma(o_k[l, :, :, c0:c0 + n], KF[:, 0:2, 0:n], is_out=True)
                            OB = [W[8], W[9]]
                            for (sq_, sc0, sn, pos0) in segs:
                                o = sc0 - c0
                                if sq_ == 0:
                                    newb = [(0, 16)] if pos0 == 0 else [(1 + (pos0 - 16) // 128 + i, 128) for i in range(sn // 128)]
                                    oldb = [] if pos0 == 0 else [(0, 16)] + [(1 + i, 128) for i in range((pos0 - 16) // 128)]
                                else:
                                    oldb = [(i, 128) for i in range(8)]
                                    newb = [(8, 32)]
                                    for c in range(2):
                                        k.dma(KT[:, c, 0:PAST], s_k[l, sq_ - 1, :, c, :], eng="pool")
                                    k.dma(VC[:, 0:8, :], s_v[l, sq_ - 1].rearrange("(b p) f -> p b f", p=128), eng="pool")
                                kc = 0
                                for (b, nk) in newb:
                                    for c in range(2):
                                        k.cp(KT[:, c, 128 * b:128 * b + nk], KF[:, c, o + kc:o + kc + nk], eng="pool")
                                    pv = prot()
                                    for c in range(8):
                                        k.mm(pv[0:nk, 0:256], XB[:, c, o + kc:o + kc + nk], WIN[:, c, 512:768],
                                             start=(c == 0), stop=(c == 7))
                                    k.tt(VTM[0:nk, :], pv[0:nk, 0:256], VBIAS[0:nk, 0:256], ALU.add)
                                    k.dma(o_v[l, sc0 + kc:sc0 + kc + nk, :], VTM[0:nk, :], is_out=True)
                                    k.cp(VC[0:nk, b, :], VTM[0:nk, :], eng="pool")
                                    kc += nk
                                slot0 = newb[0][0] * 128
                                if fox:
                                    psf = prot()
                                    for c in range(8):
                                        k.mm(psf[0:4, 0:sn], WIN[:, c, 768:772], XB[:, c, o:o + sn], start=(c == 0), stop=(c == 7))
                                    lf, e1, cl = R4[0], R4[1], R4[2]
                                    k.ts(e1[0:4, 0:sn], psf[0:4, 0:sn], pk[0:4, P_BF:P_BF + 1], -1.0, ALU.add, ALU.mult)
                                    k.act(e1[0:4, 0:sn], e1[0:4, 0:sn], AF.Exp)
                                    k.act(e1[0:4, 0:sn], e1[0:4, 0:sn], AF.Ln, bias=1.0)
                                    k.ts(lf[0:4, 0:sn], e1[0:4, 0:sn], -1.0, None, ALU.mult)
                                    k.dma(o_flf[l, :, sc0:sc0 + sn], lf[0:4, 0:sn], is_out=True)
                                    if sq_ != 0:
                                        for hh in range(4):
                                            k.dma(cl[0:4, 0:256], st_flf[l, sq_ - 1, :, 256 * hh:256 * hh + 256])
                                            k.scan(FT[0:4, 256 * hh:256 * hh + 256], ONES4[0:4, 0:256], cl[0:4, 0:256],
                                                   (0.0 if hh == 0 else FT[0:4, 256 * hh - 1:256 * hh]))
                                        init = FT[0:4, PAST - 1:PAST]
                                    elif pos0 == 0:
                                        init = 0.0
                                    else:
                                        init = FT[0:4, slot0 - 1:slot0] if pos0 > 16 else FT[0:4, 15:16]
                                    k.scan(FT[0:4, slot0:slot0 + sn], ONES4[0:4, 0:sn], lf[0:4, 0:sn], init)
                                    for (b, nk) in ((oldb if sq_ != 0 else []) + newb):
                                        pt = prot()
                                        k.tr(pt[0:nk, 0:4], FT[0:4, 128 * b:128 * b + nk], cI(4))
                                        k.ts(NFK[0:nk, b, :], pt[0:nk, 0:4], -1.0, None, ALU.mult)
                                    FQ = [W[4 + h] for h in range(4)]
                                    for h in range(4):
                                        pf = prot()
                                        k.mm(pf[:, 0:sn], CST[0:4, C_SEL4 + 128 * h:C_SEL4 + 128 * h + 128], FT[0:4, slot0:slot0 + sn])
                                        k.cp(FQ[h][:, 0:sn], pf[:, 0:sn], eng="act")
                                blocks = oldb + newb
                                nold = len(oldb)
                                kcq, acc = {}, 0
                                for bi in range(nold, len(blocks)):
                                    kcq[bi] = acc
                                    acc += blocks[bi][1]
                                order = list(range(len(blocks)))
                                if not fox:
                                    order = order[::-1]
                                units = []
                                for pr_ in range(2):
                                    for hh in range(2):
                                        for ii, bi in enumerate(order):
                                            units.append(dict(pr=pr_, hh=hh, h=2 * pr_ + hh, ii=ii, bi=bi, b=blocks[bi][0],
                                                              nk=blocks[bi][1], diag=(bi >= nold), first=(ii == 0),
                                                              last=(ii == len(order) - 1)))
                                NU = len(units)

                                def stage_S(t):
                                    u = units[t]
                                    nk, b, h = u["nk"], u["b"], u["h"]
                                    rows = slice(64 * u["hh"], 64 * u["hh"] + 64)
                                    pS = prot()
                                    k.mm(pS[0:nk, 0:sn], KT[rows, u["pr"], 128 * b:128 * b + nk], QT[rows, u["pr"], o:o + sn])
                                    if fox:
                                        tmp, P = W[t % 4], WB[t % 4]
                                        if u["diag"]:
                                            k.stt(tmp[0:nk, 0:sn], pS[0:nk, 0:sn], 1.0, FQ[h][0:nk, 0:sn], ALU.mult, ALU.add)
                                            k.ts(tmp[0:nk, 0:sn], tmp[0:nk, 0:sn], NFK[0:nk, b, h:h + 1], 60.0, ALU.add, ALU.min)
                                            k.act(P[0:nk, 0:sn], tmp[0:nk, 0:sn], AF.Exp)
                                            k.asel(P[0:nk, 0:sn], P[0:nk, 0:sn], [[1, sn]], ALU.is_ge, 0.0, -kcq[u["bi"]], -1)
                                        else:
                                            k.stt(tmp[0:nk, 0:sn], pS[0:nk, 0:sn], 1.0, FQ[h][0:nk, 0:sn], ALU.mult, ALU.add)
                                            k.act(P[0:nk, 0:sn], tmp[0:nk, 0:sn], AF.Exp, bias=NFK[0:nk, b, h:h + 1])
                                    else:
                                        e_, sp = W[t % 4], W[4 + t % 4]
                                        k.act(e_[0:nk, 0:sn], pS[0:nk, 0:sn], AF.Exp)
                                        k.act(sp[0:nk, 0:sn], e_[0:nk, 0:sn], AF.Ln, bias=1.0)
                                        if u["diag"]:
                                            k.asel(sp[0:nk, 0:sn], sp[0:nk, 0:sn], [[1, sn]], ALU.is_ge, 0.0, -kcq[u["bi"]] - 1, -1)

                                def stage_L(t):
                                    u = units[t]
                                    nk, b = u["nk"], u["b"]
                                    rows = slice(64 * u["hh"], 64 * u["hh"] + 64)
                                    sp, P = W[4 + t % 4], WB[t % 4]
                                    pL = pbank(6 + t % 2)
                                    k.mm(pL[0:nk, 0:sn], CST[0:nk, C_UTI:C_UTI + nk], sp[0:nk, 0:sn], start=True, stop=False)
                                    if not u["first"]:
                                        k.mm(pL[0:nk, 0:sn], CST[:, C_ONE:C_ONE + nk], LS[t % 2][:, 0:sn], start=False, stop=False)
                                    k.mm(pL[0:nk, 0:sn], KT[rows, u["pr"], 128 * b:128 * b + nk], QTN[rows, u["pr"], o:o + sn],
                                         start=False, stop=True, ser=(u["first"] and nk < 128))
                                    k.act(P[0:nk, 0:sn], pL[0:nk, 0:sn], AF.Exp, scale=-1.0)
                                    if u["diag"]:
                                        k.asel(P[0:nk, 0:sn], P[0:nk, 0:sn], [[1, sn]], ALU.is_ge, 0.0, -kcq[u["bi"]] - 1, -1)
                                    if not u["last"]:
                                        nxt, cur = LS[(t + 1) % 2], LS[t % 2]
                                        if u["first"]:
                                            if nk < 128:
                                                k.memset(nxt[:, 0:sn], 0.0)
                                            k.cp(nxt[0:nk, 0:sn], sp[0:nk, 0:sn])
                                        else:
                                            assert nk == 128
                                            k.tt(nxt[:, 0:sn], cur[:, 0:sn], sp[:, 0:sn], ALU.add)

                                def stage_O(t):
                                    u = units[t]
                                    nk, b, h, hh = u["nk"], u["b"], u["h"], u["hh"]
                                    rows = slice(64 * hh, 64 * hh + 64)
                                    P = WB[t % 4]
                                    pO = pbank(3 + u["pr"])
                                    k.mm(pO[rows, 0:sn], VC[0:nk, b, 64 * h:64 * h + 64], P[0:nk, 0:sn], start=u["first"], stop=u["last"], tp=(0, 64 * hh))
                                    if fox:
                                        pD = pbank(5 + u["pr"])
                                        k.mm(pD[rows, 0:sn], onesb[0:nk, 128:192], P[0:nk, 0:sn], start=u["first"], stop=u["last"], tp=(0, 64 * hh))
                                    if u["last"] and hh == 1:
                                        if fox:
                                            rD = W[10]
                                            k.recip(rD[:, 0:sn], pD[:, 0:sn])
                                            k.tt(OB[u["pr"]][:, o:o + sn], pO[:, 0:sn], rD[:, 0:sn], ALU.mult)
                                        else:
                                            k.cp(OB[u["pr"]][:, o:o + sn], pO[:, 0:sn], eng="act")

                                if fox:
                                    for t in range(NU + 2):
                                        if t < NU:
                                            stage_S(t)
                                        if t >= 2:
                                            stage_O(t - 2)
                                else:
                                    for t in range(NU + 3):
                                        if t < NU:
                                            stage_S(t)
                                        if 2 <= t < NU + 2:
                                            stage_L(t - 2)
                                        if t >= 3:
                                            stage_O(t - 3)
                            gc_ = P_GN + (2 if fox else 4)
                            rstd = rms_group(ti, [OB[0][:, 0:n], OB[1][:, 0:n]], gc_, catb)
                            for i in range(2):
                                k.stt(catv(catb + i, ti), OB[i][:, 0:n], pk[:, gc_ + i:gc_ + i + 1], rstd[:, 0:n], ALU.mult, ALU.mult)
                    k.barrier()

                with ExitStack() as dsc:
                    DNP = [k.sb("DNP%d" % i, [128, 2, TN], F32, dsc) for i in range(7)]
                    DNM = [k.sb("DNM%d" % i, [64, 512], F32, dsc) for i in range(6)]
                    DNT = [k.sb("DNT%d" % i, [64, 512], F32, dsc) for i in range(2)]
                    YH = k.sb("YH", [5, 4, 128], F32, dsc)
                    G5 = k.sb("G5", [5, TN], F32, dsc)
                    QN, KN, KB, KBG, QG, KDT, EGB = DNP
                    k.memset(G5[:, :], 1.0)
                    k.dma(WIN[:, :, 0:1032], w_g_d["D"][l], eng="pool")
                    for ti, (c0, n, segs) in enumerate(TILES if (STAGE >= 4 and (KDNM >> l) & 1) else []):
                        make_xb(XB, ti)
                        for c in range(6):
                            ps = prot()
                            proj_fm(XB, 0, n, WIN, 128 * c, ps[:, 0:n])
                            for (sq_, sc0, sn, pos0) in segs:
                                o = sc0 - c0
                                uu = lambda a, b_: U[:, c % 2, a:b_]
                                k.cp(uu(0, 3), HISTD[sq_][:, c, :])
                                k.act(uu(3, 3 + sn), ps[:, o:o + sn], AF.Identity, bias=pk[:, P_BIN + 12 + c:P_BIN + 13 + c])
                                k.cp(HISTD[sq_][:, c, :], uu(sn, sn + 3))
                                acc = W[0]
                                k.ts(acc[:, 0:sn], uu(3, 3 + sn), pk[:, P_DCW + 4 * c + 3:P_DCW + 4 * c + 4], None, ALU.mult)
                                for i in range(3):
                                    k.stt(acc[:, 0:sn], uu(i, i + sn), pk[:, P_DCW + 4 * c + i:P_DCW + 4 * c + i + 1],
                                          acc[:, 0:sn], ALU.mult, ALU.add)
                                k.act(QKV[:, c, o:o + sn], acc[:, 0:sn], AF.Silu)
                        for (sq_, sc0, sn, pos0) in segs:
                            if sq_ != 0 or sc0 + sn == NT:
                                k.dma(o_dconv[l, sq_], HISTD[sq_][:, :, :], is_out=True)
                        if DNL < 2:
                            continue
                        psa, psb = prot(), prot()
                        proj_fm(XB, 0, n, WIN, 768, psa[0:4, 0:n], M=4)
                        proj_fm(XB, 0, n, WIN, 772, psb[0:4, 0:n], M=4)
                        g_, be = R4[0], R4[1]
                        k.act(g_[0:4, 0:n], psa[0:4, 0:n], AF.Exp, bias=DNC[:, 1:2])
                        k.act(g_[0:4, 0:n], g_[0:4, 0:n], AF.Ln, bias=1.0)
                        k.ts(g_[0:4, 0:n], g_[0:4, 0:n], DNC[:, 0:1], None, ALU.mult)
                        k.act(be[0:4, 0:n], psb[0:4, 0:n], AF.Sigmoid, bias=pk[0:4, P_BB:P_BB + 1])
                        cm = CST[0:4, C_CM0:C_CM0 + n] if ti == 0 else CST[0:4, C_CM:C_CM + n]
                        k.scan(G5[0:4, 0:n], cm, g_[0:4, 0:n], 0.0)
                        if KD2 < 2:
                            continue
                        for c in range(4):
                            sq = WB[2 + c % 2]
                            k.act(sq[:, 0:n], QKV[:, c, 0:n], AF.Square)
                            pq = pbank(7)
                            k.mm(pq[:, 0:n], CSTB[:, 256:384], sq[:, 0:n])
                            rs = W[1]
                            k.act(rs[:, 0:n], pq[:, 0:n], AF.Sqrt, bias=EPS_LN[:, 1:2])
                            k.recip(rs[:, 0:n], rs[:, 0:n])
                            if c < 2:
                                k.stt(QN[:, c, 0:n], QKV[:, c, 0:n], 0.125, rs[:, 0:n], ALU.mult, ALU.mult)
                            else:
                                k.tt(KN[:, c - 2, 0:n], QKV[:, c, 0:n], rs[:, 0:n], ALU.mult)
                        if KD2 < 3:
                            continue
                        BBt, GBt = W[2], W[3]
                        for p in range(2):
                            selp = CST[0:4, C_SELP + 128 * p:C_SELP + 128 * p + 128]
                            pb_, pg_ = prot(), prot()
                            k.mm(pb_[:, 0:n], selp, be[0:4, 0:n])
                            k.mm(pg_[:, 0:n], selp, G5[0:4, 0:n])
                            k.cp(BBt[:, 0:n], pb_[:, 0:n], eng="act")
                            k.cp(GBt[:, 0:n], pg_[:, 0:n])
                            k.act(EGB[:, p, 0:n], pg_[:, 0:n], AF.Exp)
                            k.tt(KB[:, p, 0:n], KN[:, p, 0:n], BBt[:, 0:n], ALU.mult)
                            k.tt(KBG[:, p, 0:n], KB[:, p, 0:n], EGB[:, p, 0:n], ALU.mult)
                            k.tt(QG[:, p, 0:n], QN[:, p, 0:n], EGB[:, p, 0:n], ALU.mult)
                            k.tt(QKV[:, 4 + p, 0:n], QKV[:, 4 + p, 0:n], BBt[:, 0:n], ALU.mult)
                            if ti == 0:
                                chs = [(16, 32), (48, 32), (0, 16)]
                            else:
                                chs = [(64 * i, 64) for i in range(n // 64)]
                            for (co, cs) in chs:
                                k.ts(KDT[:, p, co:co + cs], GBt[:, co:co + cs], GBt[:, co + cs - 1:co + cs], -1.0,
                                     ALU.subtract, ALU.mult)
                            k.act(KDT[:, p, 0:n], KDT[:, p, 0:n], AF.Exp)
                            k.tt(KDT[:, p, 0:n], KDT[:, p, 0:n], KN[:, p, 0:n], ALU.mult)
                        if DNL < 3:
                            continue
                        if ti == 0:
                            batches = [[(16, 32, 1), (48, 32, 2)], [(0, 16, 0)]]
                        else:
                            batches = [[(128 * b_ + 64 * i, 64, 0) for i in range(2)] for b_ in range(n // 128)]
                        for bat in batches:
                            nb = len(bat)
                            wdt = nb * 256
                            mcs = max(cs for (_, cs, _) in bat)
                            nlv = {64: 5, 32: 4, 16: 3}[mcs]
                            psA, psB, psC = pbank(3), pbank(4), pbank(5)
                            if mcs < 64:
                                for pz in (psA, psB, psC):
                                    k.memset(pz[0:64, 0:wdt], 0.0)
                            sl = lambda T, j, h, r, c_: T[0:r, (j * 4 + h) * 64:(j * 4 + h) * 64 + c_]
                            for j, (co, cs, sq_) in enumerate(bat):
                                for h in range(4):
                                    py = prot()
                                    k.mm(py[0:5, 0:cs], CST[0:5, C_ZT + 5 * h:C_ZT + 5 * h + 5], G5[0:5, co:co + cs])
                                    k.cp(YH[0:5, h, 64 * j:64 * j + cs], py[0:5, 0:cs], eng="act")
                            for j, (co, cs, sq_) in enumerate(bat):
                                for h in range(4):
                                    k.mm(sl(psA, j, h, cs, cs), G5[0:5, co:co + cs], YH[0:5, h, 64 * j:64 * j + cs])
                                    k.mm(sl(psB, j, h, cs, cs), YH[0:5, h, 64 * j:64 * j + cs], G5[0:5, co:co + cs])
                            E1, E2 = DNM[0], DNM[1]
                            k.ts(E1[:, 0:wdt], psA[0:64, 0:wdt], 0.0, None, ALU.min)
                            k.act(E1[:, 0:wdt], E1[:, 0:wdt], AF.Exp)
                            k.ts(E2[:, 0:wdt], psB[0:64, 0:wdt], 0.0, None, ALU.min)
                            k.act(E2[:, 0:wdt], E2[:, 0:wdt], AF.Exp)
                            for j, (co, cs, sq_) in enumerate(bat):
                                for h in range(4):
                                    rows = slice(64 * (h % 2), 64 * (h % 2) + 64)
                                    p = h // 2
                                    k.mm(sl(psA, j, h, cs, cs), KB[rows, p, co:co + cs], KN[rows, p, co:co + cs])
                                    k.mm(sl(psB, j, h, cs, cs), KN[rows, p, co:co + cs], KB[rows, p, co:co + cs])
                                    k.mm(sl(psC, j, h, cs, cs), KN[rows, p, co:co + cs], QN[rows, p, co:co + cs])
                            Pm, PTm, AinT, AT = DNM[2], DNM[3], DNM[4], DNM[5]
                            k.tt(Pm[:, 0:wdt], psA[0:64, 0:wdt], E1[:, 0:wdt], ALU.mult)
                            k.asel(Pm[:, 0:wdt], Pm[:, 0:wdt], [[0, nb * 4], [-1, 64]], ALU.is_ge, 0.0, -1, 1)
                            k.tt(PTm[:, 0:wdt], psB[0:64, 0:wdt], E2[:, 0:wdt], ALU.mult)
                            k.asel(PTm[:, 0:wdt], PTm[:, 0:wdt], [[0, nb * 4], [1, 64]], ALU.is_ge, 0.0, -1, -1)
                            k.tt(AinT[:, 0:wdt], psC[0:64, 0:wdt], E2[:, 0:wdt], ALU.mult)
                            k.asel(AinT[:, 0:wdt], AinT[:, 0:wdt], [[0, nb * 4], [1, 64]], ALU.is_ge, 0.0, 0, -1)
                            k.stt(AT[:, 0:wdt], PTm[:, 0:wdt], -1.0, CST[0:64, C_I4:C_I4 + wdt], ALU.mult, ALU.add)
                            if DNL < 4:
                                continue
                            Pn, PTn = DNM[0], DNM[1]
                            for lv in range(nlv):
                                for j, (co, cs, sq_) in enumerate(bat):
                                    for h in range(4):
                                        k.mm(sl(psA, j, h, cs, cs), sl(PTm, j, h, cs, cs), sl(Pm, j, h, cs, cs))
                                        if lv < nlv - 1:
                                            k.mm(sl(psB, j, h, cs, cs), sl(Pm, j, h, cs, cs), sl(PTm, j, h, cs, cs))
                                k.cp(Pn[:, 0:wdt], psA[0:64, 0:wdt], eng="act")
                                if lv < nlv - 1:
                                    k.cp(PTn[:, 0:wdt], psB[0:64, 0:wdt])
                                for j, (co, cs, sq_) in enumerate(bat):
                                    for h in range(4):
                                        k.mm(sl(psC, j, h, cs, cs), sl(Pn, j, h, cs, cs), sl(AT, j, h, cs, cs))
                                k.tt(AT[:, 0:wdt], AT[:, 0:wdt], psC[0:64, 0:wdt], ALU.add)
                                Pm, Pn = Pn, Pm
                                PTm, PTn = PTn, PTm
                            if DNL < 5:
                                continue
                            VBt, KDt = DNT
                            for j, (co, cs, sq_) in enumerate(bat):
                                for p in range(2):
                                    k.tr(psA[0:cs, 256 * j + 128 * p:256 * j + 128 * p + 128], QKV[:, 4 + p, co:co + cs], CST[:, C_ID:C_ID + 128])
                                    k.tr(psB[0:cs, 256 * j + 128 * p:256 * j + 128 * p + 128], KDT[:, p, co:co + cs], CST[:, C_ID:C_ID + 128])
                            k.cp(VBt[:, 0:wdt], psA[0:64, 0:wdt], eng="act")
                            k.cp(KDt[:, 0:wdt], psB[0:64, 0:wdt])
                            if DNL < 6:
                                continue
                            for j, (co, cs, sq_) in enumerate(bat):
                                S = SDN[sq_]
                                p1 = prot()
                                for h in range(4):
                                    rows = slice(64 * (h % 2), 64 * (h % 2) + 64)
                                    k.mm(p1[0:cs, 64 * h:64 * h + 64], KBG[rows, h // 2, co:co + cs], S[rows, h // 2, :], ser=True)
                                RB, VN, OTM = W[4], W[5], W[6]
                                k.tt(RB[0:cs, 0:256], VBt[0:cs, 256 * j:256 * j + 256], p1[0:cs, 0:256], ALU.subtract)
                                if KSQ < 2:
                                    continue
                                p2 = prot()
                                for h in range(4):
                                    k.mm(p2[0:cs, 64 * h:64 * h + 64], sl(AT, j, h, cs, cs), RB[0:cs, 64 * h:64 * h + 64])
                                k.cp(VN[0:cs, 0:256], p2[0:cs, 0:256], eng="act")
                                if KSQ < 3:
                                    continue
                                p3 = prot()
                                for h in range(4):
                                    rows = slice(64 * (h % 2), 64 * (h % 2) + 64)
                                    k.mm(p3[0:cs, 64 * h:64 * h + 64], QG[rows, h // 2, co:co + cs], S[rows, h // 2, :], ser=True)
                                k.cp(OTM[0:cs, 0:256], p3[0:cs, 0:256], eng="act")
                                p3b = prot()
                                for h in range(4):
                                    k.mm(p3b[0:cs, 64 * h:64 * h + 64], sl(AinT, j, h, cs, cs), VN[0:cs, 64 * h:64 * h + 64])
                                k.tt(OTM[0:cs, 0:256], OTM[0:cs, 0:256], p3b[0:cs, 0:256], ALU.add)
                                if KSQ < 4:
                                    continue
                                p4 = pbank(6 + j % 2)
                                for h in range(4):
                                    rows = slice(64 * (h % 2), 64 * (h % 2) + 64)
                                    k.mm(p4[rows, 64 * (h // 2):64 * (h // 2) + 64], KDt[0:cs, 256 * j + 64 * h:256 * j + 64 * h + 64],
                                         VN[0:cs, 64 * h:64 * h + 64], tp=(0, 64 * (h % 2)))
                                if KSQ < 5:
                                    continue
                                for p in range(2):
                                    k.stt(S[:, p, :], S[:, p, :], EGB[:, p, co + cs - 1:co + cs], p4[:, 64 * p:64 * p + 64], ALU.mult, ALU.add)
                                    p5 = prot()
                                    k.tr(p5[:, 0:cs], OTM[0:cs, 128 * p:128 * p + 128], cI(cs))
                                    k.cp(QKV[:, p, co:co + cs], p5[:, 0:cs])
                                if sq_ != 0 or (ti == len(TILES) - 1 and co + cs == n):
                                    k.dma(o_dS[l, sq_], S[:, :, :], is_out=True)
                        if DNL < 7:
                            continue
                        for p in range(2):
                            psg = prot()
                            proj_fm(XB, 0, n, WIN, 776 + 128 * p, psg[:, 0:n])
                            gs = W[7]
                            k.act(gs[:, 0:n], psg[:, 0:n], AF.Silu, bias=pk[:, P_BIN + 18 + p:P_BIN + 19 + p])
                            sq = WB[2 + p]
                            k.act(sq[:, 0:n], QKV[:, p, 0:n], AF.Square)
                            pq = pbank(7)
                            k.mm(pq[:, 0:n], OS[:, 2, :], sq[:, 0:n])
                            rs = W[1]
                            k.act(rs[:, 0:n], pq[:, 0:n], AF.Sqrt, bias=EPS_LN[:, 1:2])
                            k.recip(rs[:, 0:n], rs[:, 0:n])
                            k.stt(rs[:, 0:n], QKV[:, p, 0:n], pk[:, P_DNG:P_DNG + 1], rs[:, 0:n], ALU.mult, ALU.mult)
                            k.tt(catv(6 + p, ti), rs[:, 0:n], gs[:, 0:n], ALU.mult)
                    k.barrier()

                rotn[0] = 6
                k.dma(WIN[:, :, 0:1024], w_o_d[l], eng="pool")
                for ti, (c0, n, segs) in enumerate(TILES if STAGE >= 5 else []):
                    for oc in range(8):
                        ps = prot()
                        for kc in range(8):
                            k.mm(ps[:, 0:n], WIN[:, kc, 128 * oc:128 * oc + 128], catv(kc, ti), start=(kc == 0), stop=(kc == 7))
                        k.stt(xv(ti, oc), xv(ti, oc), ALPHA, ps[:, 0:n], ALU.mult, ALU.add)
                    layer_norm(ti, P_LN1G, P_LN1B, pk)
                k.barrier()

            with ExitStack() as fsc:
                BIG = k.sb("BIG", [128, NFC, BIGW], BF16, fsc)
                BIGt = [[Tile(BIG.h) for _ in TILES] for _ in range(NFC)]
                XBF = [k.sb("XBF%d" % i, [128, 8, TN], BF16, fsc) for i in range(4)]
                WFG = [k.sb("WFG%d" % i, [128, 8, 256], BF16, fsc) for i in range(2)]
                WFD = [k.sb("WFD%d" % i, [128, NFC, 128], BF16, fsc) for i in range(2)]
                for G in (FFN_GROUPS if STAGE >= 6 else []):
                    g0 = TILES[G[0]][0]
                    bigv = lambda fc, ti, a=0, n_=None: BIGt[fc][ti].v(
                        BIG.h[:, fc, TILES[ti][0] - g0 + a:TILES[ti][0] - g0 + a + (TILES[ti][1] if n_ is None else n_)])
                    for gi, ti in enumerate(G):
                        make_xb(XBF[gi], ti)
                    for fc in range(NFC):
                        wg = WFG[fc % 2]
                        k.dma(wg[:, :, :], wf_in_d[l, fc], eng="pool")
                        for gi, ti in enumerate(G):
                            c0, n, segs = TILES[ti]
                            pg, pu = prot(), prot()
                            proj_fm(XBF[gi], 0, n, wg, 0, pg[:, 0:n])
                            proj_fm(XBF[gi], 0, n, wg, 128, pu[:, 0:n])
                            for (sq_, sc0, sn, pos0) in segs:
                                o = sc0 - c0
                                uu = lambda a, b_: U[:, fc % 2, a:b_]
                                k.cp(uu(0, 2), HISTF[sq_][:, fc, :])
                                k.cp(uu(2, 2 + sn), pg[:, o:o + sn], eng="act")
                                k.cp(HISTF[sq_][:, fc, :], uu(sn, sn + 2))
                                acc = W[fc % 2]
                                k.ts(acc[:, 0:sn], uu(2, 2 + sn), pk[:, P_FCW + 3 * fc + 2:P_FCW + 3 * fc + 3],
                                     pk[:, P_FCB + fc:P_FCB + fc + 1], ALU.mult, ALU.add)
                                for i in range(2):
                                    k.stt(acc[:, 0:sn], uu(i, i + sn), pk[:, P_FCW + 3 * fc + i:P_FCW + 3 * fc + i + 1],
                                          acc[:, 0:sn], ALU.mult, ALU.add)
                                k.act(acc[:, 0:sn], acc[:, 0:sn], AF.Gelu_apprx_tanh)
                                k.tt(bigv(fc, ti, o, sn), acc[:, 0:sn], pu[:, o:o + sn], ALU.mult)
                    for oc in range(8):
                        wd = WFD[oc % 2]
                        k.dma(wd[:, :, :], wf_out_d[l, oc], eng="pool")
                        for gi, ti in enumerate(G):
                            c0, n, segs = TILES[ti]
                            ps = prot()
                            for fc in range(NFC):
                                k.mm(ps[:, 0:n], wd[:, fc, :], bigv(fc, ti), start=(fc == 0), stop=(fc == NFC - 1))
                            k.stt(xv(ti, oc), xv(ti, oc), ALPHA, ps[:, 0:n], ALU.mult, ALU.add)
                    for ti in G:
                        layer_norm(ti, P_LN2G, P_LN2B, pk)
                for s in range(3):
                    k.dma(o_fconv[l, s], HISTF[s][:, :, :], is_out=True)
                k.barrier()

        for ti, (c0, n, segs) in enumerate(TILES):
            k.dma(o_y[:, :, c0:c0 + n], Xt[ti].v(X.h[:, :, c0:c0 + n]), is_out=True)
        k.finish()
        print("instructions:", k.nins)
    return nc


def _fm(a, nch):
    return np.ascontiguousarray(a.reshape(nch, 128).T)


def _consts():
    c = np.zeros((128, CW), np.float32)
    c[:, C_ID:C_ID + 128] = np.eye(128)
    c[:, C_ONE:C_ONE + 128] = 1.0
    c[0:64, C_BD:C_BD + 64] = 1.0
    c[64:128, C_BD + 64:C_BD + 128] = 1.0
    j = np.arange(128)
    c[:, C_UT:C_UT + 128] = (j[:, None] > j[None, :]).astype(np.float32)
    c[:, C_UTI:C_UTI + 128] = (j[:, None] >= j[None, :]).astype(np.float32)
    for h in range(4):
        c[h, C_SEL4 + 128 * h:C_SEL4 + 128 * h + 128] = 1.0
        c[4, C_ZT + 5 * h + h] = 1.0
        c[h, C_ZT + 5 * h + 4] = -1.0
    for p in range(2):
        c[2 * p, C_SELP + 128 * p:C_SELP + 128 * p + 64] = 1.0
        c[2 * p + 1, C_SELP + 128 * p + 64:C_SELP + 128 * p + 128] = 1.0
    cm = np.ones(256, np.float32)
    cm[::64] = 0.0
    c[0:4, C_CM:C_CM + 256] = cm
    cm0 = np.ones(80, np.float32)
    cm0[[0, 16, 48]] = 0.0
    c[0:4, C_CM0:C_CM0 + 80] = cm0
    c[0:64, C_I4:C_I4 + 512] = np.tile(np.eye(64, dtype=np.float32), (1, 8))
    return c


_NC_CACHE = {}


def prepare(inp):
    f = lambda n: np.asarray(inp[n], np.float32)
    xp, xs = f("x_prompt"), f("x_sample")
    B = xp.shape[0]
    cst = _consts()
    gpk = np.concatenate([_fm(f("ln_in_g"), 8), _fm(f("ln_in_b"), 8)], axis=1)
    b_in = f("b_in")
    lpk = np.zeros((NL, 128, PW), np.float32)
    lbd = np.zeros((NL, 128, 2, 2, 128), np.float32)
    for l in range(NL):
        p = lpk[l]
        bi = b_in[l]
        cols = [bi[O_LX:O_LX + 256], bi[O_LG:O_LG + 256], bi[O_FQ:O_FQ + 256], bi[O_FK:O_FK + 256],
                bi[O_SQ:O_SQ + 256], bi[O_SK:O_SK + 256]]
        for i, v in enumerate(cols):
            p[:, P_BIN + 2 * i:P_BIN + 2 * i + 2] = _fm(v, 2)
        p[:, P_BIN + 12:P_BIN + 18] = _fm(bi[O_DQ:O_DQ + 768], 6)
        p[:, P_BIN + 18:P_BIN + 20] = _fm(bi[O_DG:O_DG + 256], 2)
        p[0:4, P_BF] = bi[O_FF:O_FF + 4]
        p[0:4, P_BA] = bi[O_DA:O_DA + 4]
        p[0:4, P_DT] = f("dn_dt_bias")[l]
        p[0:4, P_AL] = f("dn_a_log")[l]
        p[0:4, P_BB] = bi[O_DB:O_DB + 4]
        lcw = f("lru_conv_w")[l]
        for c in range(2):
            for i in range(4):
                p[:, P_LCW + 4 * c + i] = lcw[i, 128 * c:128 * c + 128]
        p[:, P_LCB:P_LCB + 2] = _fm(f("lru_conv_b")[l], 2)
        p[:, P_LBA:P_LBA + 2] = _fm(f("lru_b_a")[l], 2)
        p[:, P_LBX:P_LBX + 2] = _fm(f("lru_b_x")[l], 2)
        p[:, P_LAM:P_LAM + 2] = _fm(f("lru_lambda")[l], 2)
        dcw = f("dn_conv_w")[l]
        for c in range(6):
            for i in range(4):
                p[:, P_DCW + 4 * c + i] = dcw[i, 128 * c:128 * c + 128]
        gn = f("grp_norm_g")[l]
        for g in range(3):
            p[:, P_GN + 2 * g:P_GN + 2 * g + 2] = _fm(gn[g], 2)
        p[:, P_DNG] = np.tile(f("dn_norm_g")[l], 2)
        p[:, P_LN1G:P_LN1G + 8] = _fm(f("ln1_g")[l], 8)
        p[:, P_LN1B:P_LN1B + 8] = _fm(f("ln1_b")[l], 8)
        p[:, P_LN2G:P_LN2G + 8] = _fm(f("ln2_g")[l], 8)
        p[:, P_LN2B:P_LN2B + 8] = _fm(f("ln2_b")[l], 8)
        fcw = f("ffn_conv_w")[l]
        for c in range(NFC):
            for i in range(3):
                p[:, P_FCW + 3 * c + i] = fcw[i, 128 * c:128 * c + 128]
        p[:, P_FCB:P_FCB + NFC] = _fm(f("ffn_conv_b")[l], NFC)
        for gi, nm in enumerate(("lru_w_a", "lru_w_x")):
            w = f(nm)[l]
            for n in range(4):
                c, hf = n // 2, n % 2
                lbd[l, 64 * hf:64 * hf + 64, gi, c, 64 * hf:64 * hf + 64] = w[n]
    pc = lambda w: np.ascontiguousarray(w.reshape(NL, -1, 128, w.shape[-1]).transpose(0, 2, 1, 3))
    w_in = f("w_in")
    shared = dict(cst=cst, gpk=gpk, lpk=lpk, lbd=lbd, b_in=b_in, w_o=pc(f("w_o")))
    for g, (o_, n_) in {"A": (O_LX, 512), "B": (O_FQ, 772), "C": (O_SQ, 768), "D": (O_DQ, 1032)}.items():
        shared["w_in" + g] = pc(w_in[:, :, o_:o_ + n_])
    wfi = f("ffn_w_in")
    gu = np.stack([wfi[:, :, :DFF].reshape(NL, 8, 128, NFC, 128), wfi[:, :, DFF:].reshape(NL, 8, 128, NFC, 128)], axis=4)
    shared["ffn_w_in"] = np.ascontiguousarray(gu.transpose(0, 3, 2, 1, 4, 5).reshape(NL, NFC, 128, 8, 256))
    wfo = f("ffn_w_out").reshape(NL, NFC, 128, 8, 128)
    shared["ffn_w_out"] = np.ascontiguousarray(wfo.transpose(0, 3, 2, 1, 4))
    if os.environ.get('KSWAP'):
        for k_ in ("lpk", "lbd", "w_inA", "w_inB", "w_inC", "w_inD", "b_in", "w_o", "ffn_w_in", "ffn_w_out"):
            a = np.array(shared[k_]); a[1] = a[0]; shared[k_] = a
    meta = f("meta_tokens")
    fmN = lambda a, nch: np.ascontiguousarray(a.T.reshape(nch, 128, -1).transpose(1, 0, 2))
    in_maps = []
    for core in range(B):
        sb = [2 * core, 2 * core + 1]
        xall = np.concatenate([meta, xs[sb[0]], xs[sb[1]], xp[core]], axis=0)
        m = dict(shared)
        m["xT"] = fmN(xall, 8)
        stk = lambda fn: np.ascontiguousarray(np.stack([np.stack([fn(l, b) for b in sb]) for l in range(NL)]))
        m["st_lconv"] = stk(lambda l, b: fmN(f("state_lru_conv")[l, b], 2))
        m["st_lh"] = stk(lambda l, b: _fm(f("state_lru_h")[l, b], 2))
        m["st_fk"] = stk(lambda l, b: fmN(f("cache_fox_k")[l, b].reshape(PAST, 256), 2))
        m["st_fv"] = stk(lambda l, b: f("cache_fox_v")[l, b].reshape(PAST, 256))
        m["st_flf"] = stk(lambda l, b: f("cache_fox_logf")[l, b].T)
        m["st_sk"] = stk(lambda l, b: fmN(f("cache_sb_k")[l, b].reshape(PAST, 256), 2))
        m["st_sv"] = stk(lambda l, b: f("cache_sb_v")[l, b].reshape(PAST, 256))
        m["st_dconv"] = stk(lambda l, b: fmN(f("state_dn_conv")[l, b], 6))
        m["st_dS"] = stk(lambda l, b: f("state_dn_S")[l, b].reshape(2, 2, 64, 64).transpose(1, 2, 0, 3).reshape(128, 2, 64))
        m["st_fconv"] = stk(lambda l, b: fmN(f("state_ffn_conv")[l, b], NFC))
        in_maps.append({k_: np.ascontiguousarray(v, dtype=np.float32) for k_, v in m.items()})
    return in_maps


def kernel(**inp):
    in_maps = prepare(inp)
    B = len(in_maps)
    if "nc" not in _NC_CACHE:
        _NC_CACHE["nc"] = build_program()
    res = run_bass_kernel_spmd(_NC_CACHE["nc"], in_maps, core_ids=list(range(B)))
    return assemble(res.results)


def assemble(R):
    B = len(R)
    Lp = NMETA + SEQ
    tm = lambda a: a.transpose(1, 0, 2).reshape(a.shape[1] * 128, a.shape[2]).T
    pcols = np.r_[0:16, 80:NT]
    y_p = np.stack([tm(R[c]["o_y"])[80:] for c in range(B)])
    y_s = np.stack([tm(R[c]["o_y"])[16 + 32 * j:48 + 32 * j] for c in range(B) for j in range(2)])

    def gather(fn_p, fn_s):
        P = np.stack([np.stack([fn_p(R[c], l) for c in range(B)]) for l in range(NL)])
        S = np.stack([np.stack([fn_s(R[c], l, j) for c in range(B) for j in range(2)]) for l in range(NL)])
        return P.astype(np.float32), S.astype(np.float32)

    scol = lambda j: slice(16 + 32 * j, 48 + 32 * j)
    lconv = gather(lambda r, l: tm(r["o_lconv"][l, 0]), lambda r, l, j: tm(r["o_lconv"][l, 1 + j]))
    lh = gather(lambda r, l: r["o_lh"][l, 0].T.reshape(256), lambda r, l, j: r["o_lh"][l, 1 + j].T.reshape(256))
    fk = gather(lambda r, l: tm(r["o_fk"][l])[pcols].reshape(Lp, 4, 64), lambda r, l, j: tm(r["o_fk"][l])[scol(j)].reshape(DSEQ, 4, 64))
    fv = gather(lambda r, l: r["o_fv"][l][pcols].reshape(Lp, 4, 64), lambda r, l, j: r["o_fv"][l][scol(j)].reshape(DSEQ, 4, 64))
    flf = gather(lambda r, l: r["o_flf"][l].T[pcols], lambda r, l, j: r["o_flf"][l].T[scol(j)])
    sk = gather(lambda r, l: tm(r["o_sk"][l])[pcols].reshape(Lp, 4, 64), lambda r, l, j: tm(r["o_sk"][l])[scol(j)].reshape(DSEQ, 4, 64))
    sv = gather(lambda r, l: r["o_sv"][l][pcols].reshape(Lp, 4, 64), lambda r, l, j: r["o_sv"][l][scol(j)].reshape(DSEQ, 4, 64))
    dconv = gather(lambda r, l: tm(r["o_dconv"][l, 0]), lambda r, l, j: tm(r["o_dconv"][l, 1 + j]))
    unS = lambda a: a.reshape(2, 64, 2, 64).transpose(2, 0, 1, 3).reshape(4, 64, 64)
    dS = gather(lambda r, l: unS(r["o_dS"][l, 0]), lambda r, l, j: unS(r["o_dS"][l, 1 + j]))
    fconv = gather(lambda r, l: tm(r["o_fconv"][l, 0]), lambda r, l, j: tm(r["o_fconv"][l, 1 + j]))
    outs = [y_p, y_s] + [g[0] for g in (lconv, lh, fk, fv, flf, sk, sv, dconv, dS, fconv)] + \
           [g[1] for g in (lconv, lh, fk, fv, flf, sk, sv, dconv, dS, fconv)]
    return tuple(np.ascontiguousarray(o, dtype=np.float32) for o in outs)
```
